# Optimizing a Trainium2 kernel written in Bass

```python
import math
import jax
import jax.numpy as jnp
from jax import lax
import numpy as np

D_MODEL = 1024
BATCH = 8
SEQ = 2048
DEPTH = 4
DEC_BATCH = 128
DEC_SEQ = 1
PAST_LEN = 8192
PAGE_SIZE = 128

N_A = DEPTH // 2
N_B = DEPTH - N_A
D_A = D_MODEL
DK_A = 128
H_A = D_A // DK_A
DV_A = D_A // H_A
CHUNK_A = 32
LB_FLOOR = 1e-30
H_B = D_MODEL // 128
NOPE = 128
ROPE_DIM = 64
V_DIM = 128
Q_LORA = 384
KV_LORA = 256
ROPE_THETA = 10000.0
Q_BLOCK = 128
SCALE = (NOPE + ROPE_DIM) ** -0.5
MASK_VALUE = -1e30
D_FF = ((8 * D_MODEL // 3 + 255) // 256) * 256
EPS = 1e-6

kernel_name = 'yoco_hgrn2_mla_decoder_step'


def rmsnorm(x, gain):
    xf = x.astype(jnp.float32)
    y = xf * lax.rsqrt(jnp.mean(xf * xf, axis=-1, keepdims=True) + EPS)
    return (y * gain.astype(jnp.float32)).astype(x.dtype)


def rope(x, pos):
    half = x.shape[-1] // 2
    inv = ROPE_THETA ** (-jnp.arange(half, dtype=jnp.float32) / half)
    ang = pos.astype(jnp.float32)[:, None] * inv[None, :]
    shape = (1, x.shape[1]) + (1,) * (x.ndim - 3) + (half,)
    cos = jnp.cos(ang).reshape(shape)
    sin = jnp.sin(ang).reshape(shape)
    xf = x.astype(jnp.float32)
    x1, x2 = xf[..., :half], xf[..., half:]
    return jnp.concatenate([x1 * cos - x2 * sin, x2 * cos + x1 * sin], axis=-1).astype(x.dtype)


def swiglu(h, w_in, w_out):
    g, u = jnp.split(h @ w_in, 2, axis=-1)
    return (jax.nn.silu(g) * u) @ w_out


def hgrn_lower_bounds(lb_logits):
    p = jax.nn.softmax(lb_logits.astype(jnp.float32), axis=0)
    return jnp.cumsum(p, axis=0) - p[0:1]


def hgrn2_chunked(q, k, v, log_f, s0, chunk):
    B, T, H, _ = q.shape
    DV = v.shape[-1]
    n = T // chunk

    def to_chunks(a):
        return a.reshape(B, n, chunk, H, a.shape[-1]).transpose(1, 0, 3, 2, 4)

    tril = jnp.tril(jnp.ones((chunk, chunk), dtype=bool))

    def step(s, inp):
        qc, kc, vc, gc = inp
        b = jnp.cumsum(gc, axis=2)
        diff = b[:, :, :, None, :] - b[:, :, None, :, :]
        decay = jnp.where(tril[:, :, None], jnp.exp(jnp.minimum(diff, 0.0)), 0.0)
        scores = jnp.einsum('bhtk,bhsk,bhtsk->bhts', qc, kc, decay)
        o = (jnp.einsum('bhts,bhsv->bhtv', scores, vc)
             + jnp.einsum('bhtk,bhkv->bhtv', qc * jnp.exp(b), s))
        b_last = b[:, :, -1:, :]
        s = (jnp.exp(b_last[:, :, 0, :, None]) * s
             + jnp.einsum('bhsk,bhsv->bhkv', kc * jnp.exp(b_last - b), vc))
        return s, o

    s, o = lax.scan(step, s0, (to_chunks(q), to_chunks(k), to_chunks(v), to_chunks(log_f)))
    o = o.transpose(1, 0, 3, 2, 4).reshape(B, T, H, DV)
    return o, s


def hgrn2_mixer(h, s0, lb, w_in, g_norm, w_out):
    B, T, _ = h.shape
    q, fz, i, gz = jnp.split(h @ w_in, 4, axis=-1)

    def heads(a):
        return a.astype(jnp.float32).reshape(B, T, H_A, -1)

    q = jax.nn.silu(heads(q)) * (DK_A ** -0.5)
    lbh = lb.reshape(H_A, DK_A)
    log_lb = jnp.log(jnp.maximum(lbh, LB_FLOOR))
    log_f = jnp.logaddexp(log_lb, jnp.log1p(-lbh) + jax.nn.log_sigmoid(heads(fz)))
    log_f = jnp.minimum(log_f, 0.0)
    k = -jnp.expm1(log_f)
    v = heads(i)
    o, s = hgrn2_chunked(q, k, v, log_f, s0.astype(jnp.float32), math.gcd(T, CHUNK_A))
    o = o * lax.rsqrt(jnp.mean(o * o, axis=-1, keepdims=True) + EPS)
    o = o * g_norm.astype(jnp.float32).reshape(H_A, DV_A) * jax.nn.silu(heads(gz))
    return o.reshape(B, T, D_A).astype(h.dtype) @ w_out, s


def mla_shared_kv(x, pos, kv_norm, w_dkv, kv_a_norm):
    ckr = rmsnorm(x, kv_norm) @ w_dkv
    c = rmsnorm(ckr[..., :KV_LORA], kv_a_norm)
    kr = rope(ckr[..., KV_LORA:], pos)
    return c, kr


def mla_attend(q_lat, q_rope, c_all, kr_all, q_pos, k_pos):
    B, T = q_lat.shape[0], q_lat.shape[1]

    def block(qb):
        ql, qr, qp = qb
        s = (jnp.einsum('bthc,bsc->bhts', ql, c_all)
             + jnp.einsum('bthr,bsr->bhts', qr, kr_all)).astype(jnp.float32) * SCALE
        s = jnp.where(k_pos[None, None, None, :] <= qp[None, None, :, None], s, MASK_VALUE)
        p = jax.nn.softmax(s, axis=-1).astype(c_all.dtype)
        return jnp.einsum('bhts,bsc->bthc', p, c_all)

    if T > Q_BLOCK and T % Q_BLOCK == 0:
        nb = T // Q_BLOCK

        def split(a):
            return a.reshape((B, nb, Q_BLOCK) + a.shape[2:]).swapaxes(0, 1)

        out = lax.map(block, (split(q_lat), split(q_rope), q_pos.reshape(nb, Q_BLOCK)))
        return out.swapaxes(0, 1).reshape(B, T, H_B, KV_LORA)
    return block((q_lat, q_rope, q_pos))


def mla_mixer(h, pos, c_all, kr_all, k_pos, w_ukv, w_dq, q_a_norm, w_uq, w_out):
    B, T, _ = h.shape
    q = (rmsnorm(h @ w_dq, q_a_norm) @ w_uq).reshape(B, T, H_B, NOPE + ROPE_DIM)
    q_nope, q_rope = q[..., :NOPE], rope(q[..., NOPE:], pos)
    w_ukv_h = w_ukv.reshape(KV_LORA, H_B, NOPE + V_DIM)
    q_lat = jnp.einsum('bthn,chn->bthc', q_nope, w_ukv_h[..., :NOPE])
    o_lat = mla_attend(q_lat, q_rope, c_all, kr_all, pos, k_pos)
    o = jnp.einsum('bthc,chv->bthv', o_lat, w_ukv_h[..., NOPE:]).reshape(B, T, H_B * V_DIM)
    return o @ w_out


def run_trunk(x, hgrn_states, c_past, kr_past, pos,
              norm_gains, w_ffn_in, w_ffn_out, w_in_a, lb_logits, g_norm_a, w_out_a,
              kv_norm, w_dkv, kv_a_norm, w_ukv, w_dq, q_a_norm, w_uq, w_out_b):
    lb = hgrn_lower_bounds(lb_logits)
    new_states = []
    c_new = kr_new = c_all = kr_all = k_pos = None
    for l in range(DEPTH):
        g = norm_gains[l]
        h = rmsnorm(x, g[0])
        if l < N_A:
            mix, s = hgrn2_mixer(h, hgrn_states[l], lb[l], w_in_a[l], g_norm_a[l], w_out_a[l])
            new_states.append(s)
        else:
            if l == N_A:
                c_new, kr_new = mla_shared_kv(x, pos, kv_norm, w_dkv, kv_a_norm)
                if c_past is None:
                    c_all, kr_all, k_pos = c_new, kr_new, pos
                else:
                    past_len = c_past.shape[1]
                    c_all = jnp.concatenate([c_past.astype(c_new.dtype), c_new], axis=1)
                    kr_all = jnp.concatenate([kr_past.astype(kr_new.dtype), kr_new], axis=1)
                    k_pos = jnp.concatenate([jnp.arange(past_len, dtype=jnp.int32), pos])
            j = l - N_A
            mix = mla_mixer(h, pos, c_all, kr_all, k_pos, w_ukv, w_dq[j], q_a_norm[j], w_uq[j], w_out_b[j])
        x = x + rmsnorm(mix, g[1])
        x = x + rmsnorm(swiglu(rmsnorm(x, g[2]), w_ffn_in[l], w_ffn_out[l]), g[3])
    return x, jnp.stack(new_states), c_new, kr_new


def setup_inputs(seed: int = 0) -> dict:
    key = jax.random.key(seed)
    ks = jax.random.split(key, 24)
    f32 = jnp.float32

    def nrm(k, shape, fan_in):
        return jax.random.normal(k, shape, f32) * (fan_in ** -0.5)

    def gain(k, shape):
        return 1.0 + 0.05 * jax.random.normal(k, shape, f32)

    n_pages = PAST_LEN // PAGE_SIZE
    n_used = DEC_BATCH * n_pages
    n_pool = n_used + max(1, n_used // 4)
    page_table = jax.random.permutation(ks[5], n_pool)[:n_used].reshape(DEC_BATCH, n_pages).astype(jnp.int32)

    return {
        'x_prompt': jax.random.normal(ks[0], (BATCH, SEQ, D_MODEL), f32),
        'x_sample': jax.random.normal(ks[1], (DEC_BATCH, DEC_SEQ, D_MODEL), f32),
        'state_hgrn': 0.3 * jax.random.normal(ks[2], (N_A, DEC_BATCH, H_A, DK_A, DV_A), f32),
        'cache_kv_latent': jax.random.normal(ks[3], (n_pool, PAGE_SIZE, KV_LORA), f32),
        'cache_k_rope': jax.random.normal(ks[4], (n_pool, PAGE_SIZE, ROPE_DIM), f32),
        'page_table': page_table,
        'norm_gains': gain(ks[6], (DEPTH, 4, D_MODEL)),
        'w_ffn_in': nrm(ks[7], (DEPTH, D_MODEL, 2 * D_FF), D_MODEL),
        'w_ffn_out': nrm(ks[8], (DEPTH, D_FF, D_MODEL), D_FF),
        'w_in_a': nrm(ks[9], (N_A, D_MODEL, 4 * D_A), D_MODEL),
        'lb_logits': 0.5 * jax.random.normal(ks[10], (N_A, D_A), f32),
        'g_norm_a': gain(ks[11], (N_A, D_A)),
        'w_out_a': nrm(ks[12], (N_A, D_A, D_MODEL), D_A),
        'kv_norm': gain(ks[13], (D_MODEL,)),
        'w_dkv': nrm(ks[14], (D_MODEL, KV_LORA + ROPE_DIM), D_MODEL),
        'kv_a_norm': gain(ks[15], (KV_LORA,)),
        'w_ukv': nrm(ks[16], (KV_LORA, H_B * (NOPE + V_DIM)), KV_LORA),
        'w_dq': nrm(ks[17], (N_B, D_MODEL, Q_LORA), D_MODEL),
        'q_a_norm': gain(ks[18], (N_B, Q_LORA)),
        'w_uq': nrm(ks[19], (N_B, Q_LORA, H_B * (NOPE + ROPE_DIM)), Q_LORA),
        'w_out_b': nrm(ks[20], (N_B, H_B * V_DIM, D_MODEL), H_B * V_DIM),
    }


def reference(x_prompt, x_sample, state_hgrn, cache_kv_latent, cache_k_rope, page_table,
              norm_gains, w_ffn_in, w_ffn_out, w_in_a, lb_logits, g_norm_a, w_out_a,
              kv_norm, w_dkv, kv_a_norm, w_ukv, w_dq, q_a_norm, w_uq, w_out_b):
    weights = (norm_gains, w_ffn_in, w_ffn_out, w_in_a, lb_logits, g_norm_a, w_out_a,
               kv_norm, w_dkv, kv_a_norm, w_ukv, w_dq, q_a_norm, w_uq, w_out_b)
    pos_p = jnp.arange(x_prompt.shape[1], dtype=jnp.int32)
    s0_p = jnp.zeros((N_A, x_prompt.shape[0], H_A, DK_A, DV_A), jnp.float32)
    y_p, st_p, c_p, kr_p = run_trunk(x_prompt, s0_p, None, None, pos_p, *weights)
    db, n_pages = page_table.shape
    past_len = n_pages * cache_kv_latent.shape[1]
    c_past = cache_kv_latent[page_table].reshape(db, past_len, KV_LORA)
    kr_past = cache_k_rope[page_table].reshape(db, past_len, ROPE_DIM)
    pos_s = past_len + jnp.arange(x_sample.shape[1], dtype=jnp.int32)
    y_s, st_s, c_s, kr_s = run_trunk(x_sample, state_hgrn, c_past, kr_past, pos_s, *weights)
    return (y_p, y_s, st_p, c_p, kr_p, st_s, c_s, kr_s)
```

```python
import contextlib
import os
import numpy as np
import concourse.bass as bass
import concourse.mybir as mybir
from concourse.bass_utils import run_bass_kernel_spmd

F32 = mybir.dt.float32
BF16 = mybir.dt.bfloat16
I32 = mybir.dt.int32
AF = mybir.ActivationFunctionType
ALU = mybir.AluOpType

D = 1024
T = 2048
NS = 16
H = 8
DFF = 2816
QL = 384
KVL = 256
RD = 64
PAST = 8192
NPG = 64
EPS = 1e-6
SCALE = (128 + 64) ** -0.5
DBG = set(os.environ.get("KDBG", "").split(","))
GN = 512
NG = T // GN


class Buf:
    __slots__ = ("name", "w", "r")

    def __init__(self, name=""):
        self.name = name
        self.w = {}
        self.r = {}


class Sched:
    ENG = ("pe", "act", "dve", "pool", "sp")

    def __init__(self, nc, stack, n_dsem=40):
        self.nc = nc
        self.ops = {e: [] for e in self.ENG}
        self.esem = {e: stack.enter_context(nc.semaphore("E" + e)) for e in self.ENG}
        self.dsem = [stack.enter_context(nc.semaphore("D%d" % i)) for i in range(n_dsem)]
        self.dval = [0] * n_dsem
        self.dnext = 0
        self.waited = {e: {} for e in self.ENG}

    def _need(self, eng, dep, waits, war=False):
        if dep is None:
            return
        if dep[0] == "e":
            _, pe, idx = dep
            if pe == eng and (eng == "pe" or war):
                return
            key = ("e", pe)
            if self.waited[eng].get(key, -1) >= idx:
                return
            self.waited[eng][key] = idx
            self.ops[pe][idx]["sig"] = True
            waits.append(dep)
        else:
            _, j, val = dep
            key = ("d", j)
            if self.waited[eng].get(key, -1) >= val:
                return
            self.waited[eng][key] = val
            waits.append(dep)

    @staticmethod
    def _flat(bufs):
        out = []
        for b in bufs:
            if isinstance(b, (list, tuple)):
                out.extend(Sched._flat(b))
            else:
                out.append(b)
        return out

    def op(self, eng, fn, reads=(), writes=(), dma=False):
        reads = self._flat(reads)
        writes = self._flat(writes)
        waits = []
        for b in reads:
            for d in b.w.values():
                self._need(eng, d, waits)
        for b in writes:
            for d in b.w.values():
                if dma and d[0] == "d" and not b.r:
                    continue
                self._need(eng, d, waits)
            for d in b.r.values():
                self._need(eng, d, waits, war=True)
        idx = len(self.ops[eng])
        rec = dict(fn=fn, waits=waits, sig=False, dma=None)
        if dma:
            j = self.dnext
            self.dnext = (self.dnext + 1) % len(self.dsem)
            if self.dval[j] > 0:
                self._need(eng, ("d", j, self.dval[j]), waits)
            self.dval[j] += 16
            rec["dma"] = j
            ev = ("d", j, self.dval[j])
            key = ("d", j)
        else:
            ev = ("e", eng, idx)
            key = ("e", eng)
        self.ops[eng].append(rec)
        for b in reads:
            b.r[key] = ev
        for b in writes:
            if dma and not b.r:
                b.w = {k: v for k, v in b.w.items() if k[0] == "d"}
                b.w[key] = ev
            else:
                b.w = {key: ev}
            b.r = {}
        return ev

    def barrier(self):
        last = {}
        for e in self.ENG:
            idx = len(self.ops[e]) - 1
            while idx >= 0 and (self.ops[e][idx]["fn"] is None or self.ops[e][idx]["dma"] is not None):
                idx -= 1
            last[e] = idx
        dvals = list(self.dval)
        for e in self.ENG:
            waits = []
            for pe, idx in last.items():
                if pe != e and idx >= 0:
                    self._need(e, ("e", pe, idx), waits)
            for j, v in enumerate(dvals):
                if v > 0:
                    self._need(e, ("d", j, v), waits)
            self.ops[e].append(dict(fn=None, waits=waits, sig=False, dma=None))

    def emit(self, block):
        for e in self.ENG:
            c = 0
            for rec in self.ops[e]:
                if rec["sig"]:
                    c += 1
                rec["sval"] = c
        S = self

        def run(name, eng):
            for rec in S.ops[name]:
                for d in rec["waits"]:
                    if d[0] == "e":
                        eng.wait_ge(S.esem[d[1]], S.ops[d[1]][d[2]]["sval"])
                    else:
                        eng.wait_ge(S.dsem[d[1]], d[2])
                if rec["fn"] is None:
                    continue
                ins = rec["fn"](eng)
                if rec["dma"] is not None:
                    ins.then_inc(S.dsem[rec["dma"]], 16)
                elif rec["sig"]:
                    ins.then_inc(S.esem[name], 1)

        @block.tensor
        def _(e):
            run("pe", e)

        @block.scalar
        def _(e):
            run("act", e)

        @block.vector
        def _(e):
            run("dve", e)

        @block.gpsimd
        def _(e):
            run("pool", e)

        @block.sync
        def _(e):
            run("sp", e)


def _consts():
    c = {}
    c["ident"] = np.eye(128, dtype=np.float32)
    s = np.arange(128)[:, None]
    t = np.arange(128)[None, :]
    c["hmask"] = ((s // 32 == t // 32) & (s <= t)).astype(np.float32)
    c["ctri"] = (s <= t).astype(np.float32)
    r = np.ones((128, GN), np.float32)
    r[:, ::32] = 0.0
    c["reset"] = r
    half = RD // 2
    inv = (np.float32(10000.0) ** (-(np.arange(half, dtype=np.float32) / np.float32(half)))).astype(np.float32)
    pos = np.concatenate([np.arange(T), np.full(NS, PAST)]).astype(np.float32)
    ang = (pos[:, None] * inv[None, :]).astype(np.float32).astype(np.float64)
    cos = np.cos(ang).astype(np.float32)
    sin = np.sin(ang).astype(np.float32)
    c["cosT"] = cos
    c["sinT"] = sin
    c["cosF"] = np.ascontiguousarray(np.concatenate([cos, cos], 1).T)
    c["sinF"] = np.ascontiguousarray(np.concatenate([sin, sin], 1).T)
    sel = np.zeros((NS, NS, 128), np.float32)
    for i in range(NS):
        sel[i, i, :] = 1.0
    c["sel"] = sel.reshape(NS, NS * 128)
    mp = np.zeros((NS, NS, H), np.float32)
    for i in range(NS):
        mp[i, i, :] = 1.0
    c["maskpad"] = np.ascontiguousarray(np.broadcast_to(mp.reshape(1, NS * NS * H), (128, NS * NS * H)))
    nm = np.zeros((NS, H, NS), np.float32)
    for i in range(NS):
        nm[i, :, i] = 1.0
    c["newmask"] = nm.reshape(NS * H, NS)
    c["cmask"] = (np.arange(128)[:, None] // 32 == np.arange(4)[None, :]).astype(np.float32)
    c["pmod"] = (np.arange(128) % 32).astype(np.float32).reshape(128, 1)
    return c


CONST_SHAPES = {k: v.shape for k, v in _consts().items()}


def build(n_pool, groups=(0, 1, 2, 3, "s"), n_layers=4, plan=None):
    nc = bass.Bass("TRN2", target_bir_lowering=False)
    req_log = []
    dram = {}

    def din(name, shape, dt=F32):
        dram[name] = nc.dram_tensor(name, list(shape), dt, kind="ExternalInput").ap()
        return dram[name]

    def dout(name, shape, dt=F32):
        dram[name] = nc.dram_tensor(name, list(shape), dt, kind="ExternalOutput").ap()
        return dram[name]

    xp = din("xp", [T, D])
    xs = din("xs", [NS, D])
    st = din("st", [2 * NS * H * 128, 128])
    ckv = din("ckv", [n_pool * 32, 4 * KVL])
    ckr = din("ckr", [n_pool * 32, 4 * RD])
    ptd = din("ptd", [NS * 16, 4], I32)
    gains = din("gains", [128, 16 * 8])
    w_ffn_in = din("w_ffn_in", [4 * D, 2 * DFF])
    w_ffn_out = din("w_ffn_out", [4 * DFF, D])
    w_in_a = din("w_in_a", [2 * D, 4 * D])
    lbl = din("lbl", [128, 2 * 8])
    gna = din("gna", [128, 2 * 8])
    w_out_a = din("w_out_a", [2 * D, D])
    kvn = din("kvn", [128, 8])
    w_dkv = din("w_dkv", [D, KVL + RD])
    kvan = din("kvan", [128, 2])
    kvan_b = din("kvan_b", [128, KVL])
    w_ukv = din("w_ukv", [KVL, H * 256])
    w_dq = din("w_dq", [2 * D, QL])
    qan = din("qan", [128, 2 * 3])
    w_uq = din("w_uq", [2 * QL, H * 192])
    w_out_b = din("w_out_b", [2 * D, D])
    cd = {k: din("c_" + k, shp) for k, shp in CONST_SHAPES.items()}

    yp = dout("yp", [T, D])
    ys = dout("ys", [NS, D])
    stp = dout("stp", [2 * H * 128, 128])
    cp = dout("cp", [T, KVL])
    krp = dout("krp", [T, RD])
    sts = dout("sts", [2 * NS * H * 128, 128])
    cs = dout("cs", [NS, KVL])
    krs = dout("krs", [NS, RD])

    with contextlib.ExitStack() as stack:
        S = Sched(nc, stack)

        def sb(name, shape, dt=F32):
            return stack.enter_context(nc.sbuf_tensor(name, list(shape), dt))

        def mm(out, lhsT, rhs, start, stop, reads, writes, **kw):
            S.op("pe", lambda e: e.matmul(out, lhsT, rhs, start=start, stop=stop, **kw), reads, writes)

        def tr(out, in_, ident, reads, writes):
            S.op("pe", lambda e: e.transpose(out, in_, ident), reads, writes)

        def act(out, in_, func, reads, writes, scale=1.0, bias=0.0, accum_out=None):
            kw = {}
            if accum_out is not None:
                kw["accum_out"] = accum_out
            S.op("act", lambda e: e.activation(out, in_, func, bias=bias, scale=scale, **kw), reads, writes)

        def tt(eng, out, in0, in1, op, reads, writes):
            S.op(eng, lambda e: e.tensor_tensor(out, in0, in1, op), reads, writes)

        def ts(eng, out, in0, s1, s2, op0, op1, reads, writes):
            S.op(eng, lambda e: e.tensor_scalar(out, in0, s1, s2, op0, op1), reads, writes)

        def stt(out, in0, scalar, in1, op0, op1, reads, writes):
            S.op("dve", lambda e: e.scalar_tensor_tensor(out, in0, scalar, in1, op0, op1), reads, writes)

        def cpy(eng, out, in_, reads, writes):
            if eng == "act":
                S.op("act", lambda e: e.copy(out, in_), reads, writes)
            else:
                S.op(eng, lambda e: e.tensor_copy(out, in_), reads, writes)

        def recip(out, in_, reads, writes):
            S.op("dve", lambda e: e.reciprocal(out, in_), reads, writes)

        def dma(eng, out, in_, reads, writes):
            S.op(eng, lambda e: e.dma_start(out=out, in_=in_), reads, writes, dma=True)

        banks = []
        for i in range(8):
            t_ = stack.enter_context(nc.psum_tensor("ps%d" % i, [128, 512], F32))
            banks.append((t_, Buf("ps%d" % i)))
        rot = {"i": 0, "n": 4}

        def ps_next():
            i = rot["i"]
            rot["i"] = (i + 1) % rot["n"]
            return banks[i]

        CB = Buf("consts")
        ident = sb("ident", [128, 128])
        identb = sb("identb", [128, 128], BF16)
        onesb = sb("onesb", [128, 128], BF16)
        hmask = sb("hmask", [128, 128])
        ctri = sb("ctri", [128, 128], BF16)
        ctri_f = sb("ctri_f", [128, 128])
        reset = sb("reset", [128, GN])
        cmask = sb("cmask", [128, 4])
        cmask4 = sb("cmask4", [128, 4, 128], BF16)
        gains_sb = sb("gains_sb", [128, 16, 8])
        lbl_sb = sb("lbl_sb", [128, 2, 8])
        gna_sb = sb("gna_sb", [128, 2, 8])
        kvn_sb = sb("kvn_sb", [128, 8])
        kvan_sb = sb("kvan_sb", [128, 2])
        kvanb_sb = sb("kvanb_sb", [128, KVL])
        qan_sb = sb("qan_sb", [128, 2, 3])
        lb_sb = sb("lb_sb", [128, 2, 8])
        oml_sb = sb("oml_sb", [128, 2, 8])
        lbtmp = sb("lbtmp", [128, 4, 8])
        for dst, src in ((ident, cd["ident"]), (hmask, cd["hmask"]), (ctri_f, cd["ctri"]), (reset, cd["reset"]), (cmask, cd["cmask"]),
                         (gains_sb, gains.rearrange("p (a k) -> p a k", a=16)),
                         (lbl_sb, lbl.rearrange("p (a k) -> p a k", a=2)),
                         (gna_sb, gna.rearrange("p (a k) -> p a k", a=2)),
                         (kvn_sb, kvn), (kvan_sb, kvan), (kvanb_sb, kvan_b),
                         (qan_sb, qan.rearrange("p (a k) -> p a k", a=2))):
            dma("sp", dst[:], src, [], [CB])
        cpy("dve", identb[:], ident[:], [CB], [CB])
        cpy("dve", ctri[:], ctri_f[:], [CB], [CB])
        S.op("dve", lambda e: e.memset(onesb[:], 1.0), [], [CB])
        cpy("dve", cmask4[:], cmask[:].unsqueeze(2).to_broadcast([128, 4, 128]), [CB], [CB])
        act(lbtmp[:, 0:2, :], lbl_sb[:], AF.Exp, [CB], [CB])
        tt("dve", lbtmp[:, 2, :], lbtmp[:, 0, :], lbtmp[:, 1, :], ALU.add, [CB], [CB])
        recip(lbtmp[:, 3, :], lbtmp[:, 2, :], [CB], [CB])
        tt("dve", lbtmp[:, 0, :], lbtmp[:, 0, :], lbtmp[:, 3, :], ALU.mult, [CB], [CB])
        tt("dve", lbtmp[:, 1, :], lbtmp[:, 1, :], lbtmp[:, 3, :], ALU.mult, [CB], [CB])
        tt("dve", lb_sb[:, 0, :], lbtmp[:, 0, :], lbtmp[:, 0, :], ALU.subtract, [CB], [CB])
        tt("dve", lbtmp[:, 2, :], lbtmp[:, 0, :], lbtmp[:, 1, :], ALU.add, [CB], [CB])
        tt("dve", lb_sb[:, 1, :], lbtmp[:, 2, :], lbtmp[:, 0, :], ALU.subtract, [CB], [CB])
        ts("dve", oml_sb[:], lb_sb[:], -1.0, 1.0, ALU.mult, ALU.add, [CB], [CB])

        WSLOT = 22 * 128
        NWS = 4
        wring = [(sb("wr%d" % i, [128, WSLOT], BF16), Buf("wr%d" % i)) for i in range(NWS)]
        wstate = {"issued": 0, "cur": 0}

        def _issue(i):
            wd, r0, K, c0, pw = plan[i]
            KC = K // 128
            t_, b_ = wring[i % NWS]
            view = t_[:, :KC * pw].rearrange("p (k c) -> p k c", k=KC)
            src = dram[wd][r0:r0 + K, c0:c0 + pw].rearrange("(k p) c -> p k c", p=128)
            dma("pool", view, src, [], [b_])

        def panel(wd, r0, K, c0, pw):
            i = wstate["cur"]
            wstate["cur"] += 1
            req_log.append((wd, r0, K, c0, pw))
            KC = K // 128
            if plan is None:
                t_, b_ = wring[i % NWS]
                view = t_[:, :KC * pw].rearrange("p (k c) -> p k c", k=KC)
                src = dram[wd][r0:r0 + K, c0:c0 + pw].rearrange("(k p) c -> p k c", p=128)
                dma("pool", view, src, [], [b_])
                return view, b_
            assert plan[i] == (wd, r0, K, c0, pw), (i, plan[i], (wd, r0, K, c0, pw))
            while wstate["issued"] < min(i + NWS - 1, len(plan)):
                _issue(wstate["issued"])
                wstate["issued"] += 1
            t_, b_ = wring[i % NWS]
            return t_[:, :KC * pw].rearrange("p (k c) -> p k c", k=KC), b_

        def linear(wd, r0, K, c0, M, rhs_fn, rhs_bufs, N, consume, PW=256):
            KC = K // 128
            mi = 0
            for p0 in range(0, M, PW):
                pw = min(PW, M - p0)
                view, wb = panel(wd, r0, K, c0 + p0, pw)
                for m0 in range(0, pw, 128):
                    mw = min(128, pw - m0)
                    pt_, pb_ = ps_next()
                    for kc in range(KC):
                        rbk = [b[kc] if isinstance(b, list) and len(b) == KC else b for b in rhs_bufs]
                        mm(pt_[:mw, :N], view[:, kc, m0:m0 + mw], rhs_fn(kc), kc == 0, kc == KC - 1,
                           [wb] + rbk, [pb_])
                    consume(mi, mw, pt_, pb_)
                    mi += 1

        NMAX = GN
        class _NS:
            pass
        A = _NS()
        xT = sb("xT", [128, 8, NMAX])
        A.XB = [Buf("xT%d" % i) for i in range(8)]
        hT = sb("hT", [128, 8, NMAX], BF16)
        A.HB = [Buf("hT%d" % i) for i in range(8)]
        mixT = sb("mixT", [128, 8, NMAX])
        A.MB = [Buf("mixT%d" % i) for i in range(8)]
        sqT = sb("sqT", [128, 2, NMAX], BF16)
        SQB = [Buf("sqT0"), Buf("sqT1")]
        rstd = sb("rstd", [128, NMAX])
        RB = Buf("rstd")
        tmpN = sb("tmpN", [128, NMAX])
        TB = Buf("tmpN")
        big = sb("big", [128, 22, NMAX], BF16)
        A.BGB = Buf("big")
        A.xT, A.hT, A.mixT, A.big = xT, hT, mixT, big
        Sst = [sb("Sst%d" % l, [128, H, 128]) for l in range(2)]
        SSB = [[Buf("S%d_%d" % (l, h)) for h in range(H)] for l in range(2)]
        Stmp = sb("Stmp", [128, 128])
        STB = Buf("Stmp")
        Stmp2 = sb("Stmp2", [128, 128])
        STB2 = Buf("Stmp2")
        Sbf = sb("Sbf", [128, 4, 128], BF16)
        SBB = [Buf("Sbf%d" % i) for i in range(4)]
        cT_all = sb("cT_all", [128, 2, T + NS], BF16)
        krT_all = sb("krT_all", [64, T + NS], BF16)
        CKB = [Buf("ck%d" % g) for g in range(NG + 1)]
        wukv = sb("wukv", [128, 2, H * 256], BF16)
        WUB = Buf("wukv")
        cosF = sb("cosF", [64, NMAX])
        sinF = sb("sinF", [64, NMAX])
        CSB = Buf("cossin")

        def rstd_of(src_fn, src_bufs, KC, N, Dn):
            pt_, pb_ = ps_next()
            for kc in range(KC):
                sbk = [b[kc] if isinstance(b, list) and len(b) == 8 else b for b in src_bufs]
                act(sqT[:, kc % 2, :N], src_fn(kc), AF.Square, sbk, [SQB[kc % 2]])
                mm(pt_[:, :N], onesb[:], sqT[:, kc % 2, :N], kc == 0, kc == KC - 1, [SQB[kc % 2], CB], [pb_])
            ts("dve", tmpN[:, :N], pt_[:, :N], 1.0 / Dn, EPS, ALU.mult, ALU.add, [pb_], [TB])
            act(tmpN[:, :N], tmpN[:, :N], AF.Ln, [TB], [TB])
            act(rstd[:, :N], tmpN[:, :N], AF.Exp, [TB], [RB], scale=-0.5)

        def prenorm(gi, N, gain_tile=None):
            rstd_of(lambda kc: A.xT[:, kc, :N], [A.XB], 8, N, D)
            for kc in range(8):
                g_ = gain_tile[:, kc:kc + 1] if gain_tile is not None else gains_sb[:, gi, kc:kc + 1]
                stt(A.hT[:, kc, :N], A.xT[:, kc, :N], g_, rstd[:, :N], ALU.mult, ALU.mult, [A.XB[kc], RB, CB], [A.HB[kc]])

        def postnorm_add(gi, N):
            rstd_of(lambda kc: A.mixT[:, kc, :N], [A.MB], 8, N, D)
            for kc in range(8):
                stt(A.mixT[:, kc, :N], A.mixT[:, kc, :N], gains_sb[:, gi, kc:kc + 1], rstd[:, :N], ALU.mult, ALU.mult,
                    [A.MB[kc], RB, CB], [A.MB[kc]])
                tt("dve", A.xT[:, kc, :N], A.xT[:, kc, :N], A.mixT[:, kc, :N], ALU.add, [A.XB[kc], A.MB[kc]], [A.XB[kc]])

        def to_mix(mi, mw, pt_, pb_, N):
            cpy("act", A.mixT[:, mi, :N], pt_[:, :N], [pb_], [A.MB[mi]])

        def ffn(l, N):
            prenorm(4 * l + 2, N)
            gtmp = sb_ffn_g
            for j in range(DFF // 256):
                vg, bg = panel("w_ffn_in", l * D, D, j * 256, 256)
                vu, bu = panel("w_ffn_in", l * D, D, DFF + j * 256, 256)
                for m in range(2):
                    pg_, pgb = ps_next()
                    for kc in range(8):
                        mm(pg_[:, :N], vg[:, kc, m * 128:(m + 1) * 128], A.hT[:, kc, :N], kc == 0, kc == 7, [bg, A.HB[kc]], [pgb])
                    pu_, pub = ps_next()
                    for kc in range(8):
                        mm(pu_[:, :N], vu[:, kc, m * 128:(m + 1) * 128], A.hT[:, kc, :N], kc == 0, kc == 7, [bu, A.HB[kc]], [pub])
                    act(gtmp[:, :N], pg_[:, :N], AF.Silu, [pgb], [GTB])
                    tt("dve", A.big[:, 2 * j + m, :N], gtmp[:, :N], pu_[:, :N], ALU.mult, [GTB, pub], [A.BGB])
            linear("w_ffn_out", l * DFF, DFF, 0, D, lambda kc: A.big[:, kc, :N], [A.BGB], N,
                   lambda mi, mw, p_, b_: to_mix(mi, mw, p_, b_, N), PW=128)
            postnorm_add(4 * l + 3, N)

        sb_ffn_g = sb("gtmp", [128, NMAX])
        GTB = Buf("gtmp")

        hq = sb("hq", [128, NMAX])
        hf = sb("hf", [128, NMAX])
        hlf = sb("hlf", [128, NMAX])
        hb = sb("hb", [128, NMAX])
        hk = sb("hk", [128, NMAX])
        heb = sb("heb", [128, NMAX])
        hr = sb("hr", [128, NMAX])
        hbl = sb("hbl", [128, NMAX // 32])
        hgate = sb("hgate", [128, NMAX], BF16)
        hqd = sb("hqd", [128, NMAX], BF16)
        hkd = sb("hkd", [128, NMAX], BF16)
        hkd2 = sb("hkd2", [128, NMAX], BF16)
        hkd2T = sb("hkd2T", [128, 4, 128], BF16)
        hv = sb("hv", [128, 4, 128], BF16)
        hA = sb("hA", [128, 4, 128], BF16)
        hvm = sb("hvm", [128, 4, 128], BF16)
        HVM = [Buf("hvm%d" % i) for i in range(4)]
        hos = sb("hos", [128, NMAX])
        HQ, HF, HLF, HK, HEB, HR, HBL, HGT, HQD, HKD, HKD2, HKT, HV, HA, HOS = [Buf(n) for n in
            "hq hf hlf hk heb hr hbl hgate hqd hkd hkd2 hkd2T hv hA hos".split()]
        HBB = Buf("hb")

        def hgrn_post(l, h, po, pob, N, gate=None, gate_bufs=None):
            if gate is None:
                gate, gate_bufs = hgate[:, :N], [HGT]
                po = po[:, :N]
            cpy("act", hos[:, :N], po, [pob], [HOS])
            act(sqT[:, 0, :N], po, AF.Square, [pob], [SQB[0]])
            ps2, ps2b = ps_next()
            mm(ps2[:, :N], onesb[:], sqT[:, 0, :N], True, True, [SQB[0], CB], [ps2b])
            ts("dve", tmpN[:, :N], ps2[:, :N], 1.0 / 128, EPS, ALU.mult, ALU.add, [ps2b], [TB])
            act(tmpN[:, :N], tmpN[:, :N], AF.Ln, [TB], [TB])
            act(rstd[:, :N], tmpN[:, :N], AF.Exp, [TB], [RB], scale=-0.5)
            stt(hos[:, :N], hos[:, :N], gna_sb[:, l, h:h + 1], rstd[:, :N], ALU.mult, ALU.mult, [HOS, RB, CB], [HOS])
            tt("dve", A.big[:, h, :N], hos[:, :N], gate, ALU.mult, [HOS] + gate_bufs, [A.BGB])

        hqd2 = [hqd, sb("hqd_b", [128, NMAX], BF16)]
        hkd_2 = [hkd, sb("hkd_b", [128, NMAX], BF16)]
        hkd2T2 = [hkd2T, sb("hkd2T_b", [128, 4, 128], BF16)]
        hv2 = [hv, sb("hv_b", [128, 4, 128], BF16)]
        hA2 = [hA, sb("hA_b", [128, 4, 128], BF16)]
        hgate2 = [hgate, sb("hgate_b", [128, NMAX], BF16)]
        hbl2 = [hbl, sb("hbl_b", [128, NMAX // 32])]
        HQD2, HKD_2, HKT2, HV2, HA2, HGT2, HBL2 = [[Buf(n + "0"), Buf(n + "1")] for n in "hqd hkd hkt hv hA hgt hbl".split()]

        def hgrn_prompt(l, g):
            N = GN
            prenorm(4 * l + 0, N)

            def stageA1p(h, bs):
                for _ in stageA1(h, bs, 0):
                    pass

            def stageA1g(h, bs):
                return stageA1(h, bs, 1)

            def stageA1(h, bs, part):
                if part == 1:
                    yield from stageA1_gate(h, bs)
                    return
                pan = lambda sec: panel("w_in_a", l * D, D, sec * D + h * 128, 128)
                vq, bq = pan(0)
                pq, pqb = ps_next()
                for kc in range(8):
                    mm(pq[:, :N], vq[:, kc, :], A.hT[:, kc, :N], kc == 0, kc == 7, [bq, A.HB[kc]], [pqb])
                act(hq[:, :N], pq[:, :N], AF.Silu, [pqb], [HQ])
                yield
                vf, bf_ = pan(1)
                pf, pfb = ps_next()
                for kc in range(8):
                    mm(pf[:, :N], vf[:, kc, :], A.hT[:, kc, :N], kc == 0, kc == 7, [bf_, A.HB[kc]], [pfb])
                act(hf[:, :N], pf[:, :N], AF.Sigmoid, [pfb], [HF])
                yield
                vi, bi = pan(2)
                pv, pvb = ps_next()
                for tt_ in range(4):
                    for kc in range(8):
                        mm(pv[:, tt_ * 128:(tt_ + 1) * 128], A.hT[:, kc, tt_ * 128:(tt_ + 1) * 128], vi[:, kc, :],
                           kc == 0, kc == 7, [bi, A.HB[kc]], [pvb])
                cpy("act", hv2[bs][:].rearrange("p a b -> p (a b)"), pv[:, :], [pvb], [HV2[bs]])
                yield
                vg, bg = pan(3)
                pg, pgb = ps_next()
                for kc in range(8):
                    mm(pg[:, :N], vg[:, kc, :], A.hT[:, kc, :N], kc == 0, kc == 7, [bg, A.HB[kc]], [pgb])
                act(hgate2[bs][:, :N], pg[:, :N], AF.Silu, [pgb], [HGT2[bs]])
                yield

            def stageA1_gate(h, bs):
                ts("dve", hf[:, :N], hf[:, :N], oml_sb[:, l, h:h + 1], lb_sb[:, l, h:h + 1], ALU.mult, ALU.add, [HF, CB], [HF])
                ts("dve", hk[:, :N], hf[:, :N], -1.0, 1.0, ALU.mult, ALU.add, [HF], [HK])
                act(hlf[:, :N], hf[:, :N], AF.Ln, [HF], [HLF])
                yield
                S.op("dve", lambda e: e.tensor_tensor_scan(hb[:, :N], reset[:, :N], hlf[:, :N], 0.0, ALU.mult, ALU.add),
                     [HLF, CB], [HBB])
                b3 = hb[:, :N].rearrange("p (c t) -> p c t", t=32)
                cpy("dve", hbl2[bs][:, :N // 32], b3[:, :, 31], [HBB], [HBL2[bs]])
                act(heb[:, :N], hb[:, :N], AF.Exp, [HBB], [HEB])
                yield
                stt(hqd2[bs][:, :N], hq[:, :N], 128 ** -0.5, heb[:, :N], ALU.mult, ALU.mult, [HQ, HEB], [HQD2[bs]])
                act(heb[:, :N], hb[:, :N], AF.Exp, [HBB], [HEB], scale=-1.0)
                yield
                tt("dve", hkd_2[bs][:, :N], hk[:, :N], heb[:, :N], ALU.mult, [HK, HEB], [HKD_2[bs]])
                tt("dve", hr[:, :N].rearrange("p (c t) -> p c t", t=32),
                   hbl2[bs][:, :N // 32].unsqueeze(2).to_broadcast([128, N // 32, 32]), b3, ALU.subtract, [HBL2[bs], HBB], [HR])
                act(hr[:, :N], hr[:, :N], AF.Exp, [HR], [HR])
                yield
                tt("dve", hkd2[:, :N], hk[:, :N], hr[:, :N], ALU.mult, [HK, HR], [HKD2])
                act(hbl2[bs][:, :N // 32], hbl2[bs][:, :N // 32], AF.Exp, [HBL2[bs]], [HBL2[bs]])
                yield

            def stageA2(h, bs):
                ptb, ptbb = ps_next()
                ptv = ptb[:].bitcast(BF16)
                for tt_ in range(4):
                    tr(ptv[:, tt_ * 128:(tt_ + 1) * 128], hkd2[:, tt_ * 128:(tt_ + 1) * 128], identb[:], [HKD2, CB], [ptbb])
                cpy("act", hkd2T2[bs][:].rearrange("p a b -> p (a b)"), ptv[:, 0:512], [ptbb], [HKT2[bs]])
                psc, pscb = ps_next()
                for tt_ in range(4):
                    sl = slice(tt_ * 128, (tt_ + 1) * 128)
                    mm(psc[:, sl], hkd_2[bs][:, sl], hqd2[bs][:, sl], True, True, [HKD_2[bs], HQD2[bs]], [pscb])
                tt("dve", hA2[bs][:], psc[:].rearrange("p (a b) -> p a b", a=4),
                   hmask[:].unsqueeze(1).to_broadcast([128, 4, 128]), ALU.mult, [pscb, CB], [HA2[bs]])

            def stageB(h, bs):
                po, pob = banks[4 + (h % 2)]
                Srot = [(Sst[l][:, h, :], SSB[l][h]), (Stmp[:], STB), (Stmp2[:], STB2)]
                for tt_ in range(4):
                    sl = slice(tt_ * 128, (tt_ + 1) * 128)
                    pd, pdb = banks[6 + (tt_ % 2)]
                    tt("dve", hvm[:], hv2[bs][:, tt_, :].unsqueeze(1).to_broadcast([128, 4, 128]), cmask4[:], ALU.mult,
                       [HV2[bs], CB], [HVM[0]])
                    for c in range(4):
                        mm(pd[:, c * 128:(c + 1) * 128], hkd2T2[bs][:, tt_, :], hvm[:, c, :],
                           True, True, [HKT2[bs], HVM[0]], [pdb])
                    mm(po[:, sl], hv2[bs][:, tt_, :], hA2[bs][:, tt_, :], True, False, [HV2[bs], HA2[bs]], [pob])
                    for c in range(4):
                        ci = tt_ * 4 + c
                        Sc = Srot[ci % 3]
                        Sx = Srot[(ci + 1) % 3]
                        cpy("pool", Sbf[:, c, :], Sc[0], [Sc[1]], [SBB[c]])
                        mm(po[:, ci * 32:(ci + 1) * 32], Sbf[:, c, :], hqd2[bs][:, ci * 32:(ci + 1) * 32], False, c == 3,
                           [SBB[c], HQD2[bs]], [pob], skip_group_check=True)
                        stt(Sx[0], Sc[0], hbl2[bs][:, ci:ci + 1], pd[:, c * 128:(c + 1) * 128],
                            ALU.mult, ALU.add, [Sc[1], HBL2[bs], pdb], [Sx[1]])
                        if c % 2 == 1:
                            yield
                cpy("dve", Srot[0][0], Srot[1][0], [Srot[1][1]], [Srot[0][1]])
                hgrn_post(l, h, po[:, :N], pob, N, gate=hgate2[bs][:, :N], gate_bufs=[HGT2[bs]])
                if g == NG - 1:
                    dma("sp", stp.rearrange("(l h k) v -> l h k v", l=2, h=H)[l, h], Sst[l][:, h, :], [SSB[l][h]], [OUTB])
                yield

            stageA1p(0, 0)
            for _ in stageA1g(0, 0):
                pass
            stageA2(0, 0)
            import itertools
            for h in range(H):
                gb = stageB(h, h % 2)
                ga = (itertools.chain(stageA1(h + 1, (h + 1) % 2, 0), stageA1g(h + 1, (h + 1) % 2))
                      if h + 1 < H else iter(()))
                doneb = donea = False
                while not (doneb and donea):
                    if not donea:
                        try:
                            next(ga)
                        except StopIteration:
                            donea = True
                    if not doneb:
                        try:
                            next(gb)
                        except StopIteration:
                            doneb = True
                if h + 1 < H:
                    stageA2(h + 1, (h + 1) % 2)
            linear("w_out_a", l * D, D, 0, D, lambda kc: A.big[:, kc, :N], [A.BGB], N,
                   lambda mi, mw, p_, b_: to_mix(mi, mw, p_, b_, N))
            postnorm_add(4 * l + 1, N)

        OUTB = Buf("out")


        qaT = sb("qaT", [128, 3, NMAX], BF16)
        QAB = Buf("qaT")
        qnT = sb("qnT", [128, NMAX], BF16)
        QNB = Buf("qnT")
        qrT = sb("qrT", [64, NMAX], BF16)
        QRB = Buf("qrT")
        wrot = sb("wrot", [128, 8, 64], BF16)
        WRB = Buf("wrot")
        knT = sb("knT", [128, T], BF16)
        KNB = Buf("knT")
        vh = sb("vh", [128, T // 128, 128], BF16)
        VHB = Buf("vh")
        pexp = sb("pexp", [128, 2, NMAX], BF16)
        PXB = [Buf("pexp0"), Buf("pexp1")]
        rt1 = sb("rt1", [64, NMAX])
        rt2 = sb("rt2", [64, NMAX])
        RT1, RT2 = Buf("rt1"), Buf("rt2")
        ctm = sb("ctm", [128, KVL + RD])
        CTM = Buf("ctm")
        cst = sb("cst", [128, 2, 32])
        CST = Buf("cst")
        ssq = sb("ssq", [128, 4])
        SSQ = Buf("ssq")
        junk = sb("junk", [128, KVL])
        JNK = Buf("junk")

        def rope_fm(dst, pr, prb, prot, protb, N, dbufs):
            tt("dve", rt1[:, :N], pr[:64, :N], cosF[:, :N], ALU.mult, [prb, CSB], [RT1])
            tt("dve", rt2[:, :N], prot[:64, :N], sinF[:, :N], ALU.mult, [protb, CSB], [RT2])
            tt("dve", dst, rt1[:, :N], rt2[:, :N], ALU.add, [RT1, RT2], dbufs)

        def make_wrot(view, vb, KC, c0):
            ts("dve", wrot[:, :KC, 0:32], view[:, :, c0 + 32:c0 + 64], -1.0, None, ALU.mult, ALU.bypass, [vb], [WRB])
            cpy("dve", wrot[:, :KC, 32:64], view[:, :, c0:c0 + 32], [vb], [WRB])

        def mla_shared(N, col0, pos0, cp_ap, krp_ap, ckb):
            prenorm(None, N, gain_tile=kvn_sb)
            view, vb = panel("w_dkv", 0, D, 0, KVL + RD)
            make_wrot(view, vb, 8, KVL)
            for mc in range(2):
                pt_, pb_ = ps_next()
                for kc in range(8):
                    mm(pt_[:, :N], view[:, kc, mc * 128:(mc + 1) * 128], A.hT[:, kc, :N], kc == 0, kc == 7, [vb, A.HB[kc]], [pb_])
                cpy("act", A.mixT[:, mc, :N], pt_[:, :N], [pb_], [A.MB])
            pr, prb = ps_next()
            for kc in range(8):
                mm(pr[:64, :N], view[:, kc, KVL:KVL + RD], A.hT[:, kc, :N], kc == 0, kc == 7, [vb, A.HB[kc]], [prb])
            prot, protb = ps_next()
            for kc in range(8):
                mm(prot[:64, :N], wrot[:, kc, :], A.hT[:, kc, :N], kc == 0, kc == 7, [WRB, A.HB[kc]], [protb])
            rope_fm(krT_all[:, col0:col0 + N], pr, prb, prot, protb, N, [ckb])
            rstd_of(lambda kc: A.mixT[:, kc, :N], [A.MB], 2, N, KVL)
            for mc in range(2):
                stt(cT_all[:, mc, col0:col0 + N], A.mixT[:, mc, :N], kvan_sb[:, mc:mc + 1], rstd[:, :N], ALU.mult, ALU.mult,
                    [A.MB, RB, CB], [ckb])
            for t0 in range(0, N, 128):
                rows = min(128, N - t0)
                pt_, pb_ = ps_next()
                for kc in range(8):
                    mm(pt_[:rows, :KVL + RD], A.hT[:, kc, t0:t0 + rows], view[:, kc, :], kc == 0, kc == 7, [vb, A.HB[kc]], [pb_])
                act(junk[:rows, :], pt_[:rows, :KVL], AF.Square, [pb_], [JNK, SSQ], accum_out=ssq[:rows, 0:1])
                act(ssq[:rows, 1:2], ssq[:rows, 0:1], AF.Sqrt, [SSQ], [SSQ], scale=1.0 / KVL, bias=EPS)
                recip(ssq[:rows, 2:3], ssq[:rows, 1:2], [SSQ], [SSQ])
                stt(ctm[:rows, :KVL], pt_[:rows, :KVL], ssq[:rows, 2:3], kvanb_sb[:rows, :], ALU.mult, ALU.mult, [pb_, SSQ, CB], [CTM])
                dma("sp", cst[:rows, 0, :], cd["cosT"][pos0 + t0:pos0 + t0 + rows, :], [], [CST])
                dma("sp", cst[:rows, 1, :], cd["sinT"][pos0 + t0:pos0 + t0 + rows, :], [], [CST])
                x1 = pt_[:rows, KVL:KVL + 32]
                x2 = pt_[:rows, KVL + 32:KVL + 64]
                o1 = ctm[:rows, KVL:KVL + 32]
                o2 = ctm[:rows, KVL + 32:KVL + 64]
                tt("dve", junk[:rows, 0:32], x2, cst[:rows, 1, :], ALU.mult, [pb_, CST], [JNK])
                tt("dve", o1, x1, cst[:rows, 0, :], ALU.mult, [pb_, CST], [CTM])
                tt("dve", o1, o1, junk[:rows, 0:32], ALU.subtract, [CTM, JNK], [CTM])
                tt("dve", junk[:rows, 32:64], x1, cst[:rows, 1, :], ALU.mult, [pb_, CST], [JNK])
                tt("dve", o2, x2, cst[:rows, 0, :], ALU.mult, [pb_, CST], [CTM])
                tt("dve", o2, o2, junk[:rows, 32:64], ALU.add, [CTM, JNK], [CTM])
                dma("sp", cp_ap[t0:t0 + rows, :], ctm[:rows, :KVL], [CTM], [OUTB])
                dma("sp", krp_ap[t0:t0 + rows, :], ctm[:rows, KVL:KVL + RD], [CTM], [OUTB])

        def mla_q(l, N):
            j = l - 2
            prenorm(4 * l + 0, N)
            linear("w_dq", j * D, D, 0, QL, lambda kc: A.hT[:, kc, :N], [A.HB], N,
                   lambda mi, mw, p_, b_: to_mix(mi, mw, p_, b_, N), PW=128)
            rstd_of(lambda kc: A.mixT[:, kc, :N], [A.MB], 3, N, QL)
            for mc in range(3):
                stt(qaT[:, mc, :N], A.mixT[:, mc, :N], qan_sb[:, j, mc:mc + 1], rstd[:, :N], ALU.mult, ALU.mult, [A.MB, RB, CB], [QAB])

        def mla_q_head(l, h, N):
            j = l - 2
            vw, vb = panel("w_uq", j * QL, QL, h * 192, 192)
            make_wrot(vw, vb, 3, 128)
            pq, pqb = ps_next()
            for kc in range(3):
                mm(pq[:, :N], vw[:, kc, 0:128], qaT[:, kc, :N], kc == 0, kc == 2, [vb, QAB], [pqb])
            cpy("act", qnT[:, :N], pq[:, :N], [pqb], [QNB])
            pr, prb = ps_next()
            for kc in range(3):
                mm(pr[:64, :N], vw[:, kc, 128:192], qaT[:, kc, :N], kc == 0, kc == 2, [vb, QAB], [prb])
            prot, protb = ps_next()
            for kc in range(3):
                mm(prot[:64, :N], wrot[:, kc, :], qaT[:, kc, :N], kc == 0, kc == 2, [WRB, QAB], [protb])
            rope_fm(qrT[:, :N], pr, prb, prot, protb, N, [QRB])

        def mla_prompt(l, g):
            N = GN
            j = l - 2
            mla_q(l, N)
            for h in range(H):
                mla_q_head(l, h, N)
                for kb in range(g + 1):
                    pk, pkb = ps_next()
                    for cc in range(2):
                        mm(pk[:, :], wukv[:, cc, h * 256:h * 256 + 128], cT_all[:, cc, kb * 512:(kb + 1) * 512], cc == 0, cc == 1,
                           [WUB, CKB[kb]], [pkb])
                    cpy("act", knT[:, kb * 512:(kb + 1) * 512], pk[:, :], [pkb], [KNB])
                    pv, pvb = ps_next()
                    for tt_ in range(4):
                        for cc in range(2):
                            mm(pv[:, tt_ * 128:(tt_ + 1) * 128], cT_all[:, cc, kb * 512 + tt_ * 128:kb * 512 + (tt_ + 1) * 128],
                               wukv[:, cc, h * 256 + 128:h * 256 + 256], cc == 0, cc == 1, [WUB, CKB[kb]], [pvb])
                    cpy("act", vh[:, kb * 4:(kb + 1) * 4, :].rearrange("p a b -> p (a b)"), pv[:, :], [pvb], [VHB])
                po, pob = banks[4 + (h % 2)]
                pden, pdenb = banks[6 + (h % 2)]
                ntile = 4 * g + 4
                def S_(i):
                    r = i - 4 * g
                    q0 = 128 * r if r > 0 else 0
                    ps_, psb = ps_next()
                    mm(ps_[:, q0:N], knT[:, i * 128:(i + 1) * 128], qnT[:, q0:N], True, False, [KNB, QNB], [psb])
                    mm(ps_[:, q0:N], krT_all[:, i * 128:(i + 1) * 128], qrT[:, q0:N], False, True, [CKB[i // 4], QRB], [psb])
                    px = pexp[:, i % 2, :]
                    pxb = PXB[i % 2]
                    act(px[:, q0:N], ps_[:, q0:N], AF.Exp, [psb], [pxb], scale=SCALE)
                    if r >= 0:
                        tt("dve", px[:, q0:q0 + 128], px[:, q0:q0 + 128], ctri[:], ALU.mult, [pxb, CB], [pxb])

                def V_(i):
                    r = i - 4 * g
                    q0 = 128 * r if r > 0 else 0
                    px = pexp[:, i % 2, :]
                    pxb = PXB[i % 2]
                    mm(po[:, q0:N], vh[:, i, :], px[:, q0:N], i == 0, i == ntile - 1, [VHB, pxb], [pob], skip_group_check=True)
                    mm(pden[:, q0:N], onesb[:], px[:, q0:N], i == 0, i == ntile - 1, [CB, pxb], [pdenb], skip_group_check=True)

                S_(0)
                for i in range(ntile):
                    if i + 1 < ntile:
                        S_(i + 1)
                    V_(i)
                recip(rstd[:, :N], pden[:, :N], [pdenb], [RB])
                tt("dve", A.big[:, h, :N], po[:, :N], rstd[:, :N], ALU.mult, [pob, RB], [A.BGB])
            linear("w_out_b", j * D, D, 0, D, lambda kc: A.big[:, kc, :N], [A.BGB], N,
                   lambda mi, mw, p_, b_: to_mix(mi, mw, p_, b_, N))
            postnorm_add(4 * l + 1, N)

        xin = sb("xin", [128, D])
        XIN = Buf("xin")

        def load_x(src_ap, rows, col0):
            dma("sp", xin[:rows, :], src_ap, [], [XIN])
            for half in range(2):
                pt_, pb_ = ps_next()
                for j in range(4):
                    kc = half * 4 + j
                    tr(pt_[:, j * 128:j * 128 + rows], xin[:rows, kc * 128:(kc + 1) * 128], ident[:rows, :rows], [XIN, CB], [pb_])
                for j in range(4):
                    kc = half * 4 + j
                    cpy("act", A.xT[:, kc, col0:col0 + rows], pt_[:, j * 128:j * 128 + rows], [pb_], [A.XB[kc]])

        def store_x(dst_ap, rows, col0):
            for half in range(2):
                pt_, pb_ = ps_next()
                for j in range(4):
                    kc = half * 4 + j
                    tr(pt_[:rows, j * 128:(j + 1) * 128], A.xT[:, kc, col0:col0 + rows], ident[:], [A.XB[kc], CB], [pb_])
                cpy("act", xin[:rows, half * 512:(half + 1) * 512], pt_[:rows, :], [pb_], [XIN])
            dma("sp", dst_ap, xin[:rows, :], [XIN], [OUTB])


        _pools = [[big[:].rearrange("p a b -> p (a b)"), 0, 22 * NMAX], [mixT[:].rearrange("p a b -> p (a b)").bitcast(BF16), 0, 16 * NMAX],
                  [xT[:].rearrange("p a b -> p (a b)").bitcast(BF16), 0, 16 * NMAX], [hT[:].rearrange("p a b -> p (a b)"), 0, 8 * NMAX]]

        def carve(shape, dt=F32):
            esz = 4 if dt in (F32, I32) else 2
            n = int(np.prod(shape[1:])) * esz // 2
            n = (n + 15) // 16 * 16
            for pl in _pools:
                if pl[1] + n <= pl[2]:
                    v = pl[0][:, pl[1]:pl[1] + n]
                    pl[1] += n
                    if esz == 4:
                        v = v.bitcast(dt)
                    v = v[:shape[0], :int(np.prod(shape[1:]))]
                    if len(shape) == 3:
                        v = v.rearrange("p (a b) -> p a b", a=shape[1])
                    elif len(shape) == 4:
                        v = v.rearrange("p (a b c) -> p a b c", a=shape[1], b=shape[2])
                    return v
            raise RuntimeError("carve: out of space " + str(shape))

        NST = 3
        stile = [(carve([128, H, 128]), Buf("stile%d" % i)) for i in range(NST)]
        sel32 = carve([NS, NS * 128])
        vs32 = carve([NS, D])
        for pl in _pools:
            pl[1] = 0
        NCT = 12
        ctile = [(carve([128, 4, KVL], BF16), carve([128, 4, RD], BF16), Buf("ct%d" % i)) for i in range(NCT)]
        qpad = carve([128, 2, NS, 128], BF16)
        rpad = carve([64, NS, 128], BF16)
        wukT = carve([128, H, KVL], BF16)
        cTs = carve([128, 2, 1024], BF16)
        rTs = sb("rTs", [64, 2, 512], BF16)
        pex2 = carve([128, 4, 512], BF16)
        snb = sb("snb", [128, H, 128], BF16)
        snb_b = carve([128, H, 128], BF16)
        snb2 = [snb, snb_b]
        SNB2 = [Buf("snb0"), Buf("snb1")]
        xTs = sb("xTs", [128, 8, NS])
        hTs = sb("hTs", [128, 8, NS], BF16)
        mixTs = sb("mixTs", [128, 8, NS])
        bigs = sb("bigs", [128, 22, NS], BF16)
        sq32 = sb("sq32", [128, H, NS])
        sf32 = sb("sf32", [128, H, NS])
        sk32 = sb("sk32", [128, H, NS])
        sgate = sb("sgate", [128, H, NS], BF16)
        sqb = sb("sqb", [128, H, NS], BF16)
        SQ32, SF32, SK32, SGT, SQBB = [Buf(n) for n in "sq32 sf32 sk32 sgate sqb".split()]
        VS32 = Buf("vs32")
        SNB = Buf("snb")
        stmp = sb("stmp", [128, 2, 128])
        STM = [Buf("stmp0"), Buf("stmp1")]

        def hgrn_sample(l):
            N = NS
            prenorm(4 * l + 0, N)
            for sec, fn, dst, db in ((0, AF.Silu, sq32, SQ32), (1, AF.Sigmoid, sf32, SF32), (3, AF.Silu, sgate, SGT)):
                linear("w_in_a", l * D, D, sec * D, D, lambda kc: A.hT[:, kc, :N], [A.HB], N,
                       (lambda fn, dst, db: (lambda mi, mw, p_, b_: act(dst[:, mi, :], p_[:, :N], fn, [b_], [db])))(fn, dst, db))
            for q4 in range(4):
                vw, vb = panel("w_in_a", l * D, D, 2 * D + q4 * 256, 256)
                pt_, pb_ = ps_next()
                for kc in range(8):
                    mm(pt_[:N, :256], A.hT[:, kc, :N], vw[:, kc, :], kc == 0, kc == 7, [vb, A.HB[kc]], [pb_])
                cpy("act", vs32[:, q4 * 256:(q4 + 1) * 256], pt_[:N, :256], [pb_], [VS32])
            lbb = lb_sb[:, l, :].unsqueeze(2).to_broadcast([128, H, NS])
            omb = oml_sb[:, l, :].unsqueeze(2).to_broadcast([128, H, NS])
            tt("dve", sf32[:], sf32[:], omb, ALU.mult, [SF32, CB], [SF32])
            tt("dve", sf32[:], sf32[:], lbb, ALU.add, [SF32, CB], [SF32])
            ts("dve", sk32[:], sf32[:], -1.0, 1.0, ALU.mult, ALU.add, [SF32], [SK32])
            ts("dve", sqb[:], sq32[:], 128 ** -0.5, None, ALU.mult, ALU.bypass, [SQ32], [SQBB])
            st_in = st.rearrange("(l s h k) v -> l s k h v", l=2, s=NS, h=H)
            st_out = sts.rearrange("(l s h k) v -> l s k h v", l=2, s=NS, h=H)
            po, pob = banks[4]
            vbanks = {}

            def vb_stage(s_):
                stl, stb = stile[s_ % NST]
                dma("sp", stl[:], st_in[l, s_], [], [stb])
                pair = []
                for half in range(2):
                    pvb_, pvbb = banks[(s_ % 2) * 2 + half]
                    mm(pvb_[:, :], sel32[:, s_ * 128:(s_ + 1) * 128], vs32[:, half * 512:(half + 1) * 512], True, True,
                       [CB, VS32], [pvbb])
                    pair.append((pvb_, pvbb))
                vbanks[s_] = pair

            def upd_stage(s_):
                stl, stb = stile[s_ % NST]
                for half in range(2):
                    pvb_, pvbb = vbanks[s_][half]
                    for hh in range(4):
                        h = half * 4 + hh
                        tb = (h % 2)
                        act(stmp[:, tb, :], pvb_[:, hh * 128:(hh + 1) * 128], AF.Copy, [pvbb, SK32], [STM[tb]],
                            scale=sk32[:, h, s_:s_ + 1])
                        stt(stl[:, h, :], stl[:, h, :], sf32[:, h, s_:s_ + 1], stmp[:, tb, :], ALU.mult, ALU.add,
                            [stb, SF32, STM[tb]], [stb])
                sn = snb2[s_ % 2]
                cpy("act", sn[:].rearrange("p a b -> p (a b)"), stl[:].rearrange("p a b -> p (a b)"), [stb], [SNB2[s_ % 2]])
                dma("sp", st_out[l, s_], stl[:], [stb], [OUTB])

            def o_stage(s_):
                sn = snb2[s_ % 2]
                for h in range(H):
                    mm(po[:, h * NS + s_:h * NS + s_ + 1], sn[:, h, :], sqb[:, h, s_:s_ + 1], True, True, [SNB2[s_ % 2], SQBB], [pob])

            vb_stage(0)
            for s_ in range(NS):
                if s_ + 1 < NS:
                    vb_stage(s_ + 1)
                upd_stage(s_)
                if s_ >= 1:
                    o_stage(s_ - 1)
            o_stage(NS - 1)
            for h in range(H):
                hgrn_post(l, h, po[:, h * NS:(h + 1) * NS], pob, N, gate=sgate[:, h, :], gate_bufs=[SGT])
            linear("w_out_a", l * D, D, 0, D, lambda kc: A.big[:, kc, :N], [A.BGB], N,
                   lambda mi, mw, p_, b_: to_mix(mi, mw, p_, b_, N))
            postnorm_add(4 * l + 1, N)

        WKT = Buf("wukT")
        qlat = sb("qlat", [128, 2, NS * H], BF16)
        QLB = Buf("qlat")
        qrall = sb("qrall", [64, NS * H], BF16)
        QRA = Buf("qrall")
        QPB = Buf("qpad")
        CTS = [Buf("cTs0"), Buf("cTs1")]
        RTS = [Buf("rTs0"), Buf("rTs1")]
        PX2 = [Buf("pex2_%d" % i) for i in range(4)]
        pTs = sb("pTs", [128, 4, 128], BF16)
        PTS = Buf("pTs")
        idx32 = sb("idx32", [128, 2 * 128], I32)
        IDXB = Buf("idx")
        ptd_sb = sb("ptd_sb", [128, 2, 4], I32)
        ptd_f = sb("ptd_f", [128, 2, 128])
        pmod = sb("pmod", [128, 1])
        newmask = sb("newmask", [128, NS])
        pnew = sb("pnew", [128, NS])
        pnewT = sb("pnewT", [NS, 128], BF16)
        ctmb = sb("ctmb", [NS, KVL], BF16)
        PNB = Buf("pnew")
        olat = sb("olat", [128, 2, NS * H], BF16)
        OLB = Buf("olat")

        def decode_setup_h():
            dma("sp", sel32[:], cd["sel"], [], [CB])

        def decode_setup():
            dma("sp", pmod[:], cd["pmod"], [], [CB])
            dma("sp", newmask[:], cd["newmask"], [], [CB])
            for h in range(H):
                pt_, pb_ = ps_next()
                pv_ = pt_[:].bitcast(BF16)
                for cc in range(2):
                    tr(pv_[:, cc * 128:(cc + 1) * 128], wukv[:, cc, h * 256:h * 256 + 128], identb[:], [WUB, CB], [pb_])
                cpy("act", wukT[:, h, :], pv_[:, 0:256], [pb_], [WKT])
            for hf_ in range(2):
                dma("sp", ptd_sb[:, hf_, :], ptd[hf_ * 128:(hf_ + 1) * 128, :], [], [IDXB])
            for hf_ in range(2):
                cpy("dve", ptd_f[:, hf_, :].rearrange("p (a b) -> p a b", b=32),
                    ptd_sb[:, hf_, :].unsqueeze(2).to_broadcast([128, 4, 32]), [IDXB], [IDXB])
                pt_, pb_ = ps_next()
                tr(pt_[:, 0:128], ptd_f[:, hf_, :], ident[:], [IDXB, CB], [pb_])
                ts("dve", idx32[:, hf_ * 128:(hf_ + 1) * 128], pt_[:, 0:128], 32.0, pmod[:, 0:1], ALU.mult, ALU.add, [pb_, CB], [IDXB])
            for i in range(4):
                S.op("dve", (lambda i=i: (lambda e: e.memset(pex2[:, i, :], 0.0)))(), [], [PX2[i]])
            S.op("dve", lambda e: e.memset(qpad[:], 0.0), [], [QPB])
            S.op("dve", lambda e: e.memset(rpad[:], 0.0), [], [QPB])

        def gather(s_, j4, slot):
            ct_, kt_, cb_ = ctile[slot]
            d_ = s_ * 16 + j4
            off = bass.IndirectOffsetOnAxis(ap=idx32[:, d_:d_ + 1], axis=0)
            S.op("pool", lambda e: e.indirect_dma_start(out=ct_[:].rearrange("p a b -> p (a b)"), out_offset=None, in_=ckv[:, :],
                                                        in_offset=off), [IDXB], [cb_], dma=True)
            off2 = bass.IndirectOffsetOnAxis(ap=idx32[:, d_:d_ + 1], axis=0)
            S.op("pool", lambda e: e.indirect_dma_start(out=kt_[:].rearrange("p a b -> p (a b)"), out_offset=None, in_=ckr[:, :],
                                                        in_offset=off2), [IDXB], [cb_], dma=True)

        def mla_sample(l):
            N = NS
            j = l - 2
            mla_q(l, N)
            for h in range(H):
                mla_q_head(l, h, N)
                pt_, pb_ = ps_next()
                for cc in range(2):
                    mm(pt_[:, cc * NS:(cc + 1) * NS], wukT[:, h, cc * 128:(cc + 1) * 128], qnT[:, :N], True, True, [WKT, QNB], [pb_])
                for cc in range(2):
                    cpy("act", qlat[:, cc, :].rearrange("p (s h) -> p s h", h=H)[:, :, h], pt_[:, cc * NS:(cc + 1) * NS], [pb_], [QLB])
                cpy("act", qrall[:, :].rearrange("p (s h) -> p s h", h=H)[:, :, h], qrT[:, :N], [QRB], [QRA])
            for s_ in range(NS):
                for cc in range(2):
                    cpy("dve", qpad[:, cc, s_, s_ * 8:(s_ + 1) * 8], qlat[:, cc, s_ * 8:(s_ + 1) * 8], [QLB], [QPB])
                cpy("dve", rpad[:, s_, s_ * 8:(s_ + 1) * 8], qrall[:, s_ * 8:(s_ + 1) * 8], [QRA], [QPB])
            po, pob = banks[4]
            pden, pdenb = banks[5]
            blocks = [(j4, sg) for j4 in range(16) for sg in range(4)]
            reqs = [(sg * 4 + k, j4) for (j4, sg) in blocks for k in range(4)]
            issued = {"n": 0}

            def ensure(n):
                while issued["n"] < min(n, len(reqs)):
                    s2, j42 = reqs[issued["n"]]
                    gather(s2, j42, issued["n"] % NCT)
                    issued["n"] += 1

            first = True
            for bi, (j4, sg) in enumerate(blocks):
                ensure(bi * 4 + NCT)
                psc, pscb = banks[6 + (bi % 2)]

                def T_(k):
                    ct_, kt_, cb_ = ctile[(bi * 4 + k) % NCT]
                    slot = (bi * 4 + k) % 2
                    ptA, ptAb = ps_next()
                    pvA = ptA[:].bitcast(BF16)
                    for cc in range(2):
                        for t4 in range(4):
                            tr(pvA[:, (cc * 4 + t4) * 128:(cc * 4 + t4 + 1) * 128], ct_[:, t4, cc * 128:(cc + 1) * 128], identb[:],
                               [cb_, CB], [ptAb])
                    cpy("act", cTs[:, slot, :], pvA[:, :], [ptAb], [CTS[slot]])
                    ptB, ptBb = ps_next()
                    pvB = ptB[:].bitcast(BF16)
                    for t4 in range(4):
                        tr(pvB[:64, t4 * 128:(t4 + 1) * 128], kt_[:, t4, :], identb[:], [cb_, CB], [ptBb])
                    cpy("dve", rTs[:, slot, :], pvB[:64, 0:512], [ptBb], [RTS[slot]])

                def S_(k):
                    s_ = sg * 4 + k
                    slot = (bi * 4 + k) % 2
                    for cc in range(2):
                        mm(psc[:, :], qpad[:, cc, s_, :], cTs[:, slot, cc * 512:(cc + 1) * 512], k == 0 and cc == 0, False,
                           [QPB, CTS[slot]], [pscb])
                    mm(psc[:, :], rpad[:, s_, :], rTs[:, slot, :], False, k == 3, [QPB, RTS[slot]], [pscb])

                T_(0)
                for k in range(4):
                    if k + 1 < 4:
                        T_(k + 1)
                    S_(k)
                band = slice(32 * sg, 32 * sg + 32)
                act(pex2[band, sg, :], psc[band, :], AF.Exp, [pscb], [PX2[sg]], scale=SCALE)
                ptP, ptPb = ps_next()
                pvP = ptP[:].bitcast(BF16)
                for t4 in range(4):
                    tr(pvP[:, t4 * 128:(t4 + 1) * 128], pex2[:, sg, t4 * 128:(t4 + 1) * 128], identb[:], [PX2[sg], CB], [ptPb])
                cpy("act", pTs[:].rearrange("p a b -> p (a b)"), pvP[:, 0:512], [ptPb], [PTS])
                for t4 in range(4):
                    mm(pden[:, 0:128], onesb[:], pTs[:, t4, :], first and t4 == 0, False, [CB, PTS], [pdenb], skip_group_check=True)
                for k in range(4):
                    s_ = sg * 4 + k
                    ct_, kt_, cb_ = ctile[(bi * 4 + k) % NCT]
                    for t4 in range(4):
                        for cc in range(2):
                            mm(po[:, cc * 128 + s_ * 8:cc * 128 + s_ * 8 + 8], ct_[:, t4, cc * 128:(cc + 1) * 128],
                               pTs[:, t4, s_ * 8:(s_ + 1) * 8], bi == 0 and k == 0 and t4 == 0 and cc == 0, False, [cb_, PTS], [pob], skip_group_check=True)
                first = False
            col0 = T
            psn, psnb = ps_next()
            for cc in range(2):
                mm(psn[:, :NS], qlat[:, cc, :], cT_all[:, cc, col0:col0 + NS], cc == 0, False, [QLB, CKB[NG]], [psnb])
            mm(psn[:, :NS], qrall[:, :], krT_all[:, col0:col0 + NS], False, True, [QRA, CKB[NG]], [psnb])
            act(pnew[:, :], psn[:, :NS], AF.Exp, [psnb], [PNB], scale=SCALE)
            tt("dve", pnew[:, :], pnew[:, :], newmask[:, :], ALU.mult, [PNB, CB], [PNB])
            ptn, ptnb = ps_next()
            tr(ptn[:NS, 0:128], pnew[:, :], ident[:], [PNB, CB], [ptnb])
            cpy("act", pnewT[:, :], ptn[:NS, 0:128], [ptnb], [PNB])
            mm(pden[:, 0:128], onesb[:NS, :], pnewT[:, :], False, True, [CB, PNB], [pdenb], skip_group_check=True)
            ptc, ptcb = ps_next()
            pvc = ptc[:].bitcast(BF16)
            for cc in range(2):
                tr(pvc[:NS, cc * 128:(cc + 1) * 128], cT_all[:, cc, col0:col0 + NS], identb[:], [CKB[NG], CB], [ptcb])
            cpy("act", ctmb[:, :], pvc[:NS, 0:KVL], [ptcb], [PNB])
            for cc in range(2):
                mm(po[:, cc * 128:(cc + 1) * 128], ctmb[:, cc * 128:(cc + 1) * 128], pnewT[:, :], False, True, [PNB], [pob],
                   skip_group_check=True)
            recip(rstd[:, :128], pden[:, 0:128], [pdenb], [RB])
            for cc in range(2):
                tt("dve", olat[:, cc, :], po[:, cc * 128:(cc + 1) * 128], rstd[:, :128], ALU.mult, [pob, RB], [OLB])
            for h in range(H):
                pt_, pb_ = ps_next()
                for cc in range(2):
                    mm(pt_[:, :NS], wukv[:, cc, h * 256 + 128:h * 256 + 256], olat[:, cc, :].rearrange("p (s h) -> p s h", h=H)[:, :, h],
                       cc == 0, cc == 1, [WUB, OLB], [pb_])
                cpy("act", A.big[:, h, :NS], pt_[:, :NS], [pb_], [A.BGB])
            linear("w_out_b", j * D, D, 0, D, lambda kc: A.big[:, kc, :N], [A.BGB], N,
                   lambda mi, mw, p_, b_: to_mix(mi, mw, p_, b_, N))
            postnorm_add(4 * l + 1, N)

        def sample_group():
            N = NS
            S.barrier()
            A.xT, A.hT, A.mixT, A.big = xTs, hTs, mixTs, bigs
            A.XB, A.HB, A.MB, A.BGB = [Buf("xTs%d" % i) for i in range(8)], [Buf("hTs%d" % i) for i in range(8)], [Buf("mixTs%d" % i) for i in range(8)], Buf("bigs")
            load_x(xs[:, :], NS, 0)
            if "sA" not in DBG:
                decode_setup_h()
            for l in range(n_layers):
                if "sA" in DBG or "sB" in DBG:
                    break
                if l < 2:
                    hgrn_sample(l)
                else:
                    if l == 2:
                        S.barrier()
                        decode_setup()
                        dma("sp", cosF[:, :N], cd["cosF"][:, T:T + N], [], [CSB])
                        dma("sp", sinF[:, :N], cd["sinF"][:, T:T + N], [], [CSB])
                        mla_shared(N, T, T, cs[:, :], krs[:, :], CKB[NG])
                    mla_sample(l)
                if "noffn" not in DBG:
                    ffn(l, N)
            store_x(ys[:, :], NS, 0)

        dma("pool", wukv[:], w_ukv.rearrange("(k p) c -> p k c", p=128), [], [WUB])
        for l in range(2):
            for h in range(H):
                S.op("dve", (lambda l=l, h=h: (lambda e: e.memset(Sst[l][:, h, :], 0.0)))(), [], [SSB[l][h]])

        for g in groups:
            if g == "s":
                sample_group()
                continue
            N = GN
            for tt_ in range(4):
                load_x(xp[g * GN + tt_ * 128: g * GN + (tt_ + 1) * 128, :], 128, tt_ * 128)
            for l in range(n_layers):
                if l < 2:
                    if "nohgrn" not in DBG:
                        hgrn_prompt(l, g)
                else:
                    if l == 2:
                        dma("sp", cosF[:, :N], cd["cosF"][:, g * GN:g * GN + N], [], [CSB])
                        dma("sp", sinF[:, :N], cd["sinF"][:, g * GN:g * GN + N], [], [CSB])
                        mla_shared(N, g * GN, g * GN, cp[g * GN:(g + 1) * GN, :], krp[g * GN:(g + 1) * GN, :], CKB[g])
                    mla_prompt(l, g)
                if "noffn" not in DBG:
                    ffn(l, N)
            for tt_ in range(4):
                store_x(yp[g * GN + tt_ * 128: g * GN + (tt_ + 1) * 128, :], 128, tt_ * 128)

        if "dump" in DBG:
            for nm_, t_, bufs_ in (("hT", hT, [A.HB]), ("big", big, [A.BGB]), ("mixT", mixT, [A.MB]), ("rstd", rstd, [RB]), ("xT", xT, [A.XB]),
                                     ("hqd", hqd, [HQD]), ("hkd", hkd, [HKD]), ("hkd2", hkd2, [HKD2]), ("hv", hv, [HV]), ("hA", hA, [HA]),
                                     ("lb_sb", lb_sb, [CB]), ("oml_sb", oml_sb, [CB]), ("lbl_sb", lbl_sb, [CB]), ("hq", hq, [HQ]), ("hb", hb, [HBB]), ("hlf", hlf, [HLF]), ("hf", hf, [HF]), ("hk", hk, [HK]), ("hbl", hbl, [HBL]), ("hgate", hgate, [HGT]), ("hos", hos, [HOS]), ("hkd2T", hkd2T, [HKT])):
                shp = list(t_.shape)
                flat = [shp[0], int(np.prod(shp[1:]))]
                dd = nc.dram_tensor("dbg_" + nm_, flat, t_.dtype, kind="ExternalOutput").ap()
                src = t_[:] if len(shp) == 2 else t_[:].rearrange("p a b -> p (a b)")
                dma("sp", dd, src, bufs_, [OUTB])
        S.op("sp", None, [OUTB], [])
        block = stack.enter_context(nc.Block())
        S.emit(block)
    return nc, req_log


def build2(n_pool, **kw):
    _, plan = build(n_pool, plan=None, **kw)
    nc, _ = build(n_pool, plan=plan, **kw)
    return nc


def _cols(v, kc):
    v = np.asarray(v, np.float32)
    R_ = v.shape[0]
    return np.ascontiguousarray(v.reshape(R_, kc, 128).transpose(2, 0, 1).reshape(128, R_ * kc))


def shared_inputs(inp):
    f = lambda a: np.ascontiguousarray(np.asarray(a, np.float32))
    m = {}
    n_pool = inp["cache_kv_latent"].shape[0]
    m["ckv"] = f(inp["cache_kv_latent"]).reshape(n_pool * 32, 4 * KVL)
    m["ckr"] = f(inp["cache_k_rope"]).reshape(n_pool * 32, 4 * RD)
    m["gains"] = _cols(f(inp["norm_gains"]).reshape(16, D), 8)
    m["w_ffn_in"] = f(inp["w_ffn_in"]).reshape(4 * D, 2 * DFF)
    m["w_ffn_out"] = f(inp["w_ffn_out"]).reshape(4 * DFF, D)
    m["w_in_a"] = f(inp["w_in_a"]).reshape(2 * D, 4 * D)
    m["lbl"] = _cols(f(inp["lb_logits"]), 8)
    m["gna"] = _cols(f(inp["g_norm_a"]), 8)
    m["w_out_a"] = f(inp["w_out_a"]).reshape(2 * D, D)
    m["kvn"] = _cols(f(inp["kv_norm"]).reshape(1, D), 8)
    m["w_dkv"] = f(inp["w_dkv"])
    m["kvan"] = _cols(f(inp["kv_a_norm"]).reshape(1, KVL), 2)
    m["kvan_b"] = np.ascontiguousarray(np.broadcast_to(f(inp["kv_a_norm"]).reshape(1, KVL), (128, KVL)))
    m["w_ukv"] = f(inp["w_ukv"])
    m["w_dq"] = f(inp["w_dq"]).reshape(2 * D, QL)
    m["qan"] = _cols(f(inp["q_a_norm"]), 3)
    m["w_uq"] = f(inp["w_uq"]).reshape(2 * QL, H * 192)
    m["w_out_b"] = f(inp["w_out_b"]).reshape(2 * D, D)
    for k, v in _consts().items():
        m["c_" + k] = v
    return m


def core_inputs(inp, shared, c):
    m = dict(shared)
    m["xp"] = np.ascontiguousarray(np.asarray(inp["x_prompt"][c], np.float32))
    m["xs"] = np.ascontiguousarray(np.asarray(inp["x_sample"][c * NS:(c + 1) * NS, 0], np.float32))
    m["st"] = np.ascontiguousarray(np.asarray(inp["state_hgrn"][:, c * NS:(c + 1) * NS], np.float32)).reshape(2 * NS * H * 128, 128)
    m["ptd"] = np.ascontiguousarray(np.asarray(inp["page_table"][c * NS:(c + 1) * NS], np.int32)).reshape(NS * 16, 4)
    return m


def kernel(**inp):
    n_cores = 8
    n_pool = inp["cache_kv_latent"].shape[0]
    nc = build2(n_pool)
    shared = shared_inputs(inp)
    in_maps = [core_inputs(inp, shared, c) for c in range(n_cores)]
    res = run_bass_kernel_spmd(nc, in_maps, core_ids=list(range(n_cores))).results
    y_p = np.stack([r["yp"] for r in res]).reshape(8, T, D)
    y_s = np.concatenate([r["ys"] for r in res]).reshape(128, 1, D)
    st_p = np.stack([r["stp"].reshape(2, H, 128, 128) for r in res], axis=1)
    c_p = np.stack([r["cp"] for r in res]).reshape(8, T, KVL)
    kr_p = np.stack([r["krp"] for r in res]).reshape(8, T, RD)
    st_s = np.concatenate([r["sts"].reshape(2, NS, H, 128, 128) for r in res], axis=1)
    c_s = np.concatenate([r["cs"] for r in res]).reshape(128, 1, KVL)
    kr_s = np.concatenate([r["krs"] for r in res]).reshape(128, 1, RD)
    return tuple(np.ascontiguousarray(a.astype(np.float32)) for a in (y_p, y_s, st_p, c_p, kr_p, st_s, c_s, kr_s))
```

```python
import contextlib
import os
import numpy as np
import concourse.bass as bass
import concourse.mybir as mybir
from concourse.bass_utils import run_bass_kernel_spmd

F32 = mybir.dt.float32
BF16 = mybir.dt.bfloat16
I32 = mybir.dt.int32
AF = mybir.ActivationFunctionType
ALU = mybir.AluOpType

D = 1024
T = 2048
NS = 16
H = 8
DFF = 2816
QL = 384
KVL = 256
RD = 64
PAST = 8192
NPG = 64
EPS = 1e-6
SCALE = (128 + 64) ** -0.5
DBG = set(os.environ.get("KDBG", "").split(","))
GN = 512
NG = T // GN


class Buf:
    __slots__ = ("name", "w", "r")

    def __init__(self, name=""):
        self.name = name
        self.w = {}
        self.r = {}


class Sched:
    ENG = ("pe", "act", "dve", "pool", "sp")

    def __init__(self, nc, stack, n_dsem=40):
        self.nc = nc
        self.ops = {e: [] for e in self.ENG}
        self.esem = {e: stack.enter_context(nc.semaphore("E" + e)) for e in self.ENG}
        self.dsem = [stack.enter_context(nc.semaphore("D%d" % i)) for i in range(n_dsem)]
        self.dval = [0] * n_dsem
        self.dnext = 0
        self.waited = {e: {} for e in self.ENG}

    def _need(self, eng, dep, waits, war=False):
        if dep is None:
            return
        if dep[0] == "e":
            _, pe, idx = dep
            if pe == eng and (eng == "pe" or war):
                return
            key = ("e", pe)
            if self.waited[eng].get(key, -1) >= idx:
                return
            self.waited[eng][key] = idx
            self.ops[pe][idx]["sig"] = True
            waits.append(dep)
        else:
            _, j, val = dep
            key = ("d", j)
            if self.waited[eng].get(key, -1) >= val:
                return
            self.waited[eng][key] = val
            waits.append(dep)

    @staticmethod
    def _flat(bufs):
        out = []
        for b in bufs:
            if isinstance(b, (list, tuple)):
                out.extend(Sched._flat(b))
            else:
                out.append(b)
        return out

    def op(self, eng, fn, reads=(), writes=(), dma=False):
        reads = self._flat(reads)
        writes = self._flat(writes)
        waits = []
        for b in reads:
            for d in b.w.values():
                self._need(eng, d, waits)
        for b in writes:
            for d in b.w.values():
                if dma and d[0] == "d" and not b.r:
                    continue
                self._need(eng, d, waits)
            for d in b.r.values():
                self._need(eng, d, waits, war=True)
        idx = len(self.ops[eng])
        rec = dict(fn=fn, waits=waits, sig=False, dma=None)
        if dma:
            j = self.dnext
            self.dnext = (self.dnext + 1) % len(self.dsem)
            if self.dval[j] > 0:
                self._need(eng, ("d", j, self.dval[j]), waits)
            self.dval[j] += 16
            rec["dma"] = j
            ev = ("d", j, self.dval[j])
            key = ("d", j)
        else:
            ev = ("e", eng, idx)
            key = ("e", eng)
        self.ops[eng].append(rec)
        for b in reads:
            b.r[key] = ev
        for b in writes:
            if dma and not b.r:
                b.w = {k: v for k, v in b.w.items() if k[0] == "d"}
                b.w[key] = ev
            else:
                b.w = {key: ev}
            b.r = {}
        return ev

    def barrier(self):
        last = {}
        for e in self.ENG:
            idx = len(self.ops[e]) - 1
            while idx >= 0 and (self.ops[e][idx]["fn"] is None or self.ops[e][idx]["dma"] is not None):
                idx -= 1
            last[e] = idx
        dvals = list(self.dval)
        for e in self.ENG:
            waits = []
            for pe, idx in last.items():
                if pe != e and idx >= 0:
                    self._need(e, ("e", pe, idx), waits)
            for j, v in enumerate(dvals):
                if v > 0:
                    self._need(e, ("d", j, v), waits)
            self.ops[e].append(dict(fn=None, waits=waits, sig=False, dma=None))

    def emit(self, block):
        for e in self.ENG:
            c = 0
            for rec in self.ops[e]:
                if rec["sig"]:
                    c += 1
                rec["sval"] = c
        S = self

        def run(name, eng):
            for rec in S.ops[name]:
                for d in rec["waits"]:
                    if d[0] == "e":
                        eng.wait_ge(S.esem[d[1]], S.ops[d[1]][d[2]]["sval"])
                    else:
                        eng.wait_ge(S.dsem[d[1]], d[2])
                if rec["fn"] is None:
                    continue
                ins = rec["fn"](eng)
                if rec["dma"] is not None:
                    ins.then_inc(S.dsem[rec["dma"]], 16)
                elif rec["sig"]:
                    ins.then_inc(S.esem[name], 1)

        @block.tensor
        def _(e):
            run("pe", e)

        @block.scalar
        def _(e):
            run("act", e)

        @block.vector
        def _(e):
            run("dve", e)

        @block.gpsimd
        def _(e):
            run("pool", e)

        @block.sync
        def _(e):
            run("sp", e)


def _consts():
    c = {}
    c["ident"] = np.eye(128, dtype=np.float32)
    s = np.arange(128)[:, None]
    t = np.arange(128)[None, :]
    c["hmask"] = ((s // 32 == t // 32) & (s <= t)).astype(np.float32)
    c["ctri"] = (s <= t).astype(np.float32)
    r = np.ones((128, GN), np.float32)
    r[:, ::32] = 0.0
    c["reset"] = r
    half = RD // 2
    inv = (np.float32(10000.0) ** (-(np.arange(half, dtype=np.float32) / np.float32(half)))).astype(np.float32)
    pos = np.concatenate([np.arange(T), np.full(NS, PAST)]).astype(np.float32)
    ang = (pos[:, None] * inv[None, :]).astype(np.float32).astype(np.float64)
    cos = np.cos(ang).astype(np.float32)
    sin = np.sin(ang).astype(np.float32)
    c["cosT"] = cos
    c["sinT"] = sin
    c["cosF"] = np.ascontiguousarray(np.concatenate([cos, cos], 1).T)
    c["sinF"] = np.ascontiguousarray(np.concatenate([sin, sin], 1).T)
    sel = np.zeros((NS, NS, 128), np.float32)
    for i in range(NS):
        sel[i, i, :] = 1.0
    c["sel"] = sel.reshape(NS, NS * 128)
    mp = np.zeros((NS, NS, H), np.float32)
    for i in range(NS):
        mp[i, i, :] = 1.0
    c["maskpad"] = np.ascontiguousarray(np.broadcast_to(mp.reshape(1, NS * NS * H), (128, NS * NS * H)))
    nm = np.zeros((NS, H, NS), np.float32)
    for i in range(NS):
        nm[i, :, i] = 1.0
    c["newmask"] = nm.reshape(NS * H, NS)
    c["cmask"] = (np.arange(128)[:, None] // 32 == np.arange(4)[None, :]).astype(np.float32)
    c["pmod"] = (np.arange(128) % 32).astype(np.float32).reshape(128, 1)
    return c


CONST_SHAPES = {k: v.shape for k, v in _consts().items()}


def build(n_pool, groups=(0, 1, 2, 3, "s"), n_layers=4, plan=None):
    nc = bass.Bass("TRN2", target_bir_lowering=False)
    req_log = []
    dram = {}

    def din(name, shape, dt=F32):
        dram[name] = nc.dram_tensor(name, list(shape), dt, kind="ExternalInput").ap()
        return dram[name]

    def dout(name, shape, dt=F32):
        dram[name] = nc.dram_tensor(name, list(shape), dt, kind="ExternalOutput").ap()
        return dram[name]

    xp = din("xp", [T, D])
    xs = din("xs", [NS, D])
    st = din("st", [2 * NS * H * 128, 128])
    ckv = din("ckv", [n_pool * 32, 4 * KVL])
    ckr = din("ckr", [n_pool * 32, 4 * RD])
    ptd = din("ptd", [NS * 16, 4], I32)
    gains = din("gains", [128, 16 * 8])
    w_ffn_in = din("w_ffn_in", [4 * D, 2 * DFF])
    w_ffn_out = din("w_ffn_out", [4 * DFF, D])
    w_in_a = din("w_in_a", [2 * D, 4 * D])
    lbl = din("lbl", [128, 2 * 8])
    gna = din("gna", [128, 2 * 8])
    w_out_a = din("w_out_a", [2 * D, D])
    kvn = din("kvn", [128, 8])
    w_dkv = din("w_dkv", [D, KVL + RD])
    kvan = din("kvan", [128, 2])
    kvan_b = din("kvan_b", [128, KVL])
    w_ukv = din("w_ukv", [KVL, H * 256])
    w_dq = din("w_dq", [2 * D, QL])
    qan = din("qan", [128, 2 * 3])
    w_uq = din("w_uq", [2 * QL, H * 192])
    w_out_b = din("w_out_b", [2 * D, D])
    cd = {k: din("c_" + k, shp) for k, shp in CONST_SHAPES.items()}

    yp = dout("yp", [T, D])
    ys = dout("ys", [NS, D])
    stp = dout("stp", [2 * H * 128, 128])
    cp = dout("cp", [T, KVL])
    krp = dout("krp", [T, RD])
    sts = dout("sts", [2 * NS * H * 128, 128])
    cs = dout("cs", [NS, KVL])
    krs = dout("krs", [NS, RD])

    with contextlib.ExitStack() as stack:
        S = Sched(nc, stack)

        def sb(name, shape, dt=F32):
            return stack.enter_context(nc.sbuf_tensor(name, list(shape), dt))

        def mm(out, lhsT, rhs, start, stop, reads, writes, **kw):
            S.op("pe", lambda e: e.matmul(out, lhsT, rhs, start=start, stop=stop, **kw), reads, writes)

        def tr(out, in_, ident, reads, writes):
            S.op("pe", lambda e: e.transpose(out, in_, ident), reads, writes)

        def act(out, in_, func, reads, writes, scale=1.0, bias=0.0, accum_out=None):
            kw = {}
            if accum_out is not None:
                kw["accum_out"] = accum_out
            S.op("act", lambda e: e.activation(out, in_, func, bias=bias, scale=scale, **kw), reads, writes)

        def tt(eng, out, in0, in1, op, reads, writes):
            S.op(eng, lambda e: e.tensor_tensor(out, in0, in1, op), reads, writes)

        def ts(eng, out, in0, s1, s2, op0, op1, reads, writes):
            S.op(eng, lambda e: e.tensor_scalar(out, in0, s1, s2, op0, op1), reads, writes)

        def stt(out, in0, scalar, in1, op0, op1, reads, writes):
            S.op("dve", lambda e: e.scalar_tensor_tensor(out, in0, scalar, in1, op0, op1), reads, writes)

        def cpy(eng, out, in_, reads, writes):
            if eng == "act":
                S.op("act", lambda e: e.copy(out, in_), reads, writes)
            else:
                S.op(eng, lambda e: e.tensor_copy(out, in_), reads, writes)

        def recip(out, in_, reads, writes):
            S.op("dve", lambda e: e.reciprocal(out, in_), reads, writes)

        def dma(eng, out, in_, reads, writes):
            S.op(eng, lambda e: e.dma_start(out=out, in_=in_), reads, writes, dma=True)

        banks = []
        for i in range(8):
            t_ = stack.enter_context(nc.psum_tensor("ps%d" % i, [128, 512], F32))
            banks.append((t_, Buf("ps%d" % i)))
        rot = {"i": 0, "n": 4}

        def ps_next():
            i = rot["i"]
            rot["i"] = (i + 1) % rot["n"]
            return banks[i]

        CB = Buf("consts")
        ident = sb("ident", [128, 128])
        identb = sb("identb", [128, 128], BF16)
        onesb = sb("onesb", [128, 128], BF16)
        hmask = sb("hmask", [128, 128])
        ctri = sb("ctri", [128, 128], BF16)
        ctri_f = sb("ctri_f", [128, 128])
        reset = sb("reset", [128, GN])
        cmask = sb("cmask", [128, 4])
        cmask4 = sb("cmask4", [128, 4, 128], BF16)
        gains_sb = sb("gains_sb", [128, 16, 8])
        lbl_sb = sb("lbl_sb", [128, 2, 8])
        gna_sb = sb("gna_sb", [128, 2, 8])
        kvn_sb = sb("kvn_sb", [128, 8])
        kvan_sb = sb("kvan_sb", [128, 2])
        kvanb_sb = sb("kvanb_sb", [128, KVL])
        qan_sb = sb("qan_sb", [128, 2, 3])
        lb_sb = sb("lb_sb", [128, 2, 8])
        oml_sb = sb("oml_sb", [128, 2, 8])
        lbtmp = sb("lbtmp", [128, 4, 8])
        for dst, src in ((ident, cd["ident"]), (hmask, cd["hmask"]), (ctri_f, cd["ctri"]), (reset, cd["reset"]), (cmask, cd["cmask"]),
                         (gains_sb, gains.rearrange("p (a k) -> p a k", a=16)),
                         (lbl_sb, lbl.rearrange("p (a k) -> p a k", a=2)),
                         (gna_sb, gna.rearrange("p (a k) -> p a k", a=2)),
                         (kvn_sb, kvn), (kvan_sb, kvan), (kvanb_sb, kvan_b),
                         (qan_sb, qan.rearrange("p (a k) -> p a k", a=2))):
            dma("sp", dst[:], src, [], [CB])
        cpy("dve", identb[:], ident[:], [CB], [CB])
        cpy("dve", ctri[:], ctri_f[:], [CB], [CB])
        S.op("dve", lambda e: e.memset(onesb[:], 1.0), [], [CB])
        cpy("dve", cmask4[:], cmask[:].unsqueeze(2).to_broadcast([128, 4, 128]), [CB], [CB])
        act(lbtmp[:, 0:2, :], lbl_sb[:], AF.Exp, [CB], [CB])
        tt("dve", lbtmp[:, 2, :], lbtmp[:, 0, :], lbtmp[:, 1, :], ALU.add, [CB], [CB])
        recip(lbtmp[:, 3, :], lbtmp[:, 2, :], [CB], [CB])
        tt("dve", lbtmp[:, 0, :], lbtmp[:, 0, :], lbtmp[:, 3, :], ALU.mult, [CB], [CB])
        tt("dve", lbtmp[:, 1, :], lbtmp[:, 1, :], lbtmp[:, 3, :], ALU.mult, [CB], [CB])
        tt("dve", lb_sb[:, 0, :], lbtmp[:, 0, :], lbtmp[:, 0, :], ALU.subtract, [CB], [CB])
        tt("dve", lbtmp[:, 2, :], lbtmp[:, 0, :], lbtmp[:, 1, :], ALU.add, [CB], [CB])
        tt("dve", lb_sb[:, 1, :], lbtmp[:, 2, :], lbtmp[:, 0, :], ALU.subtract, [CB], [CB])
        ts("dve", oml_sb[:], lb_sb[:], -1.0, 1.0, ALU.mult, ALU.add, [CB], [CB])

        WSLOT = 22 * 128
        NWS = 4
        wring = [(sb("wr%d" % i, [128, WSLOT], BF16), Buf("wr%d" % i)) for i in range(NWS)]
        wstate = {"issued": 0, "cur": 0}

        def _issue(i):
            wd, r0, K, c0, pw = plan[i]
            KC = K // 128
            t_, b_ = wring[i % NWS]
            view = t_[:, :KC * pw].rearrange("p (k c) -> p k c", k=KC)
            src = dram[wd][r0:r0 + K, c0:c0 + pw].rearrange("(k p) c -> p k c", p=128)
            dma("pool", view, src, [], [b_])

        def panel(wd, r0, K, c0, pw):
            i = wstate["cur"]
            wstate["cur"] += 1
            req_log.append((wd, r0, K, c0, pw))
            KC = K // 128
            if plan is None:
                t_, b_ = wring[i % NWS]
                view = t_[:, :KC * pw].rearrange("p (k c) -> p k c", k=KC)
                src = dram[wd][r0:r0 + K, c0:c0 + pw].rearrange("(k p) c -> p k c", p=128)
                dma("pool", view, src, [], [b_])
                return view, b_
            assert plan[i] == (wd, r0, K, c0, pw), (i, plan[i], (wd, r0, K, c0, pw))
            while wstate["issued"] < min(i + NWS - 1, len(plan)):
                _issue(wstate["issued"])
                wstate["issued"] += 1
            t_, b_ = wring[i % NWS]
            return t_[:, :KC * pw].rearrange("p (k c) -> p k c", k=KC), b_

        def linear(wd, r0, K, c0, M, rhs_fn, rhs_bufs, N, consume, PW=256):
            KC = K // 128
            mi = 0
            for p0 in range(0, M, PW):
                pw = min(PW, M - p0)
                view, wb = panel(wd, r0, K, c0 + p0, pw)
                for m0 in range(0, pw, 128):
                    mw = min(128, pw - m0)
                    pt_, pb_ = ps_next()
                    for kc in range(KC):
                        rbk = [b[kc] if isinstance(b, list) and len(b) == KC else b for b in rhs_bufs]
                        mm(pt_[:mw, :N], view[:, kc, m0:m0 + mw], rhs_fn(kc), kc == 0, kc == KC - 1,
                           [wb] + rbk, [pb_])
                    consume(mi, mw, pt_, pb_)
                    mi += 1

        NMAX = GN
        class _NS:
            pass
        A = _NS()
        xT = sb("xT", [128, 8, NMAX])
        A.XB = [Buf("xT%d" % i) for i in range(8)]
        hT = sb("hT", [128, 8, NMAX], BF16)
        A.HB = [Buf("hT%d" % i) for i in range(8)]
        mixT = sb("mixT", [128, 8, NMAX])
        A.MB = [Buf("mixT%d" % i) for i in range(8)]
        sqT = sb("sqT", [128, 2, NMAX], BF16)
        SQB = [Buf("sqT0"), Buf("sqT1")]
        rstd = sb("rstd", [128, NMAX])
        RB = Buf("rstd")
        tmpN = sb("tmpN", [128, NMAX])
        TB = Buf("tmpN")
        big = sb("big", [128, 22, NMAX], BF16)
        A.BGB = Buf("big")
        A.xT, A.hT, A.mixT, A.big = xT, hT, mixT, big
        Sst = [sb("Sst%d" % l, [128, H, 128]) for l in range(2)]
        SSB = [[Buf("S%d_%d" % (l, h)) for h in range(H)] for l in range(2)]
        Stmp = sb("Stmp", [128, 128])
        STB = Buf("Stmp")
        Stmp2 = sb("Stmp2", [128, 128])
        STB2 = Buf("Stmp2")
        Sbf = sb("Sbf", [128, 4, 128], BF16)
        SBB = [Buf("Sbf%d" % i) for i in range(4)]
        cT_all = sb("cT_all", [128, 2, T + NS], BF16)
        krT_all = sb("krT_all", [64, T + NS], BF16)
        CKB = [Buf("ck%d" % g) for g in range(NG + 1)]
        wukv = sb("wukv", [128, 2, H * 256], BF16)
        WUB = Buf("wukv")
        cosF = sb("cosF", [64, NMAX])
        sinF = sb("sinF", [64, NMAX])
        CSB = Buf("cossin")

        def rstd_of(src_fn, src_bufs, KC, N, Dn):
            pt_, pb_ = ps_next()
            for kc in range(KC):
                sbk = [b[kc] if isinstance(b, list) and len(b) == 8 else b for b in src_bufs]
                act(sqT[:, kc % 2, :N], src_fn(kc), AF.Square, sbk, [SQB[kc % 2]])
                mm(pt_[:, :N], onesb[:], sqT[:, kc % 2, :N], kc == 0, kc == KC - 1, [SQB[kc % 2], CB], [pb_])
            ts("dve", tmpN[:, :N], pt_[:, :N], 1.0 / Dn, EPS, ALU.mult, ALU.add, [pb_], [TB])
            act(tmpN[:, :N], tmpN[:, :N], AF.Ln, [TB], [TB])
            act(rstd[:, :N], tmpN[:, :N], AF.Exp, [TB], [RB], scale=-0.5)

        def prenorm(gi, N, gain_tile=None):
            rstd_of(lambda kc: A.xT[:, kc, :N], [A.XB], 8, N, D)
            for kc in range(8):
                g_ = gain_tile[:, kc:kc + 1] if gain_tile is not None else gains_sb[:, gi, kc:kc + 1]
                stt(A.hT[:, kc, :N], A.xT[:, kc, :N], g_, rstd[:, :N], ALU.mult, ALU.mult, [A.XB[kc], RB, CB], [A.HB[kc]])

        def postnorm_add(gi, N):
            rstd_of(lambda kc: A.mixT[:, kc, :N], [A.MB], 8, N, D)
            for kc in range(8):
                stt(A.mixT[:, kc, :N], A.mixT[:, kc, :N], gains_sb[:, gi, kc:kc + 1], rstd[:, :N], ALU.mult, ALU.mult,
                    [A.MB[kc], RB, CB], [A.MB[kc]])
                tt("dve", A.xT[:, kc, :N], A.xT[:, kc, :N], A.mixT[:, kc, :N], ALU.add, [A.XB[kc], A.MB[kc]], [A.XB[kc]])

        def to_mix(mi, mw, pt_, pb_, N):
            cpy("act", A.mixT[:, mi, :N], pt_[:, :N], [pb_], [A.MB[mi]])

        def ffn(l, N):
            prenorm(4 * l + 2, N)
            gtmp = sb_ffn_g
            for j in range(DFF // 256):
                vg, bg = panel("w_ffn_in", l * D, D, j * 256, 256)
                vu, bu = panel("w_ffn_in", l * D, D, DFF + j * 256, 256)
                for m in range(2):
                    pg_, pgb = ps_next()
                    for kc in range(8):
                        mm(pg_[:, :N], vg[:, kc, m * 128:(m + 1) * 128], A.hT[:, kc, :N], kc == 0, kc == 7, [bg, A.HB[kc]], [pgb])
                    pu_, pub = ps_next()
                    for kc in range(8):
                        mm(pu_[:, :N], vu[:, kc, m * 128:(m + 1) * 128], A.hT[:, kc, :N], kc == 0, kc == 7, [bu, A.HB[kc]], [pub])
                    act(gtmp[:, :N], pg_[:, :N], AF.Silu, [pgb], [GTB])
                    tt("dve", A.big[:, 2 * j + m, :N], gtmp[:, :N], pu_[:, :N], ALU.mult, [GTB, pub], [A.BGB])
            linear("w_ffn_out", l * DFF, DFF, 0, D, lambda kc: A.big[:, kc, :N], [A.BGB], N,
                   lambda mi, mw, p_, b_: to_mix(mi, mw, p_, b_, N), PW=128)
            postnorm_add(4 * l + 3, N)

        sb_ffn_g = sb("gtmp", [128, NMAX])
        GTB = Buf("gtmp")

        hq = sb("hq", [128, NMAX])
        hf = sb("hf", [128, NMAX])
        hlf = sb("hlf", [128, NMAX])
        hb = sb("hb", [128, NMAX])
        hk = sb("hk", [128, NMAX])
        heb = sb("heb", [128, NMAX])
        hr = sb("hr", [128, NMAX])
        hbl = sb("hbl", [128, NMAX // 32])
        hgate = sb("hgate", [128, NMAX], BF16)
        hqd = sb("hqd", [128, NMAX], BF16)
        hkd = sb("hkd", [128, NMAX], BF16)
        hkd2 = sb("hkd2", [128, NMAX], BF16)
        hkd2T = sb("hkd2T", [128, 4, 128], BF16)
        hv = sb("hv", [128, 4, 128], BF16)
        hA = sb("hA", [128, 4, 128], BF16)
        hvm = sb("hvm", [128, 4, 128], BF16)
        HVM = [Buf("hvm%d" % i) for i in range(4)]
        hos = sb("hos", [128, NMAX])
        HQ, HF, HLF, HK, HEB, HR, HBL, HGT, HQD, HKD, HKD2, HKT, HV, HA, HOS = [Buf(n) for n in
            "hq hf hlf hk heb hr hbl hgate hqd hkd hkd2 hkd2T hv hA hos".split()]
        HBB = Buf("hb")

        def hgrn_post(l, h, po, pob, N, gate=None, gate_bufs=None):
            if gate is None:
                gate, gate_bufs = hgate[:, :N], [HGT]
                po = po[:, :N]
            cpy("act", hos[:, :N], po, [pob], [HOS])
            act(sqT[:, 0, :N], po, AF.Square, [pob], [SQB[0]])
            ps2, ps2b = ps_next()
            mm(ps2[:, :N], onesb[:], sqT[:, 0, :N], True, True, [SQB[0], CB], [ps2b])
            ts("dve", tmpN[:, :N], ps2[:, :N], 1.0 / 128, EPS, ALU.mult, ALU.add, [ps2b], [TB])
            act(tmpN[:, :N], tmpN[:, :N], AF.Ln, [TB], [TB])
            act(rstd[:, :N], tmpN[:, :N], AF.Exp, [TB], [RB], scale=-0.5)
            stt(hos[:, :N], hos[:, :N], gna_sb[:, l, h:h + 1], rstd[:, :N], ALU.mult, ALU.mult, [HOS, RB, CB], [HOS])
            tt("dve", A.big[:, h, :N], hos[:, :N], gate, ALU.mult, [HOS] + gate_bufs, [A.BGB])

        hqd2 = [hqd, sb("hqd_b", [128, NMAX], BF16)]
        hkd_2 = [hkd, sb("hkd_b", [128, NMAX], BF16)]
        hkd2T2 = [hkd2T, sb("hkd2T_b", [128, 4, 128], BF16)]
        hv2 = [hv, sb("hv_b", [128, 4, 128], BF16)]
        hA2 = [hA, sb("hA_b", [128, 4, 128], BF16)]
        hgate2 = [hgate, sb("hgate_b", [128, NMAX], BF16)]
        hbl2 = [hbl, sb("hbl_b", [128, NMAX // 32])]
        HQD2, HKD_2, HKT2, HV2, HA2, HGT2, HBL2 = [[Buf(n + "0"), Buf(n + "1")] for n in "hqd hkd hkt hv hA hgt hbl".split()]

        def hgrn_prompt(l, g):
            N = GN
            prenorm(4 * l + 0, N)

            def stageA1p(h, bs):
                for _ in stageA1(h, bs, 0):
                    pass

            def stageA1g(h, bs):
                return stageA1(h, bs, 1)

            def stageA1(h, bs, part):
                if part == 1:
                    yield from stageA1_gate(h, bs)
                    return
                pan = lambda sec: panel("w_in_a", l * D, D, sec * D + h * 128, 128)
                vq, bq = pan(0)
                pq, pqb = ps_next()
                for kc in range(8):
                    mm(pq[:, :N], vq[:, kc, :], A.hT[:, kc, :N], kc == 0, kc == 7, [bq, A.HB[kc]], [pqb])
                act(hq[:, :N], pq[:, :N], AF.Silu, [pqb], [HQ])
                yield
                vg, bg = pan(3)
                pg, pgb = ps_next()
                for kc in range(8):
                    mm(pg[:, :N], vg[:, kc, :], A.hT[:, kc, :N], kc == 0, kc == 7, [bg, A.HB[kc]], [pgb])
                act(hgate2[bs][:, :N], pg[:, :N], AF.Silu, [pgb], [HGT2[bs]])
                yield
                vf, bf_ = pan(1)
                pf, pfb = ps_next()
                for kc in range(8):
                    mm(pf[:, :N], vf[:, kc, :], A.hT[:, kc, :N], kc == 0, kc == 7, [bf_, A.HB[kc]], [pfb])
                act(hf[:, :N], pf[:, :N], AF.Sigmoid, [pfb], [HF])
                yield
                vi, bi = pan(2)
                pv, pvb = ps_next()
                for tt_ in range(4):
                    for kc in range(8):
                        mm(pv[:, tt_ * 128:(tt_ + 1) * 128], A.hT[:, kc, tt_ * 128:(tt_ + 1) * 128], vi[:, kc, :],
                           kc == 0, kc == 7, [bi, A.HB[kc]], [pvb])
                cpy("act", hv2[bs][:].rearrange("p a b -> p (a b)"), pv[:, :], [pvb], [HV2[bs]])
                yield

            def stageA1_gate(h, bs):
                ts("dve", hf[:, :N], hf[:, :N], oml_sb[:, l, h:h + 1], lb_sb[:, l, h:h + 1], ALU.mult, ALU.add, [HF, CB], [HF])
                ts("dve", hk[:, :N], hf[:, :N], -1.0, 1.0, ALU.mult, ALU.add, [HF], [HK])
                act(hlf[:, :N], hf[:, :N], AF.Ln, [HF], [HLF])
                yield
                S.op("dve", lambda e: e.tensor_tensor_scan(hb[:, :N], reset[:, :N], hlf[:, :N], 0.0, ALU.mult, ALU.add),
                     [HLF, CB], [HBB])
                b3 = hb[:, :N].rearrange("p (c t) -> p c t", t=32)
                cpy("dve", hbl2[bs][:, :N // 32], b3[:, :, 31], [HBB], [HBL2[bs]])
                act(heb[:, :N], hb[:, :N], AF.Exp, [HBB], [HEB])
                yield
                stt(hqd2[bs][:, :N], hq[:, :N], 128 ** -0.5, heb[:, :N], ALU.mult, ALU.mult, [HQ, HEB], [HQD2[bs]])
                act(heb[:, :N], hb[:, :N], AF.Exp, [HBB], [HEB], scale=-1.0)
                yield
                tt("dve", hkd_2[bs][:, :N], hk[:, :N], heb[:, :N], ALU.mult, [HK, HEB], [HKD_2[bs]])
                tt("dve", hr[:, :N].rearrange("p (c t) -> p c t", t=32),
                   hbl2[bs][:, :N // 32].unsqueeze(2).to_broadcast([128, N // 32, 32]), b3, ALU.subtract, [HBL2[bs], HBB], [HR])
                act(hr[:, :N], hr[:, :N], AF.Exp, [HR], [HR])
                yield
                tt("dve", hkd2[:, :N], hk[:, :N], hr[:, :N], ALU.mult, [HK, HR], [HKD2])
                act(hbl2[bs][:, :N // 32], hbl2[bs][:, :N // 32], AF.Exp, [HBL2[bs]], [HBL2[bs]])
                yield

            def stageA2(h, bs):
                ptb, ptbb = ps_next()
                ptv = ptb[:].bitcast(BF16)
                for tt_ in range(4):
                    tr(ptv[:, tt_ * 128:(tt_ + 1) * 128], hkd2[:, tt_ * 128:(tt_ + 1) * 128], identb[:], [HKD2, CB], [ptbb])
                cpy("act", hkd2T2[bs][:].rearrange("p a b -> p (a b)"), ptv[:, 0:512], [ptbb], [HKT2[bs]])
                psc, pscb = ps_next()
                for tt_ in range(4):
                    sl = slice(tt_ * 128, (tt_ + 1) * 128)
                    mm(psc[:, sl], hkd_2[bs][:, sl], hqd2[bs][:, sl], True, True, [HKD_2[bs], HQD2[bs]], [pscb])
                tt("dve", hA2[bs][:], psc[:].rearrange("p (a b) -> p a b", a=4),
                   hmask[:].unsqueeze(1).to_broadcast([128, 4, 128]), ALU.mult, [pscb, CB], [HA2[bs]])

            def stageB(h, bs):
                po, pob = banks[4 + (h % 2)]
                Srot = [(Sst[l][:, h, :], SSB[l][h]), (Stmp[:], STB), (Stmp2[:], STB2)]
                for tt_ in range(4):
                    sl = slice(tt_ * 128, (tt_ + 1) * 128)
                    pd, pdb = banks[6 + (tt_ % 2)]
                    tt("dve", hvm[:], hv2[bs][:, tt_, :].unsqueeze(1).to_broadcast([128, 4, 128]), cmask4[:], ALU.mult,
                       [HV2[bs], CB], [HVM[0]])
                    for c in range(4):
                        mm(pd[:, c * 128:(c + 1) * 128], hkd2T2[bs][:, tt_, :], hvm[:, c, :],
                           True, True, [HKT2[bs], HVM[0]], [pdb])
                    mm(po[:, sl], hv2[bs][:, tt_, :], hA2[bs][:, tt_, :], True, False, [HV2[bs], HA2[bs]], [pob])
                    for c in range(4):
                        ci = tt_ * 4 + c
                        Sc = Srot[ci % 3]
                        Sx = Srot[(ci + 1) % 3]
                        cpy("pool", Sbf[:, c, :], Sc[0], [Sc[1]], [SBB[c]])
                        mm(po[:, ci * 32:(ci + 1) * 32], Sbf[:, c, :], hqd2[bs][:, ci * 32:(ci + 1) * 32], False, c == 3,
                           [SBB[c], HQD2[bs]], [pob], skip_group_check=True)
                        stt(Sx[0], Sc[0], hbl2[bs][:, ci:ci + 1], pd[:, c * 128:(c + 1) * 128],
                            ALU.mult, ALU.add, [Sc[1], HBL2[bs], pdb], [Sx[1]])
                        if c % 2 == 1:
                            yield
                cpy("dve", Srot[0][0], Srot[1][0], [Srot[1][1]], [Srot[0][1]])
                hgrn_post(l, h, po[:, :N], pob, N, gate=hgate2[bs][:, :N], gate_bufs=[HGT2[bs]])
                if g == NG - 1:
                    dma("sp", stp.rearrange("(l h k) v -> l h k v", l=2, h=H)[l, h], Sst[l][:, h, :], [SSB[l][h]], [OUTB])
                yield

            stageA1p(0, 0)
            for _ in stageA1g(0, 0):
                pass
            stageA2(0, 0)
            import itertools
            for h in range(H):
                gb = stageB(h, h % 2)
                ga = (itertools.chain(stageA1(h + 1, (h + 1) % 2, 0), stageA1g(h + 1, (h + 1) % 2))
                      if h + 1 < H else iter(()))
                doneb = donea = False
                while not (doneb and donea):
                    if not donea:
                        try:
                            next(ga)
                        except StopIteration:
                            donea = True
                    if not doneb:
                        try:
                            next(gb)
                        except StopIteration:
                            doneb = True
                if h + 1 < H:
                    stageA2(h + 1, (h + 1) % 2)
            linear("w_out_a", l * D, D, 0, D, lambda kc: A.big[:, kc, :N], [A.BGB], N,
                   lambda mi, mw, p_, b_: to_mix(mi, mw, p_, b_, N))
            postnorm_add(4 * l + 1, N)

        OUTB = Buf("out")


        qaT = sb("qaT", [128, 3, NMAX], BF16)
        QAB = Buf("qaT")
        qnT = sb("qnT", [128, NMAX], BF16)
        QNB = Buf("qnT")
        qrT = sb("qrT", [64, NMAX], BF16)
        QRB = Buf("qrT")
        qnT_b = sb("qnT_b", [128, NMAX], BF16)
        qrT_b = sb("qrT_b", [64, NMAX], BF16)
        qn2 = [qnT, qnT_b]
        qr2 = [qrT, qrT_b]
        QNB2 = [QNB, Buf("qnT_b")]
        QRB2 = [QRB, Buf("qrT_b")]
        wrot = sb("wrot", [128, 8, 64], BF16)
        WRB = Buf("wrot")
        knT = sb("knT", [128, T], BF16)
        KNB = Buf("knT")
        vh = sb("vh", [128, T // 128, 128], BF16)
        VHB = Buf("vh")
        pexp = sb("pexp", [128, 2, NMAX], BF16)
        PXB = [Buf("pexp0"), Buf("pexp1")]
        rt1 = sb("rt1", [64, NMAX])
        rt2 = sb("rt2", [64, NMAX])
        RT1, RT2 = Buf("rt1"), Buf("rt2")
        ctm = sb("ctm", [128, KVL + RD])
        CTM = Buf("ctm")
        cst = sb("cst", [128, 2, 32])
        CST = Buf("cst")
        ssq = sb("ssq", [128, 4])
        SSQ = Buf("ssq")
        junk = sb("junk", [128, KVL])
        JNK = Buf("junk")

        def rope_fm(dst, pr, prb, prot, protb, N, dbufs):
            tt("dve", rt1[:, :N], pr[:64, :N], cosF[:, :N], ALU.mult, [prb, CSB], [RT1])
            tt("dve", rt2[:, :N], prot[:64, :N], sinF[:, :N], ALU.mult, [protb, CSB], [RT2])
            tt("dve", dst, rt1[:, :N], rt2[:, :N], ALU.add, [RT1, RT2], dbufs)

        def make_wrot(view, vb, KC, c0):
            ts("dve", wrot[:, :KC, 0:32], view[:, :, c0 + 32:c0 + 64], -1.0, None, ALU.mult, ALU.bypass, [vb], [WRB])
            cpy("dve", wrot[:, :KC, 32:64], view[:, :, c0:c0 + 32], [vb], [WRB])

        def mla_shared(N, col0, pos0, cp_ap, krp_ap, ckb):
            prenorm(None, N, gain_tile=kvn_sb)
            view, vb = panel("w_dkv", 0, D, 0, KVL + RD)
            make_wrot(view, vb, 8, KVL)
            for mc in range(2):
                pt_, pb_ = ps_next()
                for kc in range(8):
                    mm(pt_[:, :N], view[:, kc, mc * 128:(mc + 1) * 128], A.hT[:, kc, :N], kc == 0, kc == 7, [vb, A.HB[kc]], [pb_])
                cpy("act", A.mixT[:, mc, :N], pt_[:, :N], [pb_], [A.MB])
            pr, prb = ps_next()
            for kc in range(8):
                mm(pr[:64, :N], view[:, kc, KVL:KVL + RD], A.hT[:, kc, :N], kc == 0, kc == 7, [vb, A.HB[kc]], [prb])
            prot, protb = ps_next()
            for kc in range(8):
                mm(prot[:64, :N], wrot[:, kc, :], A.hT[:, kc, :N], kc == 0, kc == 7, [WRB, A.HB[kc]], [protb])
            rope_fm(krT_all[:, col0:col0 + N], pr, prb, prot, protb, N, [ckb])
            rstd_of(lambda kc: A.mixT[:, kc, :N], [A.MB], 2, N, KVL)
            for mc in range(2):
                stt(cT_all[:, mc, col0:col0 + N], A.mixT[:, mc, :N], kvan_sb[:, mc:mc + 1], rstd[:, :N], ALU.mult, ALU.mult,
                    [A.MB, RB, CB], [ckb])
            for t0 in range(0, N, 128):
                rows = min(128, N - t0)
                pt_, pb_ = ps_next()
                for kc in range(8):
                    mm(pt_[:rows, :KVL + RD], A.hT[:, kc, t0:t0 + rows], view[:, kc, :], kc == 0, kc == 7, [vb, A.HB[kc]], [pb_])
                act(junk[:rows, :], pt_[:rows, :KVL], AF.Square, [pb_], [JNK, SSQ], accum_out=ssq[:rows, 0:1])
                act(ssq[:rows, 1:2], ssq[:rows, 0:1], AF.Sqrt, [SSQ], [SSQ], scale=1.0 / KVL, bias=EPS)
                recip(ssq[:rows, 2:3], ssq[:rows, 1:2], [SSQ], [SSQ])
                stt(ctm[:rows, :KVL], pt_[:rows, :KVL], ssq[:rows, 2:3], kvanb_sb[:rows, :], ALU.mult, ALU.mult, [pb_, SSQ, CB], [CTM])
                dma("sp", cst[:rows, 0, :], cd["cosT"][pos0 + t0:pos0 + t0 + rows, :], [], [CST])
                dma("sp", cst[:rows, 1, :], cd["sinT"][pos0 + t0:pos0 + t0 + rows, :], [], [CST])
                x1 = pt_[:rows, KVL:KVL + 32]
                x2 = pt_[:rows, KVL + 32:KVL + 64]
                o1 = ctm[:rows, KVL:KVL + 32]
                o2 = ctm[:rows, KVL + 32:KVL + 64]
                tt("dve", junk[:rows, 0:32], x2, cst[:rows, 1, :], ALU.mult, [pb_, CST], [JNK])
                tt("dve", o1, x1, cst[:rows, 0, :], ALU.mult, [pb_, CST], [CTM])
                tt("dve", o1, o1, junk[:rows, 0:32], ALU.subtract, [CTM, JNK], [CTM])
                tt("dve", junk[:rows, 32:64], x1, cst[:rows, 1, :], ALU.mult, [pb_, CST], [JNK])
                tt("dve", o2, x2, cst[:rows, 0, :], ALU.mult, [pb_, CST], [CTM])
                tt("dve", o2, o2, junk[:rows, 32:64], ALU.add, [CTM, JNK], [CTM])
                dma("sp", cp_ap[t0:t0 + rows, :], ctm[:rows, :KVL], [CTM], [OUTB])
                dma("sp", krp_ap[t0:t0 + rows, :], ctm[:rows, KVL:KVL + RD], [CTM], [OUTB])

        def mla_q(l, N):
            j = l - 2
            prenorm(4 * l + 0, N)
            linear("w_dq", j * D, D, 0, QL, lambda kc: A.hT[:, kc, :N], [A.HB], N,
                   lambda mi, mw, p_, b_: to_mix(mi, mw, p_, b_, N), PW=128)
            rstd_of(lambda kc: A.mixT[:, kc, :N], [A.MB], 3, N, QL)
            for mc in range(3):
                stt(qaT[:, mc, :N], A.mixT[:, mc, :N], qan_sb[:, j, mc:mc + 1], rstd[:, :N], ALU.mult, ALU.mult, [A.MB, RB, CB], [QAB])

        def mla_q_head(l, h, N, bs=0):
            j = l - 2
            qn_, qr_, qnb_, qrb_ = qn2[bs], qr2[bs], QNB2[bs], QRB2[bs]
            vw, vb = panel("w_uq", j * QL, QL, h * 192, 192)
            make_wrot(vw, vb, 3, 128)
            pq, pqb = ps_next()
            for kc in range(3):
                mm(pq[:, :N], vw[:, kc, 0:128], qaT[:, kc, :N], kc == 0, kc == 2, [vb, QAB], [pqb])
            cpy("act", qn_[:, :N], pq[:, :N], [pqb], [qnb_])
            pr, prb = ps_next()
            for kc in range(3):
                mm(pr[:64, :N], vw[:, kc, 128:192], qaT[:, kc, :N], kc == 0, kc == 2, [vb, QAB], [prb])
            prot, protb = ps_next()
            for kc in range(3):
                mm(prot[:64, :N], wrot[:, kc, :], qaT[:, kc, :N], kc == 0, kc == 2, [WRB, QAB], [protb])
            rope_fm(qr_[:, :N], pr, prb, prot, protb, N, [qrb_])

        def mla_prompt(l, g):
            N = GN
            j = l - 2
            mla_q(l, N)
            mla_q_head(l, 0, N, 0)
            for h in range(H):
                qn_, qr_, qnb_, qrb_ = qn2[h % 2], qr2[h % 2], QNB2[h % 2], QRB2[h % 2]
                for kb in range(g + 1):
                    pk, pkb = ps_next()
                    for cc in range(2):
                        mm(pk[:, :], wukv[:, cc, h * 256:h * 256 + 128], cT_all[:, cc, kb * 512:(kb + 1) * 512], cc == 0, cc == 1,
                           [WUB, CKB[kb]], [pkb])
                    cpy("act", knT[:, kb * 512:(kb + 1) * 512], pk[:, :], [pkb], [KNB])
                    pv, pvb = ps_next()
                    for tt_ in range(4):
                        for cc in range(2):
                            mm(pv[:, tt_ * 128:(tt_ + 1) * 128], cT_all[:, cc, kb * 512 + tt_ * 128:kb * 512 + (tt_ + 1) * 128],
                               wukv[:, cc, h * 256 + 128:h * 256 + 256], cc == 0, cc == 1, [WUB, CKB[kb]], [pvb])
                    cpy("act", vh[:, kb * 4:(kb + 1) * 4, :].rearrange("p a b -> p (a b)"), pv[:, :], [pvb], [VHB])
                if h + 1 < H:
                    mla_q_head(l, h + 1, N, (h + 1) % 2)
                po, pob = banks[4 + (h % 2)]
                pden, pdenb = banks[6 + (h % 2)]
                ntile = 4 * g + 4
                def S_(i):
                    r = i - 4 * g
                    q0 = 128 * r if r > 0 else 0
                    ps_, psb = ps_next()
                    mm(ps_[:, q0:N], knT[:, i * 128:(i + 1) * 128], qn_[:, q0:N], True, False, [KNB, qnb_], [psb])
                    mm(ps_[:, q0:N], krT_all[:, i * 128:(i + 1) * 128], qr_[:, q0:N], False, True, [CKB[i // 4], qrb_], [psb])
                    px = pexp[:, i % 2, :]
                    pxb = PXB[i % 2]
                    act(px[:, q0:N], ps_[:, q0:N], AF.Exp, [psb], [pxb], scale=SCALE)
                    if r >= 0:
                        tt("dve", px[:, q0:q0 + 128], px[:, q0:q0 + 128], ctri[:], ALU.mult, [pxb, CB], [pxb])

                def V_(i):
                    r = i - 4 * g
                    q0 = 128 * r if r > 0 else 0
                    px = pexp[:, i % 2, :]
                    pxb = PXB[i % 2]
                    mm(po[:, q0:N], vh[:, i, :], px[:, q0:N], i == 0, i == ntile - 1, [VHB, pxb], [pob], skip_group_check=True)
                    mm(pden[:, q0:N], onesb[:], px[:, q0:N], i == 0, i == ntile - 1, [CB, pxb], [pdenb], skip_group_check=True)

                S_(0)
                for i in range(ntile):
                    if i + 1 < ntile:
                        S_(i + 1)
                    V_(i)
                recip(rstd[:, :N], pden[:, :N], [pdenb], [RB])
                tt("dve", A.big[:, h, :N], po[:, :N], rstd[:, :N], ALU.mult, [pob, RB], [A.BGB])
            linear("w_out_b", j * D, D, 0, D, lambda kc: A.big[:, kc, :N], [A.BGB], N,
                   lambda mi, mw, p_, b_: to_mix(mi, mw, p_, b_, N))
            postnorm_add(4 * l + 1, N)

        xin = sb("xin", [128, D])
        XIN = Buf("xin")

        def load_x(src_ap, rows, col0):
            dma("sp", xin[:rows, :], src_ap, [], [XIN])
            for half in range(2):
                pt_, pb_ = ps_next()
                for j in range(4):
                    kc = half * 4 + j
                    tr(pt_[:, j * 128:j * 128 + rows], xin[:rows, kc * 128:(kc + 1) * 128], ident[:rows, :rows], [XIN, CB], [pb_])
                for j in range(4):
                    kc = half * 4 + j
                    cpy("act", A.xT[:, kc, col0:col0 + rows], pt_[:, j * 128:j * 128 + rows], [pb_], [A.XB[kc]])

        def store_x(dst_ap, rows, col0):
            for half in range(2):
                pt_, pb_ = ps_next()
                for j in range(4):
                    kc = half * 4 + j
                    tr(pt_[:rows, j * 128:(j + 1) * 128], A.xT[:, kc, col0:col0 + rows], ident[:], [A.XB[kc], CB], [pb_])
                cpy("act", xin[:rows, half * 512:(half + 1) * 512], pt_[:rows, :], [pb_], [XIN])
            dma("sp", dst_ap, xin[:rows, :], [XIN], [OUTB])


        _pools = [[big[:].rearrange("p a b -> p (a b)"), 0, 22 * NMAX], [mixT[:].rearrange("p a b -> p (a b)").bitcast(BF16), 0, 16 * NMAX],
                  [xT[:].rearrange("p a b -> p (a b)").bitcast(BF16), 0, 16 * NMAX], [hT[:].rearrange("p a b -> p (a b)"), 0, 8 * NMAX]]

        def carve(shape, dt=F32):
            esz = 4 if dt in (F32, I32) else 2
            n = int(np.prod(shape[1:])) * esz // 2
            n = (n + 15) // 16 * 16
            for pl in _pools:
                if pl[1] + n <= pl[2]:
                    v = pl[0][:, pl[1]:pl[1] + n]
                    pl[1] += n
                    if esz == 4:
                        v = v.bitcast(dt)
                    v = v[:shape[0], :int(np.prod(shape[1:]))]
                    if len(shape) == 3:
                        v = v.rearrange("p (a b) -> p a b", a=shape[1])
                    elif len(shape) == 4:
                        v = v.rearrange("p (a b c) -> p a b c", a=shape[1], b=shape[2])
                    return v
            raise RuntimeError("carve: out of space " + str(shape))

        NST = 3
        stile = [(carve([128, H, 128]), Buf("stile%d" % i)) for i in range(NST)]
        sel32 = carve([NS, NS * 128])
        vs32 = carve([NS, D])
        for pl in _pools:
            pl[1] = 0
        NCT = 12
        ctile = [(carve([128, 4, KVL], BF16), carve([128, 4, RD], BF16), Buf("ct%d" % i)) for i in range(NCT)]
        qpad = carve([128, 2, NS, 128], BF16)
        rpad = carve([64, NS, 128], BF16)
        wukT = carve([128, H, KVL], BF16)
        cTs = carve([128, 2, 1024], BF16)
        rTs = sb("rTs", [64, 2, 512], BF16)
        pex2 = carve([128, 4, 512], BF16)
        snb = sb("snb", [128, H, 128], BF16)
        snb_b = carve([128, H, 128], BF16)
        snb2 = [snb, snb_b]
        SNB2 = [Buf("snb0"), Buf("snb1")]
        xTs = sb("xTs", [128, 8, NS])
        hTs = sb("hTs", [128, 8, NS], BF16)
        mixTs = sb("mixTs", [128, 8, NS])
        bigs = sb("bigs", [128, 22, NS], BF16)
        sq32 = sb("sq32", [128, H, NS])
        sf32 = sb("sf32", [128, H, NS])
        sk32 = sb("sk32", [128, H, NS])
        sgate = sb("sgate", [128, H, NS], BF16)
        sqb = sb("sqb", [128, H, NS], BF16)
        SQ32, SF32, SK32, SGT, SQBB = [Buf(n) for n in "sq32 sf32 sk32 sgate sqb".split()]
        VS32 = Buf("vs32")
        SNB = Buf("snb")
        stmp = sb("stmp", [128, 2, 128])
        STM = [Buf("stmp0"), Buf("stmp1")]

        def hgrn_sample(l):
            N = NS
            prenorm(4 * l + 0, N)
            for sec, fn, dst, db in ((0, AF.Silu, sq32, SQ32), (1, AF.Sigmoid, sf32, SF32), (3, AF.Silu, sgate, SGT)):
                linear("w_in_a", l * D, D, sec * D, D, lambda kc: A.hT[:, kc, :N], [A.HB], N,
                       (lambda fn, dst, db: (lambda mi, mw, p_, b_: act(dst[:, mi, :], p_[:, :N], fn, [b_], [db])))(fn, dst, db))
            for q4 in range(4):
                vw, vb = panel("w_in_a", l * D, D, 2 * D + q4 * 256, 256)
                pt_, pb_ = ps_next()
                for kc in range(8):
                    mm(pt_[:N, :256], A.hT[:, kc, :N], vw[:, kc, :], kc == 0, kc == 7, [vb, A.HB[kc]], [pb_])
                cpy("act", vs32[:, q4 * 256:(q4 + 1) * 256], pt_[:N, :256], [pb_], [VS32])
            lbb = lb_sb[:, l, :].unsqueeze(2).to_broadcast([128, H, NS])
            omb = oml_sb[:, l, :].unsqueeze(2).to_broadcast([128, H, NS])
            tt("dve", sf32[:], sf32[:], omb, ALU.mult, [SF32, CB], [SF32])
            tt("dve", sf32[:], sf32[:], lbb, ALU.add, [SF32, CB], [SF32])
            ts("dve", sk32[:], sf32[:], -1.0, 1.0, ALU.mult, ALU.add, [SF32], [SK32])
            ts("dve", sqb[:], sq32[:], 128 ** -0.5, None, ALU.mult, ALU.bypass, [SQ32], [SQBB])
            st_in = st.rearrange("(l s h k) v -> l s k h v", l=2, s=NS, h=H)
            st_out = sts.rearrange("(l s h k) v -> l s k h v", l=2, s=NS, h=H)
            po, pob = banks[4]
            vbanks = {}

            def vb_stage(s_):
                stl, stb = stile[s_ % NST]
                dma("sp", stl[:], st_in[l, s_], [], [stb])
                pair = []
                for half in range(2):
                    pvb_, pvbb = banks[(s_ % 2) * 2 + half]
                    mm(pvb_[:, :], sel32[:, s_ * 128:(s_ + 1) * 128], vs32[:, half * 512:(half + 1) * 512], True, True,
                       [CB, VS32], [pvbb])
                    pair.append((pvb_, pvbb))
                vbanks[s_] = pair

            def upd_stage(s_):
                stl, stb = stile[s_ % NST]
                for half in range(2):
                    pvb_, pvbb = vbanks[s_][half]
                    for hh in range(4):
                        h = half * 4 + hh
                        tb = (h % 2)
                        act(stmp[:, tb, :], pvb_[:, hh * 128:(hh + 1) * 128], AF.Copy, [pvbb, SK32], [STM[tb]],
                            scale=sk32[:, h, s_:s_ + 1])
                        stt(stl[:, h, :], stl[:, h, :], sf32[:, h, s_:s_ + 1], stmp[:, tb, :], ALU.mult, ALU.add,
                            [stb, SF32, STM[tb]], [stb])
                sn = snb2[s_ % 2]
                cpy("act", sn[:].rearrange("p a b -> p (a b)"), stl[:].rearrange("p a b -> p (a b)"), [stb], [SNB2[s_ % 2]])
                dma("sp", st_out[l, s_], stl[:], [stb], [OUTB])

            def o_stage(s_):
                sn = snb2[s_ % 2]
                for h in range(H):
                    mm(po[:, h * NS + s_:h * NS + s_ + 1], sn[:, h, :], sqb[:, h, s_:s_ + 1], True, True, [SNB2[s_ % 2], SQBB], [pob])

            vb_stage(0)
            for s_ in range(NS):
                if s_ + 1 < NS:
                    vb_stage(s_ + 1)
                upd_stage(s_)
                if s_ >= 1:
                    o_stage(s_ - 1)
            o_stage(NS - 1)
            for h in range(H):
                hgrn_post(l, h, po[:, h * NS:(h + 1) * NS], pob, N, gate=sgate[:, h, :], gate_bufs=[SGT])
            linear("w_out_a", l * D, D, 0, D, lambda kc: A.big[:, kc, :N], [A.BGB], N,
                   lambda mi, mw, p_, b_: to_mix(mi, mw, p_, b_, N))
            postnorm_add(4 * l + 1, N)

        WKT = Buf("wukT")
        qlat = sb("qlat", [128, 2, NS * H], BF16)
        QLB = Buf("qlat")
        qrall = sb("qrall", [64, NS * H], BF16)
        QRA = Buf("qrall")
        QPB = Buf("qpad")
        CTS = [Buf("cTs0"), Buf("cTs1")]
        RTS = [Buf("rTs0"), Buf("rTs1")]
        PX2 = [Buf("pex2_%d" % i) for i in range(4)]
        pTs = sb("pTs", [128, 4, 128], BF16)
        PTS = Buf("pTs")
        idx32 = sb("idx32", [128, 2 * 128], I32)
        IDXB = Buf("idx")
        ptd_sb = sb("ptd_sb", [128, 2, 4], I32)
        ptd_f = sb("ptd_f", [128, 2, 128])
        pmod = sb("pmod", [128, 1])
        newmask = sb("newmask", [128, NS])
        pnew = sb("pnew", [128, NS])
        pnewT = sb("pnewT", [NS, 128], BF16)
        ctmb = sb("ctmb", [NS, KVL], BF16)
        PNB = Buf("pnew")
        olat = sb("olat", [128, 2, NS * H], BF16)
        OLB = Buf("olat")

        def decode_setup_h():
            dma("sp", sel32[:], cd["sel"], [], [CB])

        def decode_setup():
            dma("sp", pmod[:], cd["pmod"], [], [CB])
            dma("sp", newmask[:], cd["newmask"], [], [CB])
            for h in range(H):
                pt_, pb_ = ps_next()
                pv_ = pt_[:].bitcast(BF16)
                for cc in range(2):
                    tr(pv_[:, cc * 128:(cc + 1) * 128], wukv[:, cc, h * 256:h * 256 + 128], identb[:], [WUB, CB], [pb_])
                cpy("act", wukT[:, h, :], pv_[:, 0:256], [pb_], [WKT])
            for hf_ in range(2):
                dma("sp", ptd_sb[:, hf_, :], ptd[hf_ * 128:(hf_ + 1) * 128, :], [], [IDXB])
            for hf_ in range(2):
                cpy("dve", ptd_f[:, hf_, :].rearrange("p (a b) -> p a b", b=32),
                    ptd_sb[:, hf_, :].unsqueeze(2).to_broadcast([128, 4, 32]), [IDXB], [IDXB])
                pt_, pb_ = ps_next()
                tr(pt_[:, 0:128], ptd_f[:, hf_, :], ident[:], [IDXB, CB], [pb_])
                ts("dve", idx32[:, hf_ * 128:(hf_ + 1) * 128], pt_[:, 0:128], 32.0, pmod[:, 0:1], ALU.mult, ALU.add, [pb_, CB], [IDXB])
            for i in range(4):
                S.op("dve", (lambda i=i: (lambda e: e.memset(pex2[:, i, :], 0.0)))(), [], [PX2[i]])
            S.op("dve", lambda e: e.memset(qpad[:], 0.0), [], [QPB])
            S.op("dve", lambda e: e.memset(rpad[:], 0.0), [], [QPB])

        def gather(s_, j4, slot):
            ct_, kt_, cb_ = ctile[slot]
            d_ = s_ * 16 + j4
            off = bass.IndirectOffsetOnAxis(ap=idx32[:, d_:d_ + 1], axis=0)
            S.op("pool", lambda e: e.indirect_dma_start(out=ct_[:].rearrange("p a b -> p (a b)"), out_offset=None, in_=ckv[:, :],
                                                        in_offset=off), [IDXB], [cb_], dma=True)
            off2 = bass.IndirectOffsetOnAxis(ap=idx32[:, d_:d_ + 1], axis=0)
            S.op("pool", lambda e: e.indirect_dma_start(out=kt_[:].rearrange("p a b -> p (a b)"), out_offset=None, in_=ckr[:, :],
                                                        in_offset=off2), [IDXB], [cb_], dma=True)

        def mla_sample(l):
            N = NS
            j = l - 2
            mla_q(l, N)
            for h in range(H):
                mla_q_head(l, h, N)
                pt_, pb_ = ps_next()
                for cc in range(2):
                    mm(pt_[:, cc * NS:(cc + 1) * NS], wukT[:, h, cc * 128:(cc + 1) * 128], qnT[:, :N], True, True, [WKT, QNB], [pb_])
                for cc in range(2):
                    cpy("act", qlat[:, cc, :].rearrange("p (s h) -> p s h", h=H)[:, :, h], pt_[:, cc * NS:(cc + 1) * NS], [pb_], [QLB])
                cpy("act", qrall[:, :].rearrange("p (s h) -> p s h", h=H)[:, :, h], qrT[:, :N], [QRB], [QRA])
            for s_ in range(NS):
                for cc in range(2):
                    cpy("dve", qpad[:, cc, s_, s_ * 8:(s_ + 1) * 8], qlat[:, cc, s_ * 8:(s_ + 1) * 8], [QLB], [QPB])
                cpy("dve", rpad[:, s_, s_ * 8:(s_ + 1) * 8], qrall[:, s_ * 8:(s_ + 1) * 8], [QRA], [QPB])
            po, pob = banks[4]
            pden, pdenb = banks[5]
            blocks = [(j4, sg) for j4 in range(16) for sg in range(4)]
            reqs = [(sg * 4 + k, j4) for (j4, sg) in blocks for k in range(4)]
            issued = {"n": 0}

            def ensure(n):
                while issued["n"] < min(n, len(reqs)):
                    s2, j42 = reqs[issued["n"]]
                    gather(s2, j42, issued["n"] % NCT)
                    issued["n"] += 1

            first = True
            for bi, (j4, sg) in enumerate(blocks):
                ensure(bi * 4 + NCT)
                psc, pscb = banks[6 + (bi % 2)]

                def T_(k):
                    ct_, kt_, cb_ = ctile[(bi * 4 + k) % NCT]
                    slot = (bi * 4 + k) % 2
                    ptA, ptAb = ps_next()
                    pvA = ptA[:].bitcast(BF16)
                    for cc in range(2):
                        for t4 in range(4):
                            tr(pvA[:, (cc * 4 + t4) * 128:(cc * 4 + t4 + 1) * 128], ct_[:, t4, cc * 128:(cc + 1) * 128], identb[:],
                               [cb_, CB], [ptAb])
                    cpy("act", cTs[:, slot, :], pvA[:, :], [ptAb], [CTS[slot]])
                    ptB, ptBb = ps_next()
                    pvB = ptB[:].bitcast(BF16)
                    for t4 in range(4):
                        tr(pvB[:64, t4 * 128:(t4 + 1) * 128], kt_[:, t4, :], identb[:], [cb_, CB], [ptBb])
                    cpy("dve", rTs[:, slot, :], pvB[:64, 0:512], [ptBb], [RTS[slot]])

                def S_(k):
                    s_ = sg * 4 + k
                    slot = (bi * 4 + k) % 2
                    for cc in range(2):
                        mm(psc[:, :], qpad[:, cc, s_, :], cTs[:, slot, cc * 512:(cc + 1) * 512], k == 0 and cc == 0, False,
                           [QPB, CTS[slot]], [pscb])
                    mm(psc[:, :], rpad[:, s_, :], rTs[:, slot, :], False, k == 3, [QPB, RTS[slot]], [pscb])

                T_(0)
                for k in range(4):
                    if k + 1 < 4:
                        T_(k + 1)
                    S_(k)
                band = slice(32 * sg, 32 * sg + 32)
                act(pex2[band, sg, :], psc[band, :], AF.Exp, [pscb], [PX2[sg]], scale=SCALE)
                ptP, ptPb = ps_next()
                pvP = ptP[:].bitcast(BF16)
                for t4 in range(4):
                    tr(pvP[:, t4 * 128:(t4 + 1) * 128], pex2[:, sg, t4 * 128:(t4 + 1) * 128], identb[:], [PX2[sg], CB], [ptPb])
                cpy("act", pTs[:].rearrange("p a b -> p (a b)"), pvP[:, 0:512], [ptPb], [PTS])
                for t4 in range(4):
                    mm(pden[:, 0:128], onesb[:], pTs[:, t4, :], first and t4 == 0, False, [CB, PTS], [pdenb], skip_group_check=True)
                for k in range(4):
                    s_ = sg * 4 + k
                    ct_, kt_, cb_ = ctile[(bi * 4 + k) % NCT]
                    for t4 in range(4):
                        for cc in range(2):
                            mm(po[:, cc * 128 + s_ * 8:cc * 128 + s_ * 8 + 8], ct_[:, t4, cc * 128:(cc + 1) * 128],
                               pTs[:, t4, s_ * 8:(s_ + 1) * 8], bi == 0 and k == 0 and t4 == 0 and cc == 0, False, [cb_, PTS], [pob], skip_group_check=True)
                first = False
            col0 = T
            psn, psnb = ps_next()
            for cc in range(2):
                mm(psn[:, :NS], qlat[:, cc, :], cT_all[:, cc, col0:col0 + NS], cc == 0, False, [QLB, CKB[NG]], [psnb])
            mm(psn[:, :NS], qrall[:, :], krT_all[:, col0:col0 + NS], False, True, [QRA, CKB[NG]], [psnb])
            act(pnew[:, :], psn[:, :NS], AF.Exp, [psnb], [PNB], scale=SCALE)
            tt("dve", pnew[:, :], pnew[:, :], newmask[:, :], ALU.mult, [PNB, CB], [PNB])
            ptn, ptnb = ps_next()
            tr(ptn[:NS, 0:128], pnew[:, :], ident[:], [PNB, CB], [ptnb])
            cpy("act", pnewT[:, :], ptn[:NS, 0:128], [ptnb], [PNB])
            mm(pden[:, 0:128], onesb[:NS, :], pnewT[:, :], False, True, [CB, PNB], [pdenb], skip_group_check=True)
            ptc, ptcb = ps_next()
            pvc = ptc[:].bitcast(BF16)
            for cc in range(2):
                tr(pvc[:NS, cc * 128:(cc + 1) * 128], cT_all[:, cc, col0:col0 + NS], identb[:], [CKB[NG], CB], [ptcb])
            cpy("act", ctmb[:, :], pvc[:NS, 0:KVL], [ptcb], [PNB])
            for cc in range(2):
                mm(po[:, cc * 128:(cc + 1) * 128], ctmb[:, cc * 128:(cc + 1) * 128], pnewT[:, :], False, True, [PNB], [pob],
                   skip_group_check=True)
            recip(rstd[:, :128], pden[:, 0:128], [pdenb], [RB])
            for cc in range(2):
                tt("dve", olat[:, cc, :], po[:, cc * 128:(cc + 1) * 128], rstd[:, :128], ALU.mult, [pob, RB], [OLB])
            for h in range(H):
                pt_, pb_ = ps_next()
                for cc in range(2):
                    mm(pt_[:, :NS], wukv[:, cc, h * 256 + 128:h * 256 + 256], olat[:, cc, :].rearrange("p (s h) -> p s h", h=H)[:, :, h],
                       cc == 0, cc == 1, [WUB, OLB], [pb_])
                cpy("act", A.big[:, h, :NS], pt_[:, :NS], [pb_], [A.BGB])
            linear("w_out_b", j * D, D, 0, D, lambda kc: A.big[:, kc, :N], [A.BGB], N,
                   lambda mi, mw, p_, b_: to_mix(mi, mw, p_, b_, N))
            postnorm_add(4 * l + 1, N)

        def sample_group():
            N = NS
            S.barrier()
            A.xT, A.hT, A.mixT, A.big = xTs, hTs, mixTs, bigs
            A.XB, A.HB, A.MB, A.BGB = [Buf("xTs%d" % i) for i in range(8)], [Buf("hTs%d" % i) for i in range(8)], [Buf("mixTs%d" % i) for i in range(8)], Buf("bigs")
            load_x(xs[:, :], NS, 0)
            if "sA" not in DBG:
                decode_setup_h()
            for l in range(n_layers):
                if "sA" in DBG or "sB" in DBG:
                    break
                if l < 2:
                    hgrn_sample(l)
                else:
                    if l == 2:
                        S.barrier()
                        decode_setup()
                        dma("sp", cosF[:, :N], cd["cosF"][:, T:T + N], [], [CSB])
                        dma("sp", sinF[:, :N], cd["sinF"][:, T:T + N], [], [CSB])
                        mla_shared(N, T, T, cs[:, :], krs[:, :], CKB[NG])
                    mla_sample(l)
                if "noffn" not in DBG:
                    ffn(l, N)
            store_x(ys[:, :], NS, 0)

        dma("pool", wukv[:], w_ukv.rearrange("(k p) c -> p k c", p=128), [], [WUB])
        for l in range(2):
            for h in range(H):
                S.op("dve", (lambda l=l, h=h: (lambda e: e.memset(Sst[l][:, h, :], 0.0)))(), [], [SSB[l][h]])

        for g in groups:
            if g == "s":
                sample_group()
                continue
            N = GN
            for tt_ in range(4):
                load_x(xp[g * GN + tt_ * 128: g * GN + (tt_ + 1) * 128, :], 128, tt_ * 128)
            for l in range(n_layers):
                if l < 2:
                    if "nohgrn" not in DBG:
                        hgrn_prompt(l, g)
                else:
                    if l == 2:
                        dma("sp", cosF[:, :N], cd["cosF"][:, g * GN:g * GN + N], [], [CSB])
                        dma("sp", sinF[:, :N], cd["sinF"][:, g * GN:g * GN + N], [], [CSB])
                        mla_shared(N, g * GN, g * GN, cp[g * GN:(g + 1) * GN, :], krp[g * GN:(g + 1) * GN, :], CKB[g])
                    mla_prompt(l, g)
                if "noffn" not in DBG:
                    ffn(l, N)
            for tt_ in range(4):
                store_x(yp[g * GN + tt_ * 128: g * GN + (tt_ + 1) * 128, :], 128, tt_ * 128)

        if "dump" in DBG:
            for nm_, t_, bufs_ in (("hT", hT, [A.HB]), ("big", big, [A.BGB]), ("mixT", mixT, [A.MB]), ("rstd", rstd, [RB]), ("xT", xT, [A.XB]),
                                     ("hqd", hqd, [HQD]), ("hkd", hkd, [HKD]), ("hkd2", hkd2, [HKD2]), ("hv", hv, [HV]), ("hA", hA, [HA]),
                                     ("lb_sb", lb_sb, [CB]), ("oml_sb", oml_sb, [CB]), ("lbl_sb", lbl_sb, [CB]), ("hq", hq, [HQ]), ("hb", hb, [HBB]), ("hlf", hlf, [HLF]), ("hf", hf, [HF]), ("hk", hk, [HK]), ("hbl", hbl, [HBL]), ("hgate", hgate, [HGT]), ("hos", hos, [HOS]), ("hkd2T", hkd2T, [HKT])):
                shp = list(t_.shape)
                flat = [shp[0], int(np.prod(shp[1:]))]
                dd = nc.dram_tensor("dbg_" + nm_, flat, t_.dtype, kind="ExternalOutput").ap()
                src = t_[:] if len(shp) == 2 else t_[:].rearrange("p a b -> p (a b)")
                dma("sp", dd, src, bufs_, [OUTB])
        S.op("sp", None, [OUTB], [])
        block = stack.enter_context(nc.Block())
        S.emit(block)
    return nc, req_log


def build2(n_pool, **kw):
    _, plan = build(n_pool, plan=None, **kw)
    nc, _ = build(n_pool, plan=plan, **kw)
    return nc


def _cols(v, kc):
    v = np.asarray(v, np.float32)
    R_ = v.shape[0]
    return np.ascontiguousarray(v.reshape(R_, kc, 128).transpose(2, 0, 1).reshape(128, R_ * kc))


def shared_inputs(inp):
    f = lambda a: np.ascontiguousarray(np.asarray(a, np.float32))
    m = {}
    n_pool = inp["cache_kv_latent"].shape[0]
    m["ckv"] = f(inp["cache_kv_latent"]).reshape(n_pool * 32, 4 * KVL)
    m["ckr"] = f(inp["cache_k_rope"]).reshape(n_pool * 32, 4 * RD)
    m["gains"] = _cols(f(inp["norm_gains"]).reshape(16, D), 8)
    m["w_ffn_in"] = f(inp["w_ffn_in"]).reshape(4 * D, 2 * DFF)
    m["w_ffn_out"] = f(inp["w_ffn_out"]).reshape(4 * DFF, D)
    m["w_in_a"] = f(inp["w_in_a"]).reshape(2 * D, 4 * D)
    m["lbl"] = _cols(f(inp["lb_logits"]), 8)
    m["gna"] = _cols(f(inp["g_norm_a"]), 8)
    m["w_out_a"] = f(inp["w_out_a"]).reshape(2 * D, D)
    m["kvn"] = _cols(f(inp["kv_norm"]).reshape(1, D), 8)
    m["w_dkv"] = f(inp["w_dkv"])
    m["kvan"] = _cols(f(inp["kv_a_norm"]).reshape(1, KVL), 2)
    m["kvan_b"] = np.ascontiguousarray(np.broadcast_to(f(inp["kv_a_norm"]).reshape(1, KVL), (128, KVL)))
    m["w_ukv"] = f(inp["w_ukv"])
    m["w_dq"] = f(inp["w_dq"]).reshape(2 * D, QL)
    m["qan"] = _cols(f(inp["q_a_norm"]), 3)
    m["w_uq"] = f(inp["w_uq"]).reshape(2 * QL, H * 192)
    m["w_out_b"] = f(inp["w_out_b"]).reshape(2 * D, D)
    for k, v in _consts().items():
        m["c_" + k] = v
    return m


def core_inputs(inp, shared, c):
    m = dict(shared)
    m["xp"] = np.ascontiguousarray(np.asarray(inp["x_prompt"][c], np.float32))
    m["xs"] = np.ascontiguousarray(np.asarray(inp["x_sample"][c * NS:(c + 1) * NS, 0], np.float32))
    m["st"] = np.ascontiguousarray(np.asarray(inp["state_hgrn"][:, c * NS:(c + 1) * NS], np.float32)).reshape(2 * NS * H * 128, 128)
    m["ptd"] = np.ascontiguousarray(np.asarray(inp["page_table"][c * NS:(c + 1) * NS], np.int32)).reshape(NS * 16, 4)
    return m


def kernel(**inp):
    n_cores = 8
    n_pool = inp["cache_kv_latent"].shape[0]
    nc = build2(n_pool)
    shared = shared_inputs(inp)
    in_maps = [core_inputs(inp, shared, c) for c in range(n_cores)]
    res = run_bass_kernel_spmd(nc, in_maps, core_ids=list(range(n_cores))).results
    y_p = np.stack([r["yp"] for r in res]).reshape(8, T, D)
    y_s = np.concatenate([r["ys"] for r in res]).reshape(128, 1, D)
    st_p = np.stack([r["stp"].reshape(2, H, 128, 128) for r in res], axis=1)
    c_p = np.stack([r["cp"] for r in res]).reshape(8, T, KVL)
    kr_p = np.stack([r["krp"] for r in res]).reshape(8, T, RD)
    st_s = np.concatenate([r["sts"].reshape(2, NS, H, 128, 128) for r in res], axis=1)
    c_s = np.concatenate([r["cs"] for r in res]).reshape(128, 1, KVL)
    kr_s = np.concatenate([r["krs"] for r in res]).reshape(128, 1, RD)
    return tuple(np.ascontiguousarray(a.astype(np.float32)) for a in (y_p, y_s, st_p, c_p, kr_p, st_s, c_s, kr_s))
```

```python
import contextlib
import os
import numpy as np
import concourse.bass as bass
import concourse.mybir as mybir
from concourse.bass_utils import run_bass_kernel_spmd

F32 = mybir.dt.float32
BF16 = mybir.dt.bfloat16
I32 = mybir.dt.int32
AF = mybir.ActivationFunctionType
ALU = mybir.AluOpType

D = 1024
T = 2048
NS = 16
H = 8
DFF = 2816
QL = 384
KVL = 256
RD = 64
PAST = 8192
NPG = 64
EPS = 1e-6
SCALE = (128 + 64) ** -0.5
DBG = set(os.environ.get("KDBG", "").split(","))
GN = 512
NG = T // GN


class Buf:
    __slots__ = ("name", "w", "r")

    def __init__(self, name=""):
        self.name = name
        self.w = {}
        self.r = {}


class Sched:
    ENG = ("pe", "act", "dve", "pool", "sp")

    def __init__(self, nc, stack, n_dsem=40):
        self.nc = nc
        self.ops = {e: [] for e in self.ENG}
        self.esem = {e: stack.enter_context(nc.semaphore("E" + e)) for e in self.ENG}
        self.dsem = [stack.enter_context(nc.semaphore("D%d" % i)) for i in range(n_dsem)]
        self.dval = [0] * n_dsem
        self.dnext = 0
        self.waited = {e: {} for e in self.ENG}

    def _need(self, eng, dep, waits, war=False):
        if dep is None:
            return
        if dep[0] == "e":
            _, pe, idx = dep
            if pe == eng and (eng == "pe" or war):
                return
            key = ("e", pe)
            if self.waited[eng].get(key, -1) >= idx:
                return
            self.waited[eng][key] = idx
            self.ops[pe][idx]["sig"] = True
            waits.append(dep)
        else:
            _, j, val = dep
            key = ("d", j)
            if self.waited[eng].get(key, -1) >= val:
                return
            self.waited[eng][key] = val
            waits.append(dep)

    @staticmethod
    def _flat(bufs):
        out = []
        for b in bufs:
            if isinstance(b, (list, tuple)):
                out.extend(Sched._flat(b))
            else:
                out.append(b)
        return out

    def op(self, eng, fn, reads=(), writes=(), dma=False):
        reads = self._flat(reads)
        writes = self._flat(writes)
        waits = []
        for b in reads:
            for d in b.w.values():
                self._need(eng, d, waits)
        for b in writes:
            for d in b.w.values():
                if dma and d[0] == "d" and not b.r:
                    continue
                self._need(eng, d, waits)
            for d in b.r.values():
                self._need(eng, d, waits, war=True)
        idx = len(self.ops[eng])
        rec = dict(fn=fn, waits=waits, sig=False, dma=None)
        if dma:
            j = self.dnext
            self.dnext = (self.dnext + 1) % len(self.dsem)
            if self.dval[j] > 0:
                self._need(eng, ("d", j, self.dval[j]), waits)
            self.dval[j] += 16
            rec["dma"] = j
            ev = ("d", j, self.dval[j])
            key = ("d", j)
        else:
            ev = ("e", eng, idx)
            key = ("e", eng)
        self.ops[eng].append(rec)
        for b in reads:
            b.r[key] = ev
        for b in writes:
            if dma and not b.r:
                b.w = {k: v for k, v in b.w.items() if k[0] == "d"}
                b.w[key] = ev
            else:
                b.w = {key: ev}
            b.r = {}
        return ev

    def barrier(self):
        last = {}
        for e in self.ENG:
            idx = len(self.ops[e]) - 1
            while idx >= 0 and (self.ops[e][idx]["fn"] is None or self.ops[e][idx]["dma"] is not None):
                idx -= 1
            last[e] = idx
        dvals = list(self.dval)
        for e in self.ENG:
            waits = []
            for pe, idx in last.items():
                if pe != e and idx >= 0:
                    self._need(e, ("e", pe, idx), waits)
            for j, v in enumerate(dvals):
                if v > 0:
                    self._need(e, ("d", j, v), waits)
            self.ops[e].append(dict(fn=None, waits=waits, sig=False, dma=None))

    def emit(self, block):
        for e in self.ENG:
            c = 0
            for rec in self.ops[e]:
                if rec["sig"]:
                    c += 1
                rec["sval"] = c
        S = self

        def run(name, eng):
            for rec in S.ops[name]:
                for d in rec["waits"]:
                    if d[0] == "e":
                        eng.wait_ge(S.esem[d[1]], S.ops[d[1]][d[2]]["sval"])
                    else:
                        eng.wait_ge(S.dsem[d[1]], d[2])
                if rec["fn"] is None:
                    continue
                ins = rec["fn"](eng)
                if rec["dma"] is not None:
                    ins.then_inc(S.dsem[rec["dma"]], 16)
                elif rec["sig"]:
                    ins.then_inc(S.esem[name], 1)

        @block.tensor
        def _(e):
            run("pe", e)

        @block.scalar
        def _(e):
            run("act", e)

        @block.vector
        def _(e):
            run("dve", e)

        @block.gpsimd
        def _(e):
            run("pool", e)

        @block.sync
        def _(e):
            run("sp", e)


def _consts():
    c = {}
    c["ident"] = np.eye(128, dtype=np.float32)
    s = np.arange(128)[:, None]
    t = np.arange(128)[None, :]
    c["hmask"] = ((s // 32 == t // 32) & (s <= t)).astype(np.float32)
    c["ctri"] = (s <= t).astype(np.float32)
    r = np.ones((128, GN), np.float32)
    r[:, ::32] = 0.0
    c["reset"] = r
    half = RD // 2
    inv = (np.float32(10000.0) ** (-(np.arange(half, dtype=np.float32) / np.float32(half)))).astype(np.float32)
    pos = np.concatenate([np.arange(T), np.full(NS, PAST)]).astype(np.float32)
    ang = (pos[:, None] * inv[None, :]).astype(np.float32).astype(np.float64)
    cos = np.cos(ang).astype(np.float32)
    sin = np.sin(ang).astype(np.float32)
    c["cosT"] = cos
    c["sinT"] = sin
    c["cosF"] = np.ascontiguousarray(np.concatenate([cos, cos], 1).T)
    c["sinF"] = np.ascontiguousarray(np.concatenate([sin, sin], 1).T)
    sel = np.zeros((NS, NS, 128), np.float32)
    for i in range(NS):
        sel[i, i, :] = 1.0
    c["sel"] = sel.reshape(NS, NS * 128)
    mp = np.zeros((NS, NS, H), np.float32)
    for i in range(NS):
        mp[i, i, :] = 1.0
    c["maskpad"] = np.ascontiguousarray(np.broadcast_to(mp.reshape(1, NS * NS * H), (128, NS * NS * H)))
    nm = np.zeros((NS, H, NS), np.float32)
    for i in range(NS):
        nm[i, :, i] = 1.0
    c["newmask"] = nm.reshape(NS * H, NS)
    c["cmask"] = (np.arange(128)[:, None] // 32 == np.arange(4)[None, :]).astype(np.float32)
    c["pmod"] = (np.arange(128) % 32).astype(np.float32).reshape(128, 1)
    return c


CONST_SHAPES = {k: v.shape for k, v in _consts().items()}


def build(n_pool, groups=(0, 1, 2, 3, "s"), n_layers=4, plan=None):
    nc = bass.Bass("TRN2", target_bir_lowering=False)
    req_log = []
    dram = {}

    def din(name, shape, dt=F32):
        dram[name] = nc.dram_tensor(name, list(shape), dt, kind="ExternalInput").ap()
        return dram[name]

    def dout(name, shape, dt=F32):
        dram[name] = nc.dram_tensor(name, list(shape), dt, kind="ExternalOutput").ap()
        return dram[name]

    xp = din("xp", [T, D])
    xs = din("xs", [NS, D])
    st = din("st", [2 * NS * H * 128, 128])
    ckv = din("ckv", [n_pool * 32, 4 * KVL])
    ckr = din("ckr", [n_pool * 32, 4 * RD])
    ptd = din("ptd", [NS * 16, 4], I32)
    gains = din("gains", [128, 16 * 8])
    w_ffn_in = din("w_ffn_in", [4 * D, 2 * DFF])
    w_ffn_out = din("w_ffn_out", [4 * DFF, D])
    w_in_a = din("w_in_a", [2 * D, 4 * D])
    lbl = din("lbl", [128, 2 * 8])
    gna = din("gna", [128, 2 * 8])
    w_out_a = din("w_out_a", [2 * D, D])
    kvn = din("kvn", [128, 8])
    w_dkv = din("w_dkv", [D, KVL + RD])
    kvan = din("kvan", [128, 2])
    kvan_b = din("kvan_b", [128, KVL])
    w_ukv = din("w_ukv", [KVL, H * 256])
    w_dq = din("w_dq", [2 * D, QL])
    qan = din("qan", [128, 2 * 3])
    w_uq = din("w_uq", [2 * QL, H * 192])
    w_out_b = din("w_out_b", [2 * D, D])
    cd = {k: din("c_" + k, shp) for k, shp in CONST_SHAPES.items()}

    yp = dout("yp", [T, D])
    ys = dout("ys", [NS, D])
    stp = dout("stp", [2 * H * 128, 128])
    cp = dout("cp", [T, KVL])
    krp = dout("krp", [T, RD])
    sts = dout("sts", [2 * NS * H * 128, 128])
    cs = dout("cs", [NS, KVL])
    krs = dout("krs", [NS, RD])

    with contextlib.ExitStack() as stack:
        S = Sched(nc, stack)

        def sb(name, shape, dt=F32):
            return stack.enter_context(nc.sbuf_tensor(name, list(shape), dt))

        def mm(out, lhsT, rhs, start, stop, reads, writes, **kw):
            S.op("pe", lambda e: e.matmul(out, lhsT, rhs, start=start, stop=stop, **kw), reads, writes)

        def tr(out, in_, ident, reads, writes):
            S.op("pe", lambda e: e.transpose(out, in_, ident), reads, writes)

        def act(out, in_, func, reads, writes, scale=1.0, bias=0.0, accum_out=None):
            kw = {}
            if accum_out is not None:
                kw["accum_out"] = accum_out
            S.op("act", lambda e: e.activation(out, in_, func, bias=bias, scale=scale, **kw), reads, writes)

        def tt(eng, out, in0, in1, op, reads, writes):
            S.op(eng, lambda e: e.tensor_tensor(out, in0, in1, op), reads, writes)

        def ts(eng, out, in0, s1, s2, op0, op1, reads, writes):
            S.op(eng, lambda e: e.tensor_scalar(out, in0, s1, s2, op0, op1), reads, writes)

        def stt(out, in0, scalar, in1, op0, op1, reads, writes):
            S.op("dve", lambda e: e.scalar_tensor_tensor(out, in0, scalar, in1, op0, op1), reads, writes)

        def cpy(eng, out, in_, reads, writes):
            if eng == "act":
                S.op("act", lambda e: e.copy(out, in_), reads, writes)
            else:
                S.op(eng, lambda e: e.tensor_copy(out, in_), reads, writes)

        def recip(out, in_, reads, writes):
            S.op("dve", lambda e: e.reciprocal(out, in_), reads, writes)

        def dma(eng, out, in_, reads, writes):
            S.op(eng, lambda e: e.dma_start(out=out, in_=in_), reads, writes, dma=True)

        banks = []
        for i in range(8):
            t_ = stack.enter_context(nc.psum_tensor("ps%d" % i, [128, 512], F32))
            banks.append((t_, Buf("ps%d" % i)))
        rot = {"i": 0, "n": 4}

        def ps_next():
            i = rot["i"]
            rot["i"] = (i + 1) % rot["n"]
            return banks[i]

        CB = Buf("consts")
        ident = sb("ident", [128, 128])
        identb = sb("identb", [128, 128], BF16)
        onesb = sb("onesb", [128, 128], BF16)
        hmask = sb("hmask", [128, 128])
        ctri = sb("ctri", [128, 128], BF16)
        ctri_f = sb("ctri_f", [128, 128])
        reset = sb("reset", [128, GN])
        cmask = sb("cmask", [128, 4])
        cmask4 = sb("cmask4", [128, 4, 128], BF16)
        gains_sb = sb("gains_sb", [128, 16, 8])
        lbl_sb = sb("lbl_sb", [128, 2, 8])
        gna_sb = sb("gna_sb", [128, 2, 8])
        kvn_sb = sb("kvn_sb", [128, 8])
        kvan_sb = sb("kvan_sb", [128, 2])
        kvanb_sb = sb("kvanb_sb", [128, KVL])
        qan_sb = sb("qan_sb", [128, 2, 3])
        lb_sb = sb("lb_sb", [128, 2, 8])
        oml_sb = sb("oml_sb", [128, 2, 8])
        lbtmp = sb("lbtmp", [128, 4, 8])
        for dst, src in ((ident, cd["ident"]), (hmask, cd["hmask"]), (ctri_f, cd["ctri"]), (reset, cd["reset"]), (cmask, cd["cmask"]),
                         (gains_sb, gains.rearrange("p (a k) -> p a k", a=16)),
                         (lbl_sb, lbl.rearrange("p (a k) -> p a k", a=2)),
                         (gna_sb, gna.rearrange("p (a k) -> p a k", a=2)),
                         (kvn_sb, kvn), (kvan_sb, kvan), (kvanb_sb, kvan_b),
                         (qan_sb, qan.rearrange("p (a k) -> p a k", a=2))):
            dma("sp", dst[:], src, [], [CB])
        cpy("dve", identb[:], ident[:], [CB], [CB])
        cpy("dve", ctri[:], ctri_f[:], [CB], [CB])
        S.op("dve", lambda e: e.memset(onesb[:], 1.0), [], [CB])
        cpy("dve", cmask4[:], cmask[:].unsqueeze(2).to_broadcast([128, 4, 128]), [CB], [CB])
        act(lbtmp[:, 0:2, :], lbl_sb[:], AF.Exp, [CB], [CB])
        tt("dve", lbtmp[:, 2, :], lbtmp[:, 0, :], lbtmp[:, 1, :], ALU.add, [CB], [CB])
        recip(lbtmp[:, 3, :], lbtmp[:, 2, :], [CB], [CB])
        tt("dve", lbtmp[:, 0, :], lbtmp[:, 0, :], lbtmp[:, 3, :], ALU.mult, [CB], [CB])
        tt("dve", lbtmp[:, 1, :], lbtmp[:, 1, :], lbtmp[:, 3, :], ALU.mult, [CB], [CB])
        tt("dve", lb_sb[:, 0, :], lbtmp[:, 0, :], lbtmp[:, 0, :], ALU.subtract, [CB], [CB])
        tt("dve", lbtmp[:, 2, :], lbtmp[:, 0, :], lbtmp[:, 1, :], ALU.add, [CB], [CB])
        tt("dve", lb_sb[:, 1, :], lbtmp[:, 2, :], lbtmp[:, 0, :], ALU.subtract, [CB], [CB])
        ts("dve", oml_sb[:], lb_sb[:], -1.0, 1.0, ALU.mult, ALU.add, [CB], [CB])

        WSLOT = 22 * 128
        NWS = 4
        wring = [(sb("wr%d" % i, [128, WSLOT], BF16), Buf("wr%d" % i)) for i in range(NWS)]
        wstate = {"issued": 0, "cur": 0}

        def _issue(i):
            wd, r0, K, c0, pw = plan[i]
            KC = K // 128
            t_, b_ = wring[i % NWS]
            view = t_[:, :KC * pw].rearrange("p (k c) -> p k c", k=KC)
            src = dram[wd][r0:r0 + K, c0:c0 + pw].rearrange("(k p) c -> p k c", p=128)
            dma("pool", view, src, [], [b_])

        def panel(wd, r0, K, c0, pw):
            i = wstate["cur"]
            wstate["cur"] += 1
            req_log.append((wd, r0, K, c0, pw))
            KC = K // 128
            if plan is None:
                t_, b_ = wring[i % NWS]
                view = t_[:, :KC * pw].rearrange("p (k c) -> p k c", k=KC)
                src = dram[wd][r0:r0 + K, c0:c0 + pw].rearrange("(k p) c -> p k c", p=128)
                dma("pool", view, src, [], [b_])
                return view, b_
            assert plan[i] == (wd, r0, K, c0, pw), (i, plan[i], (wd, r0, K, c0, pw))
            while wstate["issued"] < min(i + NWS - 1, len(plan)):
                _issue(wstate["issued"])
                wstate["issued"] += 1
            t_, b_ = wring[i % NWS]
            return t_[:, :KC * pw].rearrange("p (k c) -> p k c", k=KC), b_

        def linear(wd, r0, K, c0, M, rhs_fn, rhs_bufs, N, consume, PW=256):
            KC = K // 128
            mi = 0
            for p0 in range(0, M, PW):
                pw = min(PW, M - p0)
                view, wb = panel(wd, r0, K, c0 + p0, pw)
                for m0 in range(0, pw, 128):
                    mw = min(128, pw - m0)
                    pt_, pb_ = ps_next()
                    for kc in range(KC):
                        rbk = [b[kc] if isinstance(b, list) and len(b) == KC else b for b in rhs_bufs]
                        mm(pt_[:mw, :N], view[:, kc, m0:m0 + mw], rhs_fn(kc), kc == 0, kc == KC - 1,
                           [wb] + rbk, [pb_])
                    consume(mi, mw, pt_, pb_)
                    mi += 1

        NMAX = GN
        class _NS:
            pass
        A = _NS()
        xT = sb("xT", [128, 8, NMAX])
        A.XB = [Buf("xT%d" % i) for i in range(8)]
        hT = sb("hT", [128, 8, NMAX], BF16)
        A.HB = [Buf("hT%d" % i) for i in range(8)]
        mixT = sb("mixT", [128, 8, NMAX])
        A.MB = [Buf("mixT%d" % i) for i in range(8)]
        sqT = sb("sqT", [128, 2, NMAX], BF16)
        SQB = [Buf("sqT0"), Buf("sqT1")]
        rstd = sb("rstd", [128, NMAX])
        RB = Buf("rstd")
        tmpN = sb("tmpN", [128, NMAX])
        TB = Buf("tmpN")
        big = sb("big", [128, 22, NMAX], BF16)
        A.BGB = Buf("big")
        A.xT, A.hT, A.mixT, A.big = xT, hT, mixT, big
        Sst = [sb("Sst%d" % l, [128, H, 128]) for l in range(2)]
        SSB = [[Buf("S%d_%d" % (l, h)) for h in range(H)] for l in range(2)]
        Stmp = sb("Stmp", [128, 128])
        STB = Buf("Stmp")
        Stmp2 = sb("Stmp2", [128, 128])
        STB2 = Buf("Stmp2")
        Sbf = sb("Sbf", [128, 4, 128], BF16)
        SBB = [Buf("Sbf%d" % i) for i in range(4)]
        cT_all = sb("cT_all", [128, 2, T + NS], BF16)
        krT_all = sb("krT_all", [64, T + NS], BF16)
        CKB = [Buf("ck%d" % g) for g in range(NG + 1)]
        wukv = sb("wukv", [128, 2, H * 256], BF16)
        WUB = Buf("wukv")
        cosF = sb("cosF", [64, NMAX])
        sinF = sb("sinF", [64, NMAX])
        CSB = Buf("cossin")

        def rstd_of(src_fn, src_bufs, KC, N, Dn):
            pt_, pb_ = ps_next()
            for kc in range(KC):
                sbk = [b[kc] if isinstance(b, list) and len(b) == 8 else b for b in src_bufs]
                act(sqT[:, kc % 2, :N], src_fn(kc), AF.Square, sbk, [SQB[kc % 2]])
                mm(pt_[:, :N], onesb[:], sqT[:, kc % 2, :N], kc == 0, kc == KC - 1, [SQB[kc % 2], CB], [pb_])
            act(tmpN[:, :N], pt_[:, :N], AF.Ln, [pb_], [TB], scale=1.0 / Dn, bias=EPS)
            act(rstd[:, :N], tmpN[:, :N], AF.Exp, [TB], [RB], scale=-0.5)

        def prenorm(gi, N, gain_tile=None):
            rstd_of(lambda kc: A.xT[:, kc, :N], [A.XB], 8, N, D)
            for kc in range(8):
                g_ = gain_tile[:, kc:kc + 1] if gain_tile is not None else gains_sb[:, gi, kc:kc + 1]
                stt(A.hT[:, kc, :N], A.xT[:, kc, :N], g_, rstd[:, :N], ALU.mult, ALU.mult, [A.XB[kc], RB, CB], [A.HB[kc]])

        def postnorm_add(gi, N):
            rstd_of(lambda kc: A.mixT[:, kc, :N], [A.MB], 8, N, D)
            for kc in range(8):
                stt(A.mixT[:, kc, :N], A.mixT[:, kc, :N], gains_sb[:, gi, kc:kc + 1], rstd[:, :N], ALU.mult, ALU.mult,
                    [A.MB[kc], RB, CB], [A.MB[kc]])
                tt("dve", A.xT[:, kc, :N], A.xT[:, kc, :N], A.mixT[:, kc, :N], ALU.add, [A.XB[kc], A.MB[kc]], [A.XB[kc]])

        def to_mix(mi, mw, pt_, pb_, N):
            cpy("act", A.mixT[:, mi, :N], pt_[:, :N], [pb_], [A.MB[mi]])

        def ffn(l, N):
            prenorm(4 * l + 2, N)
            gtmp = sb_ffn_g
            for j in range(DFF // 256):
                vg, bg = panel("w_ffn_in", l * D, D, j * 256, 256)
                vu, bu = panel("w_ffn_in", l * D, D, DFF + j * 256, 256)
                for m in range(2):
                    pg_, pgb = ps_next()
                    for kc in range(8):
                        mm(pg_[:, :N], vg[:, kc, m * 128:(m + 1) * 128], A.hT[:, kc, :N], kc == 0, kc == 7, [bg, A.HB[kc]], [pgb])
                    pu_, pub = ps_next()
                    for kc in range(8):
                        mm(pu_[:, :N], vu[:, kc, m * 128:(m + 1) * 128], A.hT[:, kc, :N], kc == 0, kc == 7, [bu, A.HB[kc]], [pub])
                    act(gtmp[:, :N], pg_[:, :N], AF.Silu, [pgb], [GTB])
                    tt("dve", A.big[:, 2 * j + m, :N], gtmp[:, :N], pu_[:, :N], ALU.mult, [GTB, pub], [A.BGB])
            linear("w_ffn_out", l * DFF, DFF, 0, D, lambda kc: A.big[:, kc, :N], [A.BGB], N,
                   lambda mi, mw, p_, b_: to_mix(mi, mw, p_, b_, N), PW=128)
            postnorm_add(4 * l + 3, N)

        sb_ffn_g = sb("gtmp", [128, NMAX])
        GTB = Buf("gtmp")

        hq = sb("hq", [128, NMAX])
        hf = sb("hf", [128, NMAX])
        hlf = sb("hlf", [128, NMAX])
        hb = sb("hb", [128, NMAX])
        hk = sb("hk", [128, NMAX])
        heb = sb("heb", [128, NMAX])
        hr = sb("hr", [128, NMAX])
        hbl = sb("hbl", [128, NMAX // 32])
        hgate = sb("hgate", [128, NMAX], BF16)
        hqd = sb("hqd", [128, NMAX], BF16)
        hkd = sb("hkd", [128, NMAX], BF16)
        hkd2 = sb("hkd2", [128, NMAX], BF16)
        hkd2T = sb("hkd2T", [128, 4, 128], BF16)
        hv = sb("hv", [128, 4, 128], BF16)
        hA = sb("hA", [128, 4, 128], BF16)
        hvm = sb("hvm", [128, 4, 128], BF16)
        HVM = [Buf("hvm%d" % i) for i in range(4)]
        hos = sb("hos", [128, NMAX])
        HQ, HF, HLF, HK, HEB, HR, HBL, HGT, HQD, HKD, HKD2, HKT, HV, HA, HOS = [Buf(n) for n in
            "hq hf hlf hk heb hr hbl hgate hqd hkd hkd2 hkd2T hv hA hos".split()]
        HBB = Buf("hb")

        def hgrn_post(l, h, po, pob, N, gate=None, gate_bufs=None):
            if gate is None:
                gate, gate_bufs = hgate[:, :N], [HGT]
                po = po[:, :N]
            cpy("act", hos[:, :N], po, [pob], [HOS])
            act(sqT[:, 0, :N], po, AF.Square, [pob], [SQB[0]])
            ps2, ps2b = ps_next()
            mm(ps2[:, :N], onesb[:], sqT[:, 0, :N], True, True, [SQB[0], CB], [ps2b])
            act(tmpN[:, :N], ps2[:, :N], AF.Ln, [ps2b], [TB], scale=1.0 / 128, bias=EPS)
            act(rstd[:, :N], tmpN[:, :N], AF.Exp, [TB], [RB], scale=-0.5)
            stt(hos[:, :N], hos[:, :N], gna_sb[:, l, h:h + 1], rstd[:, :N], ALU.mult, ALU.mult, [HOS, RB, CB], [HOS])
            tt("dve", A.big[:, h, :N], hos[:, :N], gate, ALU.mult, [HOS] + gate_bufs, [A.BGB])

        hqd2 = [hqd, sb("hqd_b", [128, NMAX], BF16)]
        hkd_2 = [hkd, sb("hkd_b", [128, NMAX], BF16)]
        hkd2T2 = [hkd2T, sb("hkd2T_b", [128, 4, 128], BF16)]
        hv2 = [hv, sb("hv_b", [128, 4, 128], BF16)]
        hA2 = [hA, sb("hA_b", [128, 4, 128], BF16)]
        hgate2 = [hgate, sb("hgate_b", [128, NMAX], BF16)]
        hbl2 = [hbl, sb("hbl_b", [128, NMAX // 32])]
        HQD2, HKD_2, HKT2, HV2, HA2, HGT2, HBL2 = [[Buf(n + "0"), Buf(n + "1")] for n in "hqd hkd hkt hv hA hgt hbl".split()]

        def hgrn_prompt(l, g):
            N = GN
            prenorm(4 * l + 0, N)

            def stageA1p(h, bs):
                for _ in stageA1(h, bs, 0):
                    pass

            def stageA1g(h, bs):
                return stageA1(h, bs, 1)

            def stageA1(h, bs, part):
                if part == 1:
                    yield from stageA1_gate(h, bs)
                    return
                pan = lambda sec: panel("w_in_a", l * D, D, sec * D + h * 128, 128)
                vq, bq = pan(0)
                pq, pqb = ps_next()
                for kc in range(8):
                    mm(pq[:, :N], vq[:, kc, :], A.hT[:, kc, :N], kc == 0, kc == 7, [bq, A.HB[kc]], [pqb])
                act(hq[:, :N], pq[:, :N], AF.Silu, [pqb], [HQ])
                yield
                vg, bg = pan(3)
                pg, pgb = ps_next()
                for kc in range(8):
                    mm(pg[:, :N], vg[:, kc, :], A.hT[:, kc, :N], kc == 0, kc == 7, [bg, A.HB[kc]], [pgb])
                act(hgate2[bs][:, :N], pg[:, :N], AF.Silu, [pgb], [HGT2[bs]])
                yield
                vf, bf_ = pan(1)
                pf, pfb = ps_next()
                for kc in range(8):
                    mm(pf[:, :N], vf[:, kc, :], A.hT[:, kc, :N], kc == 0, kc == 7, [bf_, A.HB[kc]], [pfb])
                act(hf[:, :N], pf[:, :N], AF.Sigmoid, [pfb], [HF])
                yield
                vi, bi = pan(2)
                pv, pvb = ps_next()
                for tt_ in range(4):
                    for kc in range(8):
                        mm(pv[:, tt_ * 128:(tt_ + 1) * 128], A.hT[:, kc, tt_ * 128:(tt_ + 1) * 128], vi[:, kc, :],
                           kc == 0, kc == 7, [bi, A.HB[kc]], [pvb])
                cpy("act", hv2[bs][:].rearrange("p a b -> p (a b)"), pv[:, :], [pvb], [HV2[bs]])
                yield

            def stageA1_gate(h, bs):
                ts("dve", hf[:, :N], hf[:, :N], oml_sb[:, l, h:h + 1], lb_sb[:, l, h:h + 1], ALU.mult, ALU.add, [HF, CB], [HF])
                ts("dve", hk[:, :N], hf[:, :N], -1.0, 1.0, ALU.mult, ALU.add, [HF], [HK])
                act(hlf[:, :N], hf[:, :N], AF.Ln, [HF], [HLF])
                yield
                S.op("dve", lambda e: e.tensor_tensor_scan(hb[:, :N], reset[:, :N], hlf[:, :N], 0.0, ALU.mult, ALU.add),
                     [HLF, CB], [HBB])
                b3 = hb[:, :N].rearrange("p (c t) -> p c t", t=32)
                cpy("dve", hbl2[bs][:, :N // 32], b3[:, :, 31], [HBB], [HBL2[bs]])
                act(heb[:, :N], hb[:, :N], AF.Exp, [HBB], [HEB])
                yield
                stt(hqd2[bs][:, :N], hq[:, :N], 128 ** -0.5, heb[:, :N], ALU.mult, ALU.mult, [HQ, HEB], [HQD2[bs]])
                act(heb[:, :N], hb[:, :N], AF.Exp, [HBB], [HEB], scale=-1.0)
                yield
                tt("dve", hkd_2[bs][:, :N], hk[:, :N], heb[:, :N], ALU.mult, [HK, HEB], [HKD_2[bs]])
                tt("dve", hr[:, :N].rearrange("p (c t) -> p c t", t=32),
                   hbl2[bs][:, :N // 32].unsqueeze(2).to_broadcast([128, N // 32, 32]), b3, ALU.subtract, [HBL2[bs], HBB], [HR])
                act(hr[:, :N], hr[:, :N], AF.Exp, [HR], [HR])
                yield
                tt("dve", hkd2[:, :N], hk[:, :N], hr[:, :N], ALU.mult, [HK, HR], [HKD2])
                act(hbl2[bs][:, :N // 32], hbl2[bs][:, :N // 32], AF.Exp, [HBL2[bs]], [HBL2[bs]])
                yield

            def stageA2(h, bs):
                ptb, ptbb = ps_next()
                ptv = ptb[:].bitcast(BF16)
                for tt_ in range(4):
                    tr(ptv[:, tt_ * 128:(tt_ + 1) * 128], hkd2[:, tt_ * 128:(tt_ + 1) * 128], identb[:], [HKD2, CB], [ptbb])
                cpy("act", hkd2T2[bs][:].rearrange("p a b -> p (a b)"), ptv[:, 0:512], [ptbb], [HKT2[bs]])
                psc, pscb = ps_next()
                for tt_ in range(4):
                    sl = slice(tt_ * 128, (tt_ + 1) * 128)
                    mm(psc[:, sl], hkd_2[bs][:, sl], hqd2[bs][:, sl], True, True, [HKD_2[bs], HQD2[bs]], [pscb])
                tt("dve", hA2[bs][:], psc[:].rearrange("p (a b) -> p a b", a=4),
                   hmask[:].unsqueeze(1).to_broadcast([128, 4, 128]), ALU.mult, [pscb, CB], [HA2[bs]])

            def stageB(h, bs):
                po, pob = banks[4 + (h % 2)]
                Srot = [(Sst[l][:, h, :], SSB[l][h]), (Stmp[:], STB), (Stmp2[:], STB2)]
                for tt_ in range(4):
                    sl = slice(tt_ * 128, (tt_ + 1) * 128)
                    pd, pdb = banks[6 + (tt_ % 2)]
                    tt("dve", hvm[:], hv2[bs][:, tt_, :].unsqueeze(1).to_broadcast([128, 4, 128]), cmask4[:], ALU.mult,
                       [HV2[bs], CB], [HVM[0]])
                    for c in range(4):
                        mm(pd[:, c * 128:(c + 1) * 128], hkd2T2[bs][:, tt_, :], hvm[:, c, :],
                           True, True, [HKT2[bs], HVM[0]], [pdb])
                    mm(po[:, sl], hv2[bs][:, tt_, :], hA2[bs][:, tt_, :], True, False, [HV2[bs], HA2[bs]], [pob])
                    for c in range(4):
                        ci = tt_ * 4 + c
                        Sc = Srot[ci % 3]
                        Sx = Srot[(ci + 1) % 3]
                        cpy("pool", Sbf[:, c, :], Sc[0], [Sc[1]], [SBB[c]])
                        mm(po[:, ci * 32:(ci + 1) * 32], Sbf[:, c, :], hqd2[bs][:, ci * 32:(ci + 1) * 32], False, c == 3,
                           [SBB[c], HQD2[bs]], [pob], skip_group_check=True)
                        stt(Sx[0], Sc[0], hbl2[bs][:, ci:ci + 1], pd[:, c * 128:(c + 1) * 128],
                            ALU.mult, ALU.add, [Sc[1], HBL2[bs], pdb], [Sx[1]])
                        if c % 2 == 1:
                            yield
                cpy("dve", Srot[0][0], Srot[1][0], [Srot[1][1]], [Srot[0][1]])
                hgrn_post(l, h, po[:, :N], pob, N, gate=hgate2[bs][:, :N], gate_bufs=[HGT2[bs]])
                if g == NG - 1:
                    dma("sp", stp.rearrange("(l h k) v -> l h k v", l=2, h=H)[l, h], Sst[l][:, h, :], [SSB[l][h]], [OUTB])
                yield

            stageA1p(0, 0)
            for _ in stageA1g(0, 0):
                pass
            stageA2(0, 0)
            import itertools
            for h in range(H):
                gb = stageB(h, h % 2)
                ga = (itertools.chain(stageA1(h + 1, (h + 1) % 2, 0), stageA1g(h + 1, (h + 1) % 2))
                      if h + 1 < H else iter(()))
                doneb = donea = False
                while not (doneb and donea):
                    if not donea:
                        try:
                            next(ga)
                        except StopIteration:
                            donea = True
                    if not doneb:
                        try:
                            next(gb)
                        except StopIteration:
                            doneb = True
                if h + 1 < H:
                    stageA2(h + 1, (h + 1) % 2)
            linear("w_out_a", l * D, D, 0, D, lambda kc: A.big[:, kc, :N], [A.BGB], N,
                   lambda mi, mw, p_, b_: to_mix(mi, mw, p_, b_, N))
            postnorm_add(4 * l + 1, N)

        OUTB = Buf("out")


        qaT = sb("qaT", [128, 3, NMAX], BF16)
        QAB = Buf("qaT")
        qnT = sb("qnT", [128, NMAX], BF16)
        QNB = Buf("qnT")
        qrT = sb("qrT", [64, NMAX], BF16)
        QRB = Buf("qrT")
        qnT_b = sb("qnT_b", [128, NMAX], BF16)
        qrT_b = sb("qrT_b", [64, NMAX], BF16)
        qn2 = [qnT, qnT_b]
        qr2 = [qrT, qrT_b]
        QNB2 = [QNB, Buf("qnT_b")]
        QRB2 = [QRB, Buf("qrT_b")]
        wrot = sb("wrot", [128, 8, 64], BF16)
        WRB = Buf("wrot")
        knT = sb("knT", [128, T], BF16)
        KNB = Buf("knT")
        vh = sb("vh", [128, T // 128, 128], BF16)
        VHB = Buf("vh")
        pexp = sb("pexp", [128, 2, NMAX], BF16)
        PXB = [Buf("pexp0"), Buf("pexp1")]
        rt1 = sb("rt1", [64, NMAX])
        rt2 = sb("rt2", [64, NMAX])
        RT1, RT2 = Buf("rt1"), Buf("rt2")
        ctm = sb("ctm", [128, KVL + RD])
        CTM = Buf("ctm")
        cst = sb("cst", [128, 2, 32])
        CST = Buf("cst")
        ssq = sb("ssq", [128, 4])
        SSQ = Buf("ssq")
        junk = sb("junk", [128, KVL])
        JNK = Buf("junk")

        def rope_fm(dst, pr, prb, prot, protb, N, dbufs):
            tt("dve", rt1[:, :N], pr[:64, :N], cosF[:, :N], ALU.mult, [prb, CSB], [RT1])
            tt("dve", rt2[:, :N], prot[:64, :N], sinF[:, :N], ALU.mult, [protb, CSB], [RT2])
            tt("dve", dst, rt1[:, :N], rt2[:, :N], ALU.add, [RT1, RT2], dbufs)

        def make_wrot(view, vb, KC, c0):
            ts("dve", wrot[:, :KC, 0:32], view[:, :, c0 + 32:c0 + 64], -1.0, None, ALU.mult, ALU.bypass, [vb], [WRB])
            cpy("dve", wrot[:, :KC, 32:64], view[:, :, c0:c0 + 32], [vb], [WRB])

        def mla_shared(N, col0, pos0, cp_ap, krp_ap, ckb):
            prenorm(None, N, gain_tile=kvn_sb)
            view, vb = panel("w_dkv", 0, D, 0, KVL + RD)
            make_wrot(view, vb, 8, KVL)
            for mc in range(2):
                pt_, pb_ = ps_next()
                for kc in range(8):
                    mm(pt_[:, :N], view[:, kc, mc * 128:(mc + 1) * 128], A.hT[:, kc, :N], kc == 0, kc == 7, [vb, A.HB[kc]], [pb_])
                cpy("act", A.mixT[:, mc, :N], pt_[:, :N], [pb_], [A.MB])
            pr, prb = ps_next()
            for kc in range(8):
                mm(pr[:64, :N], view[:, kc, KVL:KVL + RD], A.hT[:, kc, :N], kc == 0, kc == 7, [vb, A.HB[kc]], [prb])
            prot, protb = ps_next()
            for kc in range(8):
                mm(prot[:64, :N], wrot[:, kc, :], A.hT[:, kc, :N], kc == 0, kc == 7, [WRB, A.HB[kc]], [protb])
            rope_fm(krT_all[:, col0:col0 + N], pr, prb, prot, protb, N, [ckb])
            rstd_of(lambda kc: A.mixT[:, kc, :N], [A.MB], 2, N, KVL)
            for mc in range(2):
                stt(cT_all[:, mc, col0:col0 + N], A.mixT[:, mc, :N], kvan_sb[:, mc:mc + 1], rstd[:, :N], ALU.mult, ALU.mult,
                    [A.MB, RB, CB], [ckb])
            for t0 in range(0, N, 128):
                rows = min(128, N - t0)
                pt_, pb_ = ps_next()
                for kc in range(8):
                    mm(pt_[:rows, :KVL + RD], A.hT[:, kc, t0:t0 + rows], view[:, kc, :], kc == 0, kc == 7, [vb, A.HB[kc]], [pb_])
                act(junk[:rows, :], pt_[:rows, :KVL], AF.Square, [pb_], [JNK, SSQ], accum_out=ssq[:rows, 0:1])
                act(ssq[:rows, 1:2], ssq[:rows, 0:1], AF.Sqrt, [SSQ], [SSQ], scale=1.0 / KVL, bias=EPS)
                recip(ssq[:rows, 2:3], ssq[:rows, 1:2], [SSQ], [SSQ])
                stt(ctm[:rows, :KVL], pt_[:rows, :KVL], ssq[:rows, 2:3], kvanb_sb[:rows, :], ALU.mult, ALU.mult, [pb_, SSQ, CB], [CTM])
                dma("sp", cst[:rows, 0, :], cd["cosT"][pos0 + t0:pos0 + t0 + rows, :], [], [CST])
                dma("sp", cst[:rows, 1, :], cd["sinT"][pos0 + t0:pos0 + t0 + rows, :], [], [CST])
                x1 = pt_[:rows, KVL:KVL + 32]
                x2 = pt_[:rows, KVL + 32:KVL + 64]
                o1 = ctm[:rows, KVL:KVL + 32]
                o2 = ctm[:rows, KVL + 32:KVL + 64]
                tt("dve", junk[:rows, 0:32], x2, cst[:rows, 1, :], ALU.mult, [pb_, CST], [JNK])
                tt("dve", o1, x1, cst[:rows, 0, :], ALU.mult, [pb_, CST], [CTM])
                tt("dve", o1, o1, junk[:rows, 0:32], ALU.subtract, [CTM, JNK], [CTM])
                tt("dve", junk[:rows, 32:64], x1, cst[:rows, 1, :], ALU.mult, [pb_, CST], [JNK])
                tt("dve", o2, x2, cst[:rows, 0, :], ALU.mult, [pb_, CST], [CTM])
                tt("dve", o2, o2, junk[:rows, 32:64], ALU.add, [CTM, JNK], [CTM])
                dma("sp", cp_ap[t0:t0 + rows, :], ctm[:rows, :KVL], [CTM], [OUTB])
                dma("sp", krp_ap[t0:t0 + rows, :], ctm[:rows, KVL:KVL + RD], [CTM], [OUTB])

        def mla_q(l, N):
            j = l - 2
            prenorm(4 * l + 0, N)
            linear("w_dq", j * D, D, 0, QL, lambda kc: A.hT[:, kc, :N], [A.HB], N,
                   lambda mi, mw, p_, b_: to_mix(mi, mw, p_, b_, N), PW=128)
            rstd_of(lambda kc: A.mixT[:, kc, :N], [A.MB], 3, N, QL)
            for mc in range(3):
                stt(qaT[:, mc, :N], A.mixT[:, mc, :N], qan_sb[:, j, mc:mc + 1], rstd[:, :N], ALU.mult, ALU.mult, [A.MB, RB, CB], [QAB])

        def mla_q_head(l, h, N, bs=0):
            j = l - 2
            qn_, qr_, qnb_, qrb_ = qn2[bs], qr2[bs], QNB2[bs], QRB2[bs]
            vw, vb = panel("w_uq", j * QL, QL, h * 192, 192)
            make_wrot(vw, vb, 3, 128)
            pq, pqb = ps_next()
            for kc in range(3):
                mm(pq[:, :N], vw[:, kc, 0:128], qaT[:, kc, :N], kc == 0, kc == 2, [vb, QAB], [pqb])
            cpy("act", qn_[:, :N], pq[:, :N], [pqb], [qnb_])
            pr, prb = ps_next()
            for kc in range(3):
                mm(pr[:64, :N], vw[:, kc, 128:192], qaT[:, kc, :N], kc == 0, kc == 2, [vb, QAB], [prb])
            prot, protb = ps_next()
            for kc in range(3):
                mm(prot[:64, :N], wrot[:, kc, :], qaT[:, kc, :N], kc == 0, kc == 2, [WRB, QAB], [protb])
            rope_fm(qr_[:, :N], pr, prb, prot, protb, N, [qrb_])

        def mla_prompt(l, g):
            N = GN
            j = l - 2
            mla_q(l, N)
            mla_q_head(l, 0, N, 0)
            for h in range(H):
                qn_, qr_, qnb_, qrb_ = qn2[h % 2], qr2[h % 2], QNB2[h % 2], QRB2[h % 2]
                for kb in range(g + 1):
                    pk, pkb = ps_next()
                    for cc in range(2):
                        mm(pk[:, :], wukv[:, cc, h * 256:h * 256 + 128], cT_all[:, cc, kb * 512:(kb + 1) * 512], cc == 0, cc == 1,
                           [WUB, CKB[kb]], [pkb])
                    cpy("act", knT[:, kb * 512:(kb + 1) * 512], pk[:, :], [pkb], [KNB])
                    pv, pvb = ps_next()
                    for tt_ in range(4):
                        for cc in range(2):
                            mm(pv[:, tt_ * 128:(tt_ + 1) * 128], cT_all[:, cc, kb * 512 + tt_ * 128:kb * 512 + (tt_ + 1) * 128],
                               wukv[:, cc, h * 256 + 128:h * 256 + 256], cc == 0, cc == 1, [WUB, CKB[kb]], [pvb])
                    cpy("act", vh[:, kb * 4:(kb + 1) * 4, :].rearrange("p a b -> p (a b)"), pv[:, :], [pvb], [VHB])
                if h + 1 < H:
                    mla_q_head(l, h + 1, N, (h + 1) % 2)
                po, pob = banks[4 + (h % 2)]
                pden, pdenb = banks[6 + (h % 2)]
                ntile = 4 * g + 4
                def S_(i):
                    r = i - 4 * g
                    q0 = 128 * r if r > 0 else 0
                    ps_, psb = ps_next()
                    mm(ps_[:, q0:N], knT[:, i * 128:(i + 1) * 128], qn_[:, q0:N], True, False, [KNB, qnb_], [psb])
                    mm(ps_[:, q0:N], krT_all[:, i * 128:(i + 1) * 128], qr_[:, q0:N], False, True, [CKB[i // 4], qrb_], [psb])
                    px = pexp[:, i % 2, :]
                    pxb = PXB[i % 2]
                    act(px[:, q0:N], ps_[:, q0:N], AF.Exp, [psb], [pxb], scale=SCALE)
                    if r >= 0:
                        tt("dve", px[:, q0:q0 + 128], px[:, q0:q0 + 128], ctri[:], ALU.mult, [pxb, CB], [pxb])

                def V_(i):
                    r = i - 4 * g
                    q0 = 128 * r if r > 0 else 0
                    px = pexp[:, i % 2, :]
                    pxb = PXB[i % 2]
                    mm(po[:, q0:N], vh[:, i, :], px[:, q0:N], i == 0, i == ntile - 1, [VHB, pxb], [pob], skip_group_check=True)
                    mm(pden[:, q0:N], onesb[:], px[:, q0:N], i == 0, i == ntile - 1, [CB, pxb], [pdenb], skip_group_check=True)

                S_(0)
                for i in range(ntile):
                    if i + 1 < ntile:
                        S_(i + 1)
                    V_(i)
                recip(rstd[:, :N], pden[:, :N], [pdenb], [RB])
                tt("dve", A.big[:, h, :N], po[:, :N], rstd[:, :N], ALU.mult, [pob, RB], [A.BGB])
            linear("w_out_b", j * D, D, 0, D, lambda kc: A.big[:, kc, :N], [A.BGB], N,
                   lambda mi, mw, p_, b_: to_mix(mi, mw, p_, b_, N))
            postnorm_add(4 * l + 1, N)

        xin = sb("xin", [128, D])
        XIN = Buf("xin")

        def load_x(src_ap, rows, col0):
            dma("sp", xin[:rows, :], src_ap, [], [XIN])
            for half in range(2):
                pt_, pb_ = ps_next()
                for j in range(4):
                    kc = half * 4 + j
                    tr(pt_[:, j * 128:j * 128 + rows], xin[:rows, kc * 128:(kc + 1) * 128], ident[:rows, :rows], [XIN, CB], [pb_])
                for j in range(4):
                    kc = half * 4 + j
                    cpy("act", A.xT[:, kc, col0:col0 + rows], pt_[:, j * 128:j * 128 + rows], [pb_], [A.XB[kc]])

        def store_x(dst_ap, rows, col0):
            for half in range(2):
                pt_, pb_ = ps_next()
                for j in range(4):
                    kc = half * 4 + j
                    tr(pt_[:rows, j * 128:(j + 1) * 128], A.xT[:, kc, col0:col0 + rows], ident[:], [A.XB[kc], CB], [pb_])
                cpy("act", xin[:rows, half * 512:(half + 1) * 512], pt_[:rows, :], [pb_], [XIN])
            dma("sp", dst_ap, xin[:rows, :], [XIN], [OUTB])


        _pools = [[big[:].rearrange("p a b -> p (a b)"), 0, 22 * NMAX], [mixT[:].rearrange("p a b -> p (a b)").bitcast(BF16), 0, 16 * NMAX],
                  [xT[:].rearrange("p a b -> p (a b)").bitcast(BF16), 0, 16 * NMAX], [hT[:].rearrange("p a b -> p (a b)"), 0, 8 * NMAX]]

        def carve(shape, dt=F32):
            esz = 4 if dt in (F32, I32) else 2
            n = int(np.prod(shape[1:])) * esz // 2
            n = (n + 15) // 16 * 16
            for pl in _pools:
                if pl[1] + n <= pl[2]:
                    v = pl[0][:, pl[1]:pl[1] + n]
                    pl[1] += n
                    if esz == 4:
                        v = v.bitcast(dt)
                    v = v[:shape[0], :int(np.prod(shape[1:]))]
                    if len(shape) == 3:
                        v = v.rearrange("p (a b) -> p a b", a=shape[1])
                    elif len(shape) == 4:
                        v = v.rearrange("p (a b c) -> p a b c", a=shape[1], b=shape[2])
                    return v
            raise RuntimeError("carve: out of space " + str(shape))

        NST = 3
        stile = [(carve([128, H, 128]), Buf("stile%d" % i)) for i in range(NST)]
        sel32 = carve([NS, NS * 128])
        vs32 = carve([NS, D])
        for pl in _pools:
            pl[1] = 0
        NCT = 12
        ctile = [(carve([128, 4, KVL], BF16), carve([128, 4, RD], BF16), Buf("ct%d" % i)) for i in range(NCT)]
        qpad = carve([128, 2, NS, 128], BF16)
        rpad = carve([64, NS, 128], BF16)
        wukT = carve([128, H, KVL], BF16)
        cTs = carve([128, 2, 1024], BF16)
        rTs = sb("rTs", [64, 2, 512], BF16)
        pex2 = carve([128, 4, 512], BF16)
        snb = sb("snb", [128, H, 128], BF16)
        snb_b = carve([128, H, 128], BF16)
        snb2 = [snb, snb_b]
        SNB2 = [Buf("snb0"), Buf("snb1")]
        xTs = sb("xTs", [128, 8, NS])
        hTs = sb("hTs", [128, 8, NS], BF16)
        mixTs = sb("mixTs", [128, 8, NS])
        bigs = sb("bigs", [128, 22, NS], BF16)
        sq32 = sb("sq32", [128, H, NS])
        sf32 = sb("sf32", [128, H, NS])
        sk32 = sb("sk32", [128, H, NS])
        sgate = sb("sgate", [128, H, NS], BF16)
        sqb = sb("sqb", [128, H, NS], BF16)
        SQ32, SF32, SK32, SGT, SQBB = [Buf(n) for n in "sq32 sf32 sk32 sgate sqb".split()]
        VS32 = Buf("vs32")
        SNB = Buf("snb")
        stmp = sb("stmp", [128, 2, 128])
        STM = [Buf("stmp0"), Buf("stmp1")]

        def hgrn_sample(l):
            N = NS
            prenorm(4 * l + 0, N)
            for sec, fn, dst, db in ((0, AF.Silu, sq32, SQ32), (1, AF.Sigmoid, sf32, SF32), (3, AF.Silu, sgate, SGT)):
                linear("w_in_a", l * D, D, sec * D, D, lambda kc: A.hT[:, kc, :N], [A.HB], N,
                       (lambda fn, dst, db: (lambda mi, mw, p_, b_: act(dst[:, mi, :], p_[:, :N], fn, [b_], [db])))(fn, dst, db))
            for q4 in range(4):
                vw, vb = panel("w_in_a", l * D, D, 2 * D + q4 * 256, 256)
                pt_, pb_ = ps_next()
                for kc in range(8):
                    mm(pt_[:N, :256], A.hT[:, kc, :N], vw[:, kc, :], kc == 0, kc == 7, [vb, A.HB[kc]], [pb_])
                cpy("act", vs32[:, q4 * 256:(q4 + 1) * 256], pt_[:N, :256], [pb_], [VS32])
            lbb = lb_sb[:, l, :].unsqueeze(2).to_broadcast([128, H, NS])
            omb = oml_sb[:, l, :].unsqueeze(2).to_broadcast([128, H, NS])
            tt("dve", sf32[:], sf32[:], omb, ALU.mult, [SF32, CB], [SF32])
            tt("dve", sf32[:], sf32[:], lbb, ALU.add, [SF32, CB], [SF32])
            ts("dve", sk32[:], sf32[:], -1.0, 1.0, ALU.mult, ALU.add, [SF32], [SK32])
            ts("dve", sqb[:], sq32[:], 128 ** -0.5, None, ALU.mult, ALU.bypass, [SQ32], [SQBB])
            st_in = st.rearrange("(l s h k) v -> l s k h v", l=2, s=NS, h=H)
            st_out = sts.rearrange("(l s h k) v -> l s k h v", l=2, s=NS, h=H)
            po, pob = banks[4]
            vbanks = {}

            def vb_stage(s_):
                stl, stb = stile[s_ % NST]
                dma("sp", stl[:], st_in[l, s_], [], [stb])
                pair = []
                for half in range(2):
                    pvb_, pvbb = banks[(s_ % 2) * 2 + half]
                    mm(pvb_[:, :], sel32[:, s_ * 128:(s_ + 1) * 128], vs32[:, half * 512:(half + 1) * 512], True, True,
                       [CB, VS32], [pvbb])
                    pair.append((pvb_, pvbb))
                vbanks[s_] = pair

            def upd_stage(s_):
                stl, stb = stile[s_ % NST]
                for half in range(2):
                    pvb_, pvbb = vbanks[s_][half]
                    for hh in range(4):
                        h = half * 4 + hh
                        tb = (h % 2)
                        act(stmp[:, tb, :], pvb_[:, hh * 128:(hh + 1) * 128], AF.Copy, [pvbb, SK32], [STM[tb]],
                            scale=sk32[:, h, s_:s_ + 1])
                        stt(stl[:, h, :], stl[:, h, :], sf32[:, h, s_:s_ + 1], stmp[:, tb, :], ALU.mult, ALU.add,
                            [stb, SF32, STM[tb]], [stb])
                sn = snb2[s_ % 2]
                cpy("act", sn[:].rearrange("p a b -> p (a b)"), stl[:].rearrange("p a b -> p (a b)"), [stb], [SNB2[s_ % 2]])
                dma("sp", st_out[l, s_], stl[:], [stb], [OUTB])

            def o_stage(s_):
                sn = snb2[s_ % 2]
                for h in range(H):
                    mm(po[:, h * NS + s_:h * NS + s_ + 1], sn[:, h, :], sqb[:, h, s_:s_ + 1], True, True, [SNB2[s_ % 2], SQBB], [pob])

            vb_stage(0)
            for s_ in range(NS):
                if s_ + 1 < NS:
                    vb_stage(s_ + 1)
                upd_stage(s_)
                if s_ >= 1:
                    o_stage(s_ - 1)
            o_stage(NS - 1)
            for h in range(H):
                hgrn_post(l, h, po[:, h * NS:(h + 1) * NS], pob, N, gate=sgate[:, h, :], gate_bufs=[SGT])
            linear("w_out_a", l * D, D, 0, D, lambda kc: A.big[:, kc, :N], [A.BGB], N,
                   lambda mi, mw, p_, b_: to_mix(mi, mw, p_, b_, N))
            postnorm_add(4 * l + 1, N)

        WKT = Buf("wukT")
        qlat = sb("qlat", [128, 2, NS * H], BF16)
        QLB = Buf("qlat")
        qrall = sb("qrall", [64, NS * H], BF16)
        QRA = Buf("qrall")
        QPB = Buf("qpad")
        CTS = [Buf("cTs0"), Buf("cTs1")]
        RTS = [Buf("rTs0"), Buf("rTs1")]
        PX2 = [Buf("pex2_%d" % i) for i in range(4)]
        pTs = sb("pTs", [128, 4, 128], BF16)
        PTS = Buf("pTs")
        idx32 = sb("idx32", [128, 2 * 128], I32)
        IDXB = Buf("idx")
        ptd_sb = sb("ptd_sb", [128, 2, 4], I32)
        ptd_f = sb("ptd_f", [128, 2, 128])
        pmod = sb("pmod", [128, 1])
        newmask = sb("newmask", [128, NS])
        pnew = sb("pnew", [128, NS])
        pnewT = sb("pnewT", [NS, 128], BF16)
        ctmb = sb("ctmb", [NS, KVL], BF16)
        PNB = Buf("pnew")
        olat = sb("olat", [128, 2, NS * H], BF16)
        OLB = Buf("olat")

        def decode_setup_h():
            dma("sp", sel32[:], cd["sel"], [], [CB])

        def decode_setup():
            dma("sp", pmod[:], cd["pmod"], [], [CB])
            dma("sp", newmask[:], cd["newmask"], [], [CB])
            for h in range(H):
                pt_, pb_ = ps_next()
                pv_ = pt_[:].bitcast(BF16)
                for cc in range(2):
                    tr(pv_[:, cc * 128:(cc + 1) * 128], wukv[:, cc, h * 256:h * 256 + 128], identb[:], [WUB, CB], [pb_])
                cpy("act", wukT[:, h, :], pv_[:, 0:256], [pb_], [WKT])
            for hf_ in range(2):
                dma("sp", ptd_sb[:, hf_, :], ptd[hf_ * 128:(hf_ + 1) * 128, :], [], [IDXB])
            for hf_ in range(2):
                cpy("dve", ptd_f[:, hf_, :].rearrange("p (a b) -> p a b", b=32),
                    ptd_sb[:, hf_, :].unsqueeze(2).to_broadcast([128, 4, 32]), [IDXB], [IDXB])
                pt_, pb_ = ps_next()
                tr(pt_[:, 0:128], ptd_f[:, hf_, :], ident[:], [IDXB, CB], [pb_])
                ts("dve", idx32[:, hf_ * 128:(hf_ + 1) * 128], pt_[:, 0:128], 32.0, pmod[:, 0:1], ALU.mult, ALU.add, [pb_, CB], [IDXB])
            for i in range(4):
                S.op("dve", (lambda i=i: (lambda e: e.memset(pex2[:, i, :], 0.0)))(), [], [PX2[i]])
            S.op("dve", lambda e: e.memset(qpad[:], 0.0), [], [QPB])
            S.op("dve", lambda e: e.memset(rpad[:], 0.0), [], [QPB])

        def gather(s_, j4, slot):
            ct_, kt_, cb_ = ctile[slot]
            d_ = s_ * 16 + j4
            off = bass.IndirectOffsetOnAxis(ap=idx32[:, d_:d_ + 1], axis=0)
            S.op("pool", lambda e: e.indirect_dma_start(out=ct_[:].rearrange("p a b -> p (a b)"), out_offset=None, in_=ckv[:, :],
                                                        in_offset=off), [IDXB], [cb_], dma=True)
            off2 = bass.IndirectOffsetOnAxis(ap=idx32[:, d_:d_ + 1], axis=0)
            S.op("pool", lambda e: e.indirect_dma_start(out=kt_[:].rearrange("p a b -> p (a b)"), out_offset=None, in_=ckr[:, :],
                                                        in_offset=off2), [IDXB], [cb_], dma=True)

        def mla_sample(l):
            N = NS
            j = l - 2
            mla_q(l, N)
            for h in range(H):
                mla_q_head(l, h, N)
                pt_, pb_ = ps_next()
                for cc in range(2):
                    mm(pt_[:, cc * NS:(cc + 1) * NS], wukT[:, h, cc * 128:(cc + 1) * 128], qnT[:, :N], True, True, [WKT, QNB], [pb_])
                for cc in range(2):
                    cpy("act", qlat[:, cc, :].rearrange("p (s h) -> p s h", h=H)[:, :, h], pt_[:, cc * NS:(cc + 1) * NS], [pb_], [QLB])
                cpy("act", qrall[:, :].rearrange("p (s h) -> p s h", h=H)[:, :, h], qrT[:, :N], [QRB], [QRA])
            for s_ in range(NS):
                for cc in range(2):
                    cpy("dve", qpad[:, cc, s_, s_ * 8:(s_ + 1) * 8], qlat[:, cc, s_ * 8:(s_ + 1) * 8], [QLB], [QPB])
                cpy("dve", rpad[:, s_, s_ * 8:(s_ + 1) * 8], qrall[:, s_ * 8:(s_ + 1) * 8], [QRA], [QPB])
            po, pob = banks[4]
            pden, pdenb = banks[5]
            blocks = [(j4, sg) for j4 in range(16) for sg in range(4)]
            reqs = [(sg * 4 + k, j4) for (j4, sg) in blocks for k in range(4)]
            issued = {"n": 0}

            def ensure(n):
                while issued["n"] < min(n, len(reqs)):
                    s2, j42 = reqs[issued["n"]]
                    gather(s2, j42, issued["n"] % NCT)
                    issued["n"] += 1

            first = True
            for bi, (j4, sg) in enumerate(blocks):
                ensure(bi * 4 + NCT)
                psc, pscb = banks[6 + (bi % 2)]

                def T_(k):
                    ct_, kt_, cb_ = ctile[(bi * 4 + k) % NCT]
                    slot = (bi * 4 + k) % 2
                    ptA, ptAb = ps_next()
                    pvA = ptA[:].bitcast(BF16)
                    for cc in range(2):
                        for t4 in range(4):
                            tr(pvA[:, (cc * 4 + t4) * 128:(cc * 4 + t4 + 1) * 128], ct_[:, t4, cc * 128:(cc + 1) * 128], identb[:],
                               [cb_, CB], [ptAb])
                    cpy("act", cTs[:, slot, :], pvA[:, :], [ptAb], [CTS[slot]])
                    ptB, ptBb = ps_next()
                    pvB = ptB[:].bitcast(BF16)
                    for t4 in range(4):
                        tr(pvB[:64, t4 * 128:(t4 + 1) * 128], kt_[:, t4, :], identb[:], [cb_, CB], [ptBb])
                    cpy("dve", rTs[:, slot, :], pvB[:64, 0:512], [ptBb], [RTS[slot]])

                def S_(k):
                    s_ = sg * 4 + k
                    slot = (bi * 4 + k) % 2
                    for cc in range(2):
                        mm(psc[:, :], qpad[:, cc, s_, :], cTs[:, slot, cc * 512:(cc + 1) * 512], k == 0 and cc == 0, False,
                           [QPB, CTS[slot]], [pscb])
                    mm(psc[:, :], rpad[:, s_, :], rTs[:, slot, :], False, k == 3, [QPB, RTS[slot]], [pscb])

                T_(0)
                for k in range(4):
                    if k + 1 < 4:
                        T_(k + 1)
                    S_(k)
                band = slice(32 * sg, 32 * sg + 32)
                act(pex2[band, sg, :], psc[band, :], AF.Exp, [pscb], [PX2[sg]], scale=SCALE)
                ptP, ptPb = ps_next()
                pvP = ptP[:].bitcast(BF16)
                for t4 in range(4):
                    tr(pvP[:, t4 * 128:(t4 + 1) * 128], pex2[:, sg, t4 * 128:(t4 + 1) * 128], identb[:], [PX2[sg], CB], [ptPb])
                cpy("act", pTs[:].rearrange("p a b -> p (a b)"), pvP[:, 0:512], [ptPb], [PTS])
                for t4 in range(4):
                    mm(pden[:, 0:128], onesb[:], pTs[:, t4, :], first and t4 == 0, False, [CB, PTS], [pdenb], skip_group_check=True)
                for k in range(4):
                    s_ = sg * 4 + k
                    ct_, kt_, cb_ = ctile[(bi * 4 + k) % NCT]
                    for t4 in range(4):
                        for cc in range(2):
                            mm(po[:, cc * 128 + s_ * 8:cc * 128 + s_ * 8 + 8], ct_[:, t4, cc * 128:(cc + 1) * 128],
                               pTs[:, t4, s_ * 8:(s_ + 1) * 8], bi == 0 and k == 0 and t4 == 0 and cc == 0, False, [cb_, PTS], [pob], skip_group_check=True)
                first = False
            col0 = T
            psn, psnb = ps_next()
            for cc in range(2):
                mm(psn[:, :NS], qlat[:, cc, :], cT_all[:, cc, col0:col0 + NS], cc == 0, False, [QLB, CKB[NG]], [psnb])
            mm(psn[:, :NS], qrall[:, :], krT_all[:, col0:col0 + NS], False, True, [QRA, CKB[NG]], [psnb])
            act(pnew[:, :], psn[:, :NS], AF.Exp, [psnb], [PNB], scale=SCALE)
            tt("dve", pnew[:, :], pnew[:, :], newmask[:, :], ALU.mult, [PNB, CB], [PNB])
            ptn, ptnb = ps_next()
            tr(ptn[:NS, 0:128], pnew[:, :], ident[:], [PNB, CB], [ptnb])
            cpy("act", pnewT[:, :], ptn[:NS, 0:128], [ptnb], [PNB])
            mm(pden[:, 0:128], onesb[:NS, :], pnewT[:, :], False, True, [CB, PNB], [pdenb], skip_group_check=True)
            ptc, ptcb = ps_next()
            pvc = ptc[:].bitcast(BF16)
            for cc in range(2):
                tr(pvc[:NS, cc * 128:(cc + 1) * 128], cT_all[:, cc, col0:col0 + NS], identb[:], [CKB[NG], CB], [ptcb])
            cpy("act", ctmb[:, :], pvc[:NS, 0:KVL], [ptcb], [PNB])
            for cc in range(2):
                mm(po[:, cc * 128:(cc + 1) * 128], ctmb[:, cc * 128:(cc + 1) * 128], pnewT[:, :], False, True, [PNB], [pob],
                   skip_group_check=True)
            recip(rstd[:, :128], pden[:, 0:128], [pdenb], [RB])
            for cc in range(2):
                tt("dve", olat[:, cc, :], po[:, cc * 128:(cc + 1) * 128], rstd[:, :128], ALU.mult, [pob, RB], [OLB])
            for h in range(H):
                pt_, pb_ = ps_next()
                for cc in range(2):
                    mm(pt_[:, :NS], wukv[:, cc, h * 256 + 128:h * 256 + 256], olat[:, cc, :].rearrange("p (s h) -> p s h", h=H)[:, :, h],
                       cc == 0, cc == 1, [WUB, OLB], [pb_])
                cpy("act", A.big[:, h, :NS], pt_[:, :NS], [pb_], [A.BGB])
            linear("w_out_b", j * D, D, 0, D, lambda kc: A.big[:, kc, :N], [A.BGB], N,
                   lambda mi, mw, p_, b_: to_mix(mi, mw, p_, b_, N))
            postnorm_add(4 * l + 1, N)

        def sample_group():
            N = NS
            S.barrier()
            A.xT, A.hT, A.mixT, A.big = xTs, hTs, mixTs, bigs
            A.XB, A.HB, A.MB, A.BGB = [Buf("xTs%d" % i) for i in range(8)], [Buf("hTs%d" % i) for i in range(8)], [Buf("mixTs%d" % i) for i in range(8)], Buf("bigs")
            load_x(xs[:, :], NS, 0)
            if "sA" not in DBG:
                decode_setup_h()
            for l in range(n_layers):
                if "sA" in DBG or "sB" in DBG:
                    break
                if l < 2:
                    hgrn_sample(l)
                else:
                    if l == 2:
                        S.barrier()
                        decode_setup()
                        dma("sp", cosF[:, :N], cd["cosF"][:, T:T + N], [], [CSB])
                        dma("sp", sinF[:, :N], cd["sinF"][:, T:T + N], [], [CSB])
                        mla_shared(N, T, T, cs[:, :], krs[:, :], CKB[NG])
                    mla_sample(l)
                if "noffn" not in DBG:
                    ffn(l, N)
            store_x(ys[:, :], NS, 0)

        dma("pool", wukv[:], w_ukv.rearrange("(k p) c -> p k c", p=128), [], [WUB])
        for l in range(2):
            for h in range(H):
                S.op("dve", (lambda l=l, h=h: (lambda e: e.memset(Sst[l][:, h, :], 0.0)))(), [], [SSB[l][h]])

        for g in groups:
            if g == "s":
                sample_group()
                continue
            N = GN
            for tt_ in range(4):
                load_x(xp[g * GN + tt_ * 128: g * GN + (tt_ + 1) * 128, :], 128, tt_ * 128)
            for l in range(n_layers):
                if l < 2:
                    if "nohgrn" not in DBG:
                        hgrn_prompt(l, g)
                else:
                    if l == 2:
                        dma("sp", cosF[:, :N], cd["cosF"][:, g * GN:g * GN + N], [], [CSB])
                        dma("sp", sinF[:, :N], cd["sinF"][:, g * GN:g * GN + N], [], [CSB])
                        mla_shared(N, g * GN, g * GN, cp[g * GN:(g + 1) * GN, :], krp[g * GN:(g + 1) * GN, :], CKB[g])
                    mla_prompt(l, g)
                if "noffn" not in DBG:
                    ffn(l, N)
            for tt_ in range(4):
                store_x(yp[g * GN + tt_ * 128: g * GN + (tt_ + 1) * 128, :], 128, tt_ * 128)

        if "dump" in DBG:
            for nm_, t_, bufs_ in (("hT", hT, [A.HB]), ("big", big, [A.BGB]), ("mixT", mixT, [A.MB]), ("rstd", rstd, [RB]), ("xT", xT, [A.XB]),
                                     ("hqd", hqd, [HQD]), ("hkd", hkd, [HKD]), ("hkd2", hkd2, [HKD2]), ("hv", hv, [HV]), ("hA", hA, [HA]),
                                     ("lb_sb", lb_sb, [CB]), ("oml_sb", oml_sb, [CB]), ("lbl_sb", lbl_sb, [CB]), ("hq", hq, [HQ]), ("hb", hb, [HBB]), ("hlf", hlf, [HLF]), ("hf", hf, [HF]), ("hk", hk, [HK]), ("hbl", hbl, [HBL]), ("hgate", hgate, [HGT]), ("hos", hos, [HOS]), ("hkd2T", hkd2T, [HKT])):
                shp = list(t_.shape)
                flat = [shp[0], int(np.prod(shp[1:]))]
                dd = nc.dram_tensor("dbg_" + nm_, flat, t_.dtype, kind="ExternalOutput").ap()
                src = t_[:] if len(shp) == 2 else t_[:].rearrange("p a b -> p (a b)")
                dma("sp", dd, src, bufs_, [OUTB])
        S.op("sp", None, [OUTB], [])
        block = stack.enter_context(nc.Block())
        S.emit(block)
    return nc, req_log


def build2(n_pool, **kw):
    _, plan = build(n_pool, plan=None, **kw)
    nc, _ = build(n_pool, plan=plan, **kw)
    return nc


def _cols(v, kc):
    v = np.asarray(v, np.float32)
    R_ = v.shape[0]
    return np.ascontiguousarray(v.reshape(R_, kc, 128).transpose(2, 0, 1).reshape(128, R_ * kc))


def shared_inputs(inp):
    f = lambda a: np.ascontiguousarray(np.asarray(a, np.float32))
    m = {}
    n_pool = inp["cache_kv_latent"].shape[0]
    m["ckv"] = f(inp["cache_kv_latent"]).reshape(n_pool * 32, 4 * KVL)
    m["ckr"] = f(inp["cache_k_rope"]).reshape(n_pool * 32, 4 * RD)
    m["gains"] = _cols(f(inp["norm_gains"]).reshape(16, D), 8)
    m["w_ffn_in"] = f(inp["w_ffn_in"]).reshape(4 * D, 2 * DFF)
    m["w_ffn_out"] = f(inp["w_ffn_out"]).reshape(4 * DFF, D)
    m["w_in_a"] = f(inp["w_in_a"]).reshape(2 * D, 4 * D)
    m["lbl"] = _cols(f(inp["lb_logits"]), 8)
    m["gna"] = _cols(f(inp["g_norm_a"]), 8)
    m["w_out_a"] = f(inp["w_out_a"]).reshape(2 * D, D)
    m["kvn"] = _cols(f(inp["kv_norm"]).reshape(1, D), 8)
    m["w_dkv"] = f(inp["w_dkv"])
    m["kvan"] = _cols(f(inp["kv_a_norm"]).reshape(1, KVL), 2)
    m["kvan_b"] = np.ascontiguousarray(np.broadcast_to(f(inp["kv_a_norm"]).reshape(1, KVL), (128, KVL)))
    m["w_ukv"] = f(inp["w_ukv"])
    m["w_dq"] = f(inp["w_dq"]).reshape(2 * D, QL)
    m["qan"] = _cols(f(inp["q_a_norm"]), 3)
    m["w_uq"] = f(inp["w_uq"]).reshape(2 * QL, H * 192)
    m["w_out_b"] = f(inp["w_out_b"]).reshape(2 * D, D)
    for k, v in _consts().items():
        m["c_" + k] = v
    return m


def core_inputs(inp, shared, c):
    m = dict(shared)
    m["xp"] = np.ascontiguousarray(np.asarray(inp["x_prompt"][c], np.float32))
    m["xs"] = np.ascontiguousarray(np.asarray(inp["x_sample"][c * NS:(c + 1) * NS, 0], np.float32))
    m["st"] = np.ascontiguousarray(np.asarray(inp["state_hgrn"][:, c * NS:(c + 1) * NS], np.float32)).reshape(2 * NS * H * 128, 128)
    m["ptd"] = np.ascontiguousarray(np.asarray(inp["page_table"][c * NS:(c + 1) * NS], np.int32)).reshape(NS * 16, 4)
    return m


def kernel(**inp):
    n_cores = 8
    n_pool = inp["cache_kv_latent"].shape[0]
    nc = build2(n_pool)
    shared = shared_inputs(inp)
    in_maps = [core_inputs(inp, shared, c) for c in range(n_cores)]
    res = run_bass_kernel_spmd(nc, in_maps, core_ids=list(range(n_cores))).results
    y_p = np.stack([r["yp"] for r in res]).reshape(8, T, D)
    y_s = np.concatenate([r["ys"] for r in res]).reshape(128, 1, D)
    st_p = np.stack([r["stp"].reshape(2, H, 128, 128) for r in res], axis=1)
    c_p = np.stack([r["cp"] for r in res]).reshape(8, T, KVL)
    kr_p = np.stack([r["krp"] for r in res]).reshape(8, T, RD)
    st_s = np.concatenate([r["sts"].reshape(2, NS, H, 128, 128) for r in res], axis=1)
    c_s = np.concatenate([r["cs"] for r in res]).reshape(128, 1, KVL)
    kr_s = np.concatenate([r["krs"] for r in res]).reshape(128, 1, RD)
    return tuple(np.ascontiguousarray(a.astype(np.float32)) for a in (y_p, y_s, st_p, c_p, kr_p, st_s, c_s, kr_s))
```

```python
import contextlib
import os
import numpy as np
import concourse.bass as bass
import concourse.mybir as mybir
from concourse.bass_utils import run_bass_kernel_spmd

F32 = mybir.dt.float32
BF16 = mybir.dt.bfloat16
I32 = mybir.dt.int32
AF = mybir.ActivationFunctionType
ALU = mybir.AluOpType

D = 1024
T = 2048
NS = 16
H = 8
DFF = 2816
QL = 384
KVL = 256
RD = 64
PAST = 8192
NPG = 64
EPS = 1e-6
SCALE = (128 + 64) ** -0.5
DBG = set(os.environ.get("KDBG", "").split(","))
GN = 512
NG = T // GN


class Buf:
    __slots__ = ("name", "w", "r")

    def __init__(self, name=""):
        self.name = name
        self.w = {}
        self.r = {}


class Sched:
    ENG = ("pe", "act", "dve", "pool", "sp")

    def __init__(self, nc, stack, n_dsem=40):
        self.nc = nc
        self.ops = {e: [] for e in self.ENG}
        self.esem = {e: stack.enter_context(nc.semaphore("E" + e)) for e in self.ENG}
        self.dsem = [stack.enter_context(nc.semaphore("D%d" % i)) for i in range(n_dsem)]
        self.dval = [0] * n_dsem
        self.dnext = 0
        self.waited = {e: {} for e in self.ENG}

    def _need(self, eng, dep, waits, war=False):
        if dep is None:
            return
        if dep[0] == "e":
            _, pe, idx = dep
            if pe == eng and (eng == "pe" or war):
                return
            key = ("e", pe)
            if self.waited[eng].get(key, -1) >= idx:
                return
            self.waited[eng][key] = idx
            self.ops[pe][idx]["sig"] = True
            waits.append(dep)
        else:
            _, j, val = dep
            key = ("d", j)
            if self.waited[eng].get(key, -1) >= val:
                return
            self.waited[eng][key] = val
            waits.append(dep)

    @staticmethod
    def _flat(bufs):
        out = []
        for b in bufs:
            if isinstance(b, (list, tuple)):
                out.extend(Sched._flat(b))
            else:
                out.append(b)
        return out

    def op(self, eng, fn, reads=(), writes=(), dma=False):
        reads = self._flat(reads)
        writes = self._flat(writes)
        waits = []
        for b in reads:
            for d in b.w.values():
                self._need(eng, d, waits)
        for b in writes:
            for d in b.w.values():
                if dma and d[0] == "d" and not b.r:
                    continue
                self._need(eng, d, waits)
            for d in b.r.values():
                self._need(eng, d, waits, war=True)
        idx = len(self.ops[eng])
        rec = dict(fn=fn, waits=waits, sig=False, dma=None)
        if dma:
            j = self.dnext
            self.dnext = (self.dnext + 1) % len(self.dsem)
            if self.dval[j] > 0:
                self._need(eng, ("d", j, self.dval[j]), waits)
            self.dval[j] += 16
            rec["dma"] = j
            ev = ("d", j, self.dval[j])
            key = ("d", j)
        else:
            ev = ("e", eng, idx)
            key = ("e", eng)
        self.ops[eng].append(rec)
        for b in reads:
            b.r[key] = ev
        for b in writes:
            if dma and not b.r:
                b.w = {k: v for k, v in b.w.items() if k[0] == "d"}
                b.w[key] = ev
            else:
                b.w = {key: ev}
            b.r = {}
        return ev

    def barrier(self):
        last = {}
        for e in self.ENG:
            idx = len(self.ops[e]) - 1
            while idx >= 0 and (self.ops[e][idx]["fn"] is None or self.ops[e][idx]["dma"] is not None):
                idx -= 1
            last[e] = idx
        dvals = list(self.dval)
        for e in self.ENG:
            waits = []
            for pe, idx in last.items():
                if pe != e and idx >= 0:
                    self._need(e, ("e", pe, idx), waits)
            for j, v in enumerate(dvals):
                if v > 0:
                    self._need(e, ("d", j, v), waits)
            self.ops[e].append(dict(fn=None, waits=waits, sig=False, dma=None))

    def emit(self, block):
        for e in self.ENG:
            c = 0
            for rec in self.ops[e]:
                if rec["sig"]:
                    c += 1
                rec["sval"] = c
        S = self

        def run(name, eng):
            for rec in S.ops[name]:
                for d in rec["waits"]:
                    if d[0] == "e":
                        eng.wait_ge(S.esem[d[1]], S.ops[d[1]][d[2]]["sval"])
                    else:
                        eng.wait_ge(S.dsem[d[1]], d[2])
                if rec["fn"] is None:
                    continue
                ins = rec["fn"](eng)
                if rec["dma"] is not None:
                    ins.then_inc(S.dsem[rec["dma"]], 16)
                elif rec["sig"]:
                    ins.then_inc(S.esem[name], 1)

        @block.tensor
        def _(e):
            run("pe", e)

        @block.scalar
        def _(e):
            run("act", e)

        @block.vector
        def _(e):
            run("dve", e)

        @block.gpsimd
        def _(e):
            run("pool", e)

        @block.sync
        def _(e):
            run("sp", e)


def _consts():
    c = {}
    c["ident"] = np.eye(128, dtype=np.float32)
    s = np.arange(128)[:, None]
    t = np.arange(128)[None, :]
    c["hmask"] = ((s // 32 == t // 32) & (s <= t)).astype(np.float32)
    c["ctri"] = (s <= t).astype(np.float32)
    r = np.ones((128, GN), np.float32)
    r[:, ::32] = 0.0
    c["reset"] = r
    half = RD // 2
    inv = (np.float32(10000.0) ** (-(np.arange(half, dtype=np.float32) / np.float32(half)))).astype(np.float32)
    pos = np.concatenate([np.arange(T), np.full(NS, PAST)]).astype(np.float32)
    ang = (pos[:, None] * inv[None, :]).astype(np.float32).astype(np.float64)
    cos = np.cos(ang).astype(np.float32)
    sin = np.sin(ang).astype(np.float32)
    c["cosT"] = cos
    c["sinT"] = sin
    c["cosF"] = np.ascontiguousarray(np.concatenate([cos, cos], 1).T)
    c["sinF"] = np.ascontiguousarray(np.concatenate([sin, sin], 1).T)
    sel = np.zeros((NS, NS, 128), np.float32)
    for i in range(NS):
        sel[i, i, :] = 1.0
    c["sel"] = sel.reshape(NS, NS * 128)
    mp = np.zeros((NS, NS, H), np.float32)
    for i in range(NS):
        mp[i, i, :] = 1.0
    c["maskpad"] = np.ascontiguousarray(np.broadcast_to(mp.reshape(1, NS * NS * H), (128, NS * NS * H)))
    nm = np.zeros((NS, H, NS), np.float32)
    for i in range(NS):
        nm[i, :, i] = 1.0
    c["newmask"] = nm.reshape(NS * H, NS)
    c["cmask"] = (np.arange(128)[:, None] // 32 == np.arange(4)[None, :]).astype(np.float32)
    c["pmod"] = (np.arange(128) % 32).astype(np.float32).reshape(128, 1)
    return c


CONST_SHAPES = {k: v.shape for k, v in _consts().items()}


def build(n_pool, groups=(0, 1, 2, 3, "s"), n_layers=4, plan=None):
    nc = bass.Bass("TRN2", target_bir_lowering=False)
    req_log = []
    dram = {}

    def din(name, shape, dt=F32):
        dram[name] = nc.dram_tensor(name, list(shape), dt, kind="ExternalInput").ap()
        return dram[name]

    def dout(name, shape, dt=F32):
        dram[name] = nc.dram_tensor(name, list(shape), dt, kind="ExternalOutput").ap()
        return dram[name]

    xp = din("xp", [T, D])
    xs = din("xs", [NS, D])
    st = din("st", [2 * NS * H * 128, 128])
    ckv = din("ckv", [n_pool * 32, 4 * KVL])
    ckr = din("ckr", [n_pool * 32, 4 * RD])
    ptd = din("ptd", [NS * 16, 4], I32)
    gains = din("gains", [128, 16 * 8])
    w_ffn_in = din("w_ffn_in", [4 * D, 2 * DFF])
    w_ffn_out = din("w_ffn_out", [4 * DFF, D])
    w_in_a = din("w_in_a", [2 * D, 4 * D])
    lbl = din("lbl", [128, 2 * 8])
    gna = din("gna", [128, 2 * 8])
    w_out_a = din("w_out_a", [2 * D, D])
    kvn = din("kvn", [128, 8])
    w_dkv = din("w_dkv", [D, KVL + RD])
    kvan = din("kvan", [128, 2])
    kvan_b = din("kvan_b", [128, KVL])
    w_ukv = din("w_ukv", [KVL, H * 256])
    w_dq = din("w_dq", [2 * D, QL])
    qan = din("qan", [128, 2 * 3])
    w_uq = din("w_uq", [2 * QL, H * 192])
    w_out_b = din("w_out_b", [2 * D, D])
    cd = {k: din("c_" + k, shp) for k, shp in CONST_SHAPES.items()}

    yp = dout("yp", [T, D])
    ys = dout("ys", [NS, D])
    stp = dout("stp", [2 * H * 128, 128])
    cp = dout("cp", [T, KVL])
    krp = dout("krp", [T, RD])
    sts = dout("sts", [2 * NS * H * 128, 128])
    cs = dout("cs", [NS, KVL])
    krs = dout("krs", [NS, RD])

    with contextlib.ExitStack() as stack:
        S = Sched(nc, stack)

        def sb(name, shape, dt=F32):
            return stack.enter_context(nc.sbuf_tensor(name, list(shape), dt))

        def mm(out, lhsT, rhs, start, stop, reads, writes, **kw):
            S.op("pe", lambda e: e.matmul(out, lhsT, rhs, start=start, stop=stop, **kw), reads, writes)

        def tr(out, in_, ident, reads, writes):
            S.op("pe", lambda e: e.transpose(out, in_, ident), reads, writes)

        def act(out, in_, func, reads, writes, scale=1.0, bias=0.0, accum_out=None):
            kw = {}
            if accum_out is not None:
                kw["accum_out"] = accum_out
            S.op("act", lambda e: e.activation(out, in_, func, bias=bias, scale=scale, **kw), reads, writes)

        def tt(eng, out, in0, in1, op, reads, writes):
            S.op(eng, lambda e: e.tensor_tensor(out, in0, in1, op), reads, writes)

        def ts(eng, out, in0, s1, s2, op0, op1, reads, writes):
            S.op(eng, lambda e: e.tensor_scalar(out, in0, s1, s2, op0, op1), reads, writes)

        def stt(out, in0, scalar, in1, op0, op1, reads, writes):
            S.op("dve", lambda e: e.scalar_tensor_tensor(out, in0, scalar, in1, op0, op1), reads, writes)

        def cpy(eng, out, in_, reads, writes):
            if eng == "act":
                S.op("act", lambda e: e.copy(out, in_), reads, writes)
            else:
                S.op(eng, lambda e: e.tensor_copy(out, in_), reads, writes)

        def recip(out, in_, reads, writes):
            S.op("dve", lambda e: e.reciprocal(out, in_), reads, writes)

        def dma(eng, out, in_, reads, writes):
            S.op(eng, lambda e: e.dma_start(out=out, in_=in_), reads, writes, dma=True)

        banks = []
        for i in range(8):
            t_ = stack.enter_context(nc.psum_tensor("ps%d" % i, [128, 512], F32))
            banks.append((t_, Buf("ps%d" % i)))
        rot = {"i": 0, "n": 4}

        def ps_next():
            i = rot["i"]
            rot["i"] = (i + 1) % rot["n"]
            return banks[i]

        CB = Buf("consts")
        ident = sb("ident", [128, 128])
        identb = sb("identb", [128, 128], BF16)
        onesb = sb("onesb", [128, 128], BF16)
        hmask = sb("hmask", [128, 128])
        ctri = sb("ctri", [128, 128], BF16)
        ctri_f = sb("ctri_f", [128, 128])
        reset = sb("reset", [128, GN])
        cmask = sb("cmask", [128, 4])
        cmask4 = sb("cmask4", [128, 4, 128], BF16)
        gains_sb = sb("gains_sb", [128, 16, 8])
        lbl_sb = sb("lbl_sb", [128, 2, 8])
        gna_sb = sb("gna_sb", [128, 2, 8])
        kvn_sb = sb("kvn_sb", [128, 8])
        kvan_sb = sb("kvan_sb", [128, 2])
        kvanb_sb = sb("kvanb_sb", [128, KVL])
        qan_sb = sb("qan_sb", [128, 2, 3])
        lb_sb = sb("lb_sb", [128, 2, 8])
        oml_sb = sb("oml_sb", [128, 2, 8])
        lbtmp = sb("lbtmp", [128, 4, 8])
        for dst, src in ((ident, cd["ident"]), (hmask, cd["hmask"]), (ctri_f, cd["ctri"]), (reset, cd["reset"]), (cmask, cd["cmask"]),
                         (gains_sb, gains.rearrange("p (a k) -> p a k", a=16)),
                         (lbl_sb, lbl.rearrange("p (a k) -> p a k", a=2)),
                         (gna_sb, gna.rearrange("p (a k) -> p a k", a=2)),
                         (kvn_sb, kvn), (kvan_sb, kvan), (kvanb_sb, kvan_b),
                         (qan_sb, qan.rearrange("p (a k) -> p a k", a=2))):
            dma("sp", dst[:], src, [], [CB])
        cpy("dve", identb[:], ident[:], [CB], [CB])
        cpy("dve", ctri[:], ctri_f[:], [CB], [CB])
        S.op("dve", lambda e: e.memset(onesb[:], 1.0), [], [CB])
        cpy("dve", cmask4[:], cmask[:].unsqueeze(2).to_broadcast([128, 4, 128]), [CB], [CB])
        act(lbtmp[:, 0:2, :], lbl_sb[:], AF.Exp, [CB], [CB])
        tt("dve", lbtmp[:, 2, :], lbtmp[:, 0, :], lbtmp[:, 1, :], ALU.add, [CB], [CB])
        recip(lbtmp[:, 3, :], lbtmp[:, 2, :], [CB], [CB])
        tt("dve", lbtmp[:, 0, :], lbtmp[:, 0, :], lbtmp[:, 3, :], ALU.mult, [CB], [CB])
        tt("dve", lbtmp[:, 1, :], lbtmp[:, 1, :], lbtmp[:, 3, :], ALU.mult, [CB], [CB])
        tt("dve", lb_sb[:, 0, :], lbtmp[:, 0, :], lbtmp[:, 0, :], ALU.subtract, [CB], [CB])
        tt("dve", lbtmp[:, 2, :], lbtmp[:, 0, :], lbtmp[:, 1, :], ALU.add, [CB], [CB])
        tt("dve", lb_sb[:, 1, :], lbtmp[:, 2, :], lbtmp[:, 0, :], ALU.subtract, [CB], [CB])
        ts("dve", oml_sb[:], lb_sb[:], -1.0, 1.0, ALU.mult, ALU.add, [CB], [CB])

        WSLOT = 22 * 128
        NWS = 4
        wring = [(sb("wr%d" % i, [128, WSLOT], BF16), Buf("wr%d" % i)) for i in range(NWS)]
        wstate = {"issued": 0, "cur": 0}

        def _issue(i):
            wd, r0, K, c0, pw = plan[i]
            KC = K // 128
            t_, b_ = wring[i % NWS]
            view = t_[:, :KC * pw].rearrange("p (k c) -> p k c", k=KC)
            src = dram[wd][r0:r0 + K, c0:c0 + pw].rearrange("(k p) c -> p k c", p=128)
            dma("pool", view, src, [], [b_])

        def panel(wd, r0, K, c0, pw):
            i = wstate["cur"]
            wstate["cur"] += 1
            req_log.append((wd, r0, K, c0, pw))
            KC = K // 128
            if plan is None:
                t_, b_ = wring[i % NWS]
                view = t_[:, :KC * pw].rearrange("p (k c) -> p k c", k=KC)
                src = dram[wd][r0:r0 + K, c0:c0 + pw].rearrange("(k p) c -> p k c", p=128)
                dma("pool", view, src, [], [b_])
                return view, b_
            assert plan[i] == (wd, r0, K, c0, pw), (i, plan[i], (wd, r0, K, c0, pw))
            while wstate["issued"] < min(i + NWS - 1, len(plan)):
                _issue(wstate["issued"])
                wstate["issued"] += 1
            t_, b_ = wring[i % NWS]
            return t_[:, :KC * pw].rearrange("p (k c) -> p k c", k=KC), b_

        def linear(wd, r0, K, c0, M, rhs_fn, rhs_bufs, N, consume, PW=256):
            KC = K // 128
            mi = 0
            for p0 in range(0, M, PW):
                pw = min(PW, M - p0)
                view, wb = panel(wd, r0, K, c0 + p0, pw)
                for m0 in range(0, pw, 128):
                    mw = min(128, pw - m0)
                    pt_, pb_ = ps_next()
                    for kc in range(KC):
                        rbk = [b[kc] if isinstance(b, list) and len(b) == KC else b for b in rhs_bufs]
                        mm(pt_[:mw, :N], view[:, kc, m0:m0 + mw], rhs_fn(kc), kc == 0, kc == KC - 1,
                           [wb] + rbk, [pb_])
                    consume(mi, mw, pt_, pb_)
                    mi += 1

        NMAX = GN
        class _NS:
            pass
        A = _NS()
        xT = sb("xT", [128, 8, NMAX])
        A.XB = [Buf("xT%d" % i) for i in range(8)]
        hT = sb("hT", [128, 8, NMAX], BF16)
        A.HB = [Buf("hT%d" % i) for i in range(8)]
        mixT = sb("mixT", [128, 8, NMAX])
        A.MB = [Buf("mixT%d" % i) for i in range(8)]
        sqT = sb("sqT", [128, 2, NMAX], BF16)
        SQB = [Buf("sqT0"), Buf("sqT1")]
        rstd = sb("rstd", [128, NMAX])
        RB = Buf("rstd")
        tmpN = sb("tmpN", [128, NMAX])
        TB = Buf("tmpN")
        big = sb("big", [128, 22, NMAX], BF16)
        A.BGB = Buf("big")
        A.xT, A.hT, A.mixT, A.big = xT, hT, mixT, big
        Sst = [sb("Sst%d" % l, [128, H, 128]) for l in range(2)]
        SSB = [[Buf("S%d_%d" % (l, h)) for h in range(H)] for l in range(2)]
        Stmp = sb("Stmp", [128, 128])
        STB = Buf("Stmp")
        Stmp2 = sb("Stmp2", [128, 128])
        STB2 = Buf("Stmp2")
        Sbf = sb("Sbf", [128, 4, 128], BF16)
        SBB = [Buf("Sbf%d" % i) for i in range(4)]
        cT_all = sb("cT_all", [128, 2, T + NS], BF16)
        krT_all = sb("krT_all", [64, T + NS], BF16)
        CKB = [Buf("ck%d" % g) for g in range(NG + 1)]
        wukv = sb("wukv", [128, 2, H * 256], BF16)
        WUB = Buf("wukv")
        cosF = sb("cosF", [64, NMAX])
        sinF = sb("sinF", [64, NMAX])
        CSB = Buf("cossin")

        def rstd_of(src_fn, src_bufs, KC, N, Dn):
            pt_, pb_ = ps_next()
            for kc in range(KC):
                sbk = [b[kc] if isinstance(b, list) and len(b) == 8 else b for b in src_bufs]
                act(sqT[:, kc % 2, :N], src_fn(kc), AF.Square, sbk, [SQB[kc % 2]])
                mm(pt_[:, :N], onesb[:], sqT[:, kc % 2, :N], kc == 0, kc == KC - 1, [SQB[kc % 2], CB], [pb_])
            act(tmpN[:, :N], pt_[:, :N], AF.Ln, [pb_], [TB], scale=1.0 / Dn, bias=EPS)
            act(rstd[:, :N], tmpN[:, :N], AF.Exp, [TB], [RB], scale=-0.5)

        def prenorm(gi, N, gain_tile=None):
            rstd_of(lambda kc: A.xT[:, kc, :N], [A.XB], 8, N, D)
            for kc in range(8):
                g_ = gain_tile[:, kc:kc + 1] if gain_tile is not None else gains_sb[:, gi, kc:kc + 1]
                stt(A.hT[:, kc, :N], A.xT[:, kc, :N], g_, rstd[:, :N], ALU.mult, ALU.mult, [A.XB[kc], RB, CB], [A.HB[kc]])

        def postnorm_add(gi, N):
            rstd_of(lambda kc: A.mixT[:, kc, :N], [A.MB], 8, N, D)
            for kc in range(8):
                stt(A.mixT[:, kc, :N], A.mixT[:, kc, :N], gains_sb[:, gi, kc:kc + 1], rstd[:, :N], ALU.mult, ALU.mult,
                    [A.MB[kc], RB, CB], [A.MB[kc]])
                tt("dve", A.xT[:, kc, :N], A.xT[:, kc, :N], A.mixT[:, kc, :N], ALU.add, [A.XB[kc], A.MB[kc]], [A.XB[kc]])

        def to_mix(mi, mw, pt_, pb_, N):
            cpy("act", A.mixT[:, mi, :N], pt_[:, :N], [pb_], [A.MB[mi]])

        def ffn(l, N):
            prenorm(4 * l + 2, N)
            gtmp = sb_ffn_g
            for j in range(DFF // 256):
                vg, bg = panel("w_ffn_in", l * D, D, j * 256, 256)
                vu, bu = panel("w_ffn_in", l * D, D, DFF + j * 256, 256)
                for m in range(2):
                    pg_, pgb = ps_next()
                    for kc in range(8):
                        mm(pg_[:, :N], vg[:, kc, m * 128:(m + 1) * 128], A.hT[:, kc, :N], kc == 0, kc == 7, [bg, A.HB[kc]], [pgb])
                    pu_, pub = ps_next()
                    for kc in range(8):
                        mm(pu_[:, :N], vu[:, kc, m * 128:(m + 1) * 128], A.hT[:, kc, :N], kc == 0, kc == 7, [bu, A.HB[kc]], [pub])
                    act(gtmp[:, :N], pg_[:, :N], AF.Silu, [pgb], [GTB])
                    tt("dve", A.big[:, 2 * j + m, :N], gtmp[:, :N], pu_[:, :N], ALU.mult, [GTB, pub], [A.BGB])
            linear("w_ffn_out", l * DFF, DFF, 0, D, lambda kc: A.big[:, kc, :N], [A.BGB], N,
                   lambda mi, mw, p_, b_: to_mix(mi, mw, p_, b_, N), PW=128)
            postnorm_add(4 * l + 3, N)

        sb_ffn_g = sb("gtmp", [128, NMAX])
        GTB = Buf("gtmp")

        hq = sb("hq", [128, NMAX])
        hf = sb("hf", [128, NMAX])
        hlf = sb("hlf", [128, NMAX])
        hb = sb("hb", [128, NMAX])
        hk = sb("hk", [128, NMAX])
        heb = sb("heb", [128, NMAX])
        hr = sb("hr", [128, NMAX])
        hbl = sb("hbl", [128, NMAX // 32])
        hgate = sb("hgate", [128, NMAX], BF16)
        hqd = sb("hqd", [128, NMAX], BF16)
        hkd = sb("hkd", [128, NMAX], BF16)
        hkd2 = sb("hkd2", [128, NMAX], BF16)
        hkd2T = sb("hkd2T", [128, 4, 128], BF16)
        hv = sb("hv", [128, 4, 128], BF16)
        hA = sb("hA", [128, 4, 128], BF16)
        hvm = sb("hvm", [128, 4, 128], BF16)
        HVM = [Buf("hvm%d" % i) for i in range(4)]
        hos = sb("hos", [128, NMAX])
        HQ, HF, HLF, HK, HEB, HR, HBL, HGT, HQD, HKD, HKD2, HKT, HV, HA, HOS = [Buf(n) for n in
            "hq hf hlf hk heb hr hbl hgate hqd hkd hkd2 hkd2T hv hA hos".split()]
        HBB = Buf("hb")

        def hgrn_post(l, h, po, pob, N, gate=None, gate_bufs=None):
            if gate is None:
                gate, gate_bufs = hgate[:, :N], [HGT]
                po = po[:, :N]
            cpy("act", hos[:, :N], po, [pob], [HOS])
            act(sqT[:, 0, :N], po, AF.Square, [pob], [SQB[0]])
            ps2, ps2b = ps_next()
            mm(ps2[:, :N], onesb[:], sqT[:, 0, :N], True, True, [SQB[0], CB], [ps2b])
            act(tmpN[:, :N], ps2[:, :N], AF.Ln, [ps2b], [TB], scale=1.0 / 128, bias=EPS)
            act(rstd[:, :N], tmpN[:, :N], AF.Exp, [TB], [RB], scale=-0.5)
            stt(hos[:, :N], hos[:, :N], gna_sb[:, l, h:h + 1], rstd[:, :N], ALU.mult, ALU.mult, [HOS, RB, CB], [HOS])
            tt("dve", A.big[:, h, :N], hos[:, :N], gate, ALU.mult, [HOS] + gate_bufs, [A.BGB])

        hqd2 = [hqd, sb("hqd_b", [128, NMAX], BF16)]
        hkd_2 = [hkd, sb("hkd_b", [128, NMAX], BF16)]
        hkd2T2 = [hkd2T, sb("hkd2T_b", [128, 4, 128], BF16)]
        hv2 = [hv, sb("hv_b", [128, 4, 128], BF16)]
        hA2 = [hA, sb("hA_b", [128, 4, 128], BF16)]
        hgate2 = [hgate, sb("hgate_b", [128, NMAX], BF16)]
        hbl2 = [hbl, sb("hbl_b", [128, NMAX // 32])]
        HQD2, HKD_2, HKT2, HV2, HA2, HGT2, HBL2 = [[Buf(n + "0"), Buf(n + "1")] for n in "hqd hkd hkt hv hA hgt hbl".split()]

        def hgrn_prompt(l, g):
            N = GN
            prenorm(4 * l + 0, N)

            def stageA1p(h, bs):
                for _ in stageA1(h, bs, 0):
                    pass

            def stageA1g(h, bs):
                return stageA1(h, bs, 1)

            def stageA1(h, bs, part):
                if part == 1:
                    yield from stageA1_gate(h, bs)
                    return
                pan = lambda sec: panel("w_in_a", l * D, D, sec * D + h * 128, 128)
                vq, bq = pan(0)
                pq, pqb = ps_next()
                for kc in range(8):
                    mm(pq[:, :N], vq[:, kc, :], A.hT[:, kc, :N], kc == 0, kc == 7, [bq, A.HB[kc]], [pqb])
                act(hq[:, :N], pq[:, :N], AF.Silu, [pqb], [HQ])
                yield
                vg, bg = pan(3)
                pg, pgb = ps_next()
                for kc in range(8):
                    mm(pg[:, :N], vg[:, kc, :], A.hT[:, kc, :N], kc == 0, kc == 7, [bg, A.HB[kc]], [pgb])
                act(hgate2[bs][:, :N], pg[:, :N], AF.Silu, [pgb], [HGT2[bs]])
                yield
                vf, bf_ = pan(1)
                pf, pfb = ps_next()
                for kc in range(8):
                    mm(pf[:, :N], vf[:, kc, :], A.hT[:, kc, :N], kc == 0, kc == 7, [bf_, A.HB[kc]], [pfb])
                act(hf[:, :N], pf[:, :N], AF.Sigmoid, [pfb], [HF])
                yield
                vi, bi = pan(2)
                pv, pvb = ps_next()
                for tt_ in range(4):
                    for kc in range(8):
                        mm(pv[:, tt_ * 128:(tt_ + 1) * 128], A.hT[:, kc, tt_ * 128:(tt_ + 1) * 128], vi[:, kc, :],
                           kc == 0, kc == 7, [bi, A.HB[kc]], [pvb])
                cpy("act", hv2[bs][:].rearrange("p a b -> p (a b)"), pv[:, :], [pvb], [HV2[bs]])
                yield

            def stageA1_gate(h, bs):
                ts("dve", hf[:, :N], hf[:, :N], oml_sb[:, l, h:h + 1], lb_sb[:, l, h:h + 1], ALU.mult, ALU.add, [HF, CB], [HF])
                ts("dve", hk[:, :N], hf[:, :N], -1.0, 1.0, ALU.mult, ALU.add, [HF], [HK])
                act(hlf[:, :N], hf[:, :N], AF.Ln, [HF], [HLF])
                yield
                S.op("dve", lambda e: e.tensor_tensor_scan(hb[:, :N], reset[:, :N], hlf[:, :N], 0.0, ALU.mult, ALU.add),
                     [HLF, CB], [HBB])
                b3 = hb[:, :N].rearrange("p (c t) -> p c t", t=32)
                cpy("dve", hbl2[bs][:, :N // 32], b3[:, :, 31], [HBB], [HBL2[bs]])
                act(heb[:, :N], hb[:, :N], AF.Exp, [HBB], [HEB])
                yield
                stt(hqd2[bs][:, :N], hq[:, :N], 128 ** -0.5, heb[:, :N], ALU.mult, ALU.mult, [HQ, HEB], [HQD2[bs]])
                act(heb[:, :N], hb[:, :N], AF.Exp, [HBB], [HEB], scale=-1.0)
                yield
                tt("dve", hkd_2[bs][:, :N], hk[:, :N], heb[:, :N], ALU.mult, [HK, HEB], [HKD_2[bs]])
                tt("dve", hr[:, :N].rearrange("p (c t) -> p c t", t=32),
                   hbl2[bs][:, :N // 32].unsqueeze(2).to_broadcast([128, N // 32, 32]), b3, ALU.subtract, [HBL2[bs], HBB], [HR])
                act(hr[:, :N], hr[:, :N], AF.Exp, [HR], [HR])
                yield
                tt("dve", hkd2[:, :N], hk[:, :N], hr[:, :N], ALU.mult, [HK, HR], [HKD2])
                act(hbl2[bs][:, :N // 32], hbl2[bs][:, :N // 32], AF.Exp, [HBL2[bs]], [HBL2[bs]])
                yield

            def stageA2(h, bs):
                ptb, ptbb = ps_next()
                ptv = ptb[:].bitcast(BF16)
                for tt_ in range(4):
                    tr(ptv[:, tt_ * 128:(tt_ + 1) * 128], hkd2[:, tt_ * 128:(tt_ + 1) * 128], identb[:], [HKD2, CB], [ptbb])
                cpy("act", hkd2T2[bs][:].rearrange("p a b -> p (a b)"), ptv[:, 0:512], [ptbb], [HKT2[bs]])
                psc, pscb = ps_next()
                for tt_ in range(4):
                    sl = slice(tt_ * 128, (tt_ + 1) * 128)
                    mm(psc[:, sl], hkd_2[bs][:, sl], hqd2[bs][:, sl], True, True, [HKD_2[bs], HQD2[bs]], [pscb])
                tt("dve", hA2[bs][:], psc[:].rearrange("p (a b) -> p a b", a=4),
                   hmask[:].unsqueeze(1).to_broadcast([128, 4, 128]), ALU.mult, [pscb, CB], [HA2[bs]])

            def stageB(h, bs):
                po, pob = banks[4 + (h % 2)]
                Srot = [(Sst[l][:, h, :], SSB[l][h]), (Stmp[:], STB), (Stmp2[:], STB2)]
                for tt_ in range(4):
                    sl = slice(tt_ * 128, (tt_ + 1) * 128)
                    pd, pdb = banks[6 + (tt_ % 2)]
                    tt("dve", hvm[:], hv2[bs][:, tt_, :].unsqueeze(1).to_broadcast([128, 4, 128]), cmask4[:], ALU.mult,
                       [HV2[bs], CB], [HVM[0]])
                    for c in range(4):
                        mm(pd[:, c * 128:(c + 1) * 128], hkd2T2[bs][:, tt_, :], hvm[:, c, :],
                           True, True, [HKT2[bs], HVM[0]], [pdb])
                    mm(po[:, sl], hv2[bs][:, tt_, :], hA2[bs][:, tt_, :], True, False, [HV2[bs], HA2[bs]], [pob])
                    for c in range(4):
                        ci = tt_ * 4 + c
                        Sc = Srot[ci % 3]
                        Sx = Srot[(ci + 1) % 3]
                        cpy("pool", Sbf[:, c, :], Sc[0], [Sc[1]], [SBB[c]])
                        mm(po[:, ci * 32:(ci + 1) * 32], Sbf[:, c, :], hqd2[bs][:, ci * 32:(ci + 1) * 32], False, c == 3,
                           [SBB[c], HQD2[bs]], [pob], skip_group_check=True)
                        stt(Sx[0], Sc[0], hbl2[bs][:, ci:ci + 1], pd[:, c * 128:(c + 1) * 128],
                            ALU.mult, ALU.add, [Sc[1], HBL2[bs], pdb], [Sx[1]])
                        if c % 2 == 1:
                            yield
                cpy("dve", Srot[0][0], Srot[1][0], [Srot[1][1]], [Srot[0][1]])
                hgrn_post(l, h, po[:, :N], pob, N, gate=hgate2[bs][:, :N], gate_bufs=[HGT2[bs]])
                if g == NG - 1:
                    dma("sp", stp.rearrange("(l h k) v -> l h k v", l=2, h=H)[l, h], Sst[l][:, h, :], [SSB[l][h]], [OUTB])
                yield

            stageA1p(0, 0)
            for _ in stageA1g(0, 0):
                pass
            stageA2(0, 0)
            import itertools
            for h in range(H):
                gb = stageB(h, h % 2)
                ga = (itertools.chain(stageA1(h + 1, (h + 1) % 2, 0), stageA1g(h + 1, (h + 1) % 2))
                      if h + 1 < H else iter(()))
                doneb = donea = False
                while not (doneb and donea):
                    if not donea:
                        try:
                            next(ga)
                        except StopIteration:
                            donea = True
                    if not doneb:
                        try:
                            next(gb)
                        except StopIteration:
                            doneb = True
                if h + 1 < H:
                    stageA2(h + 1, (h + 1) % 2)
            linear("w_out_a", l * D, D, 0, D, lambda kc: A.big[:, kc, :N], [A.BGB], N,
                   lambda mi, mw, p_, b_: to_mix(mi, mw, p_, b_, N))
            postnorm_add(4 * l + 1, N)

        OUTB = Buf("out")


        qaT = sb("qaT", [128, 3, NMAX], BF16)
        QAB = Buf("qaT")
        qnT = sb("qnT", [128, NMAX], BF16)
        QNB = Buf("qnT")
        qrT = sb("qrT", [64, NMAX], BF16)
        QRB = Buf("qrT")
        qnT_b = sb("qnT_b", [128, NMAX], BF16)
        qrT_b = sb("qrT_b", [64, NMAX], BF16)
        qn2 = [qnT, qnT_b]
        qr2 = [qrT, qrT_b]
        QNB2 = [QNB, Buf("qnT_b")]
        QRB2 = [QRB, Buf("qrT_b")]
        wrot = sb("wrot", [128, 8, 64], BF16)
        WRB = Buf("wrot")
        knT = sb("knT", [128, T], BF16)
        KNB = Buf("knT")
        vh = sb("vh", [128, T // 128, 128], BF16)
        VHB = Buf("vh")
        pexp = sb("pexp", [128, 2, NMAX], BF16)
        PXB = [Buf("pexp0"), Buf("pexp1")]
        rt1 = sb("rt1", [64, NMAX])
        rt2 = sb("rt2", [64, NMAX])
        RT1, RT2 = Buf("rt1"), Buf("rt2")
        ctm = sb("ctm", [128, KVL + RD])
        CTM = Buf("ctm")
        cst = sb("cst", [128, 2, 32])
        CST = Buf("cst")
        ssq = sb("ssq", [128, 4])
        SSQ = Buf("ssq")
        junk = sb("junk", [128, KVL])
        JNK = Buf("junk")

        def rope_fm(dst, pr, prb, prot, protb, N, dbufs):
            tt("dve", rt1[:, :N], pr[:64, :N], cosF[:, :N], ALU.mult, [prb, CSB], [RT1])
            tt("dve", rt2[:, :N], prot[:64, :N], sinF[:, :N], ALU.mult, [protb, CSB], [RT2])
            tt("dve", dst, rt1[:, :N], rt2[:, :N], ALU.add, [RT1, RT2], dbufs)

        def make_wrot(view, vb, KC, c0):
            ts("dve", wrot[:, :KC, 0:32], view[:, :, c0 + 32:c0 + 64], -1.0, None, ALU.mult, ALU.bypass, [vb], [WRB])
            cpy("dve", wrot[:, :KC, 32:64], view[:, :, c0:c0 + 32], [vb], [WRB])

        def mla_shared(N, col0, pos0, cp_ap, krp_ap, ckb):
            prenorm(None, N, gain_tile=kvn_sb)
            view, vb = panel("w_dkv", 0, D, 0, KVL + RD)
            make_wrot(view, vb, 8, KVL)
            for mc in range(2):
                pt_, pb_ = ps_next()
                for kc in range(8):
                    mm(pt_[:, :N], view[:, kc, mc * 128:(mc + 1) * 128], A.hT[:, kc, :N], kc == 0, kc == 7, [vb, A.HB[kc]], [pb_])
                cpy("act", A.mixT[:, mc, :N], pt_[:, :N], [pb_], [A.MB])
            pr, prb = ps_next()
            for kc in range(8):
                mm(pr[:64, :N], view[:, kc, KVL:KVL + RD], A.hT[:, kc, :N], kc == 0, kc == 7, [vb, A.HB[kc]], [prb])
            prot, protb = ps_next()
            for kc in range(8):
                mm(prot[:64, :N], wrot[:, kc, :], A.hT[:, kc, :N], kc == 0, kc == 7, [WRB, A.HB[kc]], [protb])
            rope_fm(krT_all[:, col0:col0 + N], pr, prb, prot, protb, N, [ckb])
            rstd_of(lambda kc: A.mixT[:, kc, :N], [A.MB], 2, N, KVL)
            for mc in range(2):
                stt(cT_all[:, mc, col0:col0 + N], A.mixT[:, mc, :N], kvan_sb[:, mc:mc + 1], rstd[:, :N], ALU.mult, ALU.mult,
                    [A.MB, RB, CB], [ckb])
            for t0 in range(0, N, 128):
                rows = min(128, N - t0)
                pt_, pb_ = ps_next()
                for kc in range(8):
                    mm(pt_[:rows, :KVL + RD], A.hT[:, kc, t0:t0 + rows], view[:, kc, :], kc == 0, kc == 7, [vb, A.HB[kc]], [pb_])
                act(junk[:rows, :], pt_[:rows, :KVL], AF.Square, [pb_], [JNK, SSQ], accum_out=ssq[:rows, 0:1])
                act(ssq[:rows, 1:2], ssq[:rows, 0:1], AF.Sqrt, [SSQ], [SSQ], scale=1.0 / KVL, bias=EPS)
                recip(ssq[:rows, 2:3], ssq[:rows, 1:2], [SSQ], [SSQ])
                stt(ctm[:rows, :KVL], pt_[:rows, :KVL], ssq[:rows, 2:3], kvanb_sb[:rows, :], ALU.mult, ALU.mult, [pb_, SSQ, CB], [CTM])
                dma("sp", cst[:rows, 0, :], cd["cosT"][pos0 + t0:pos0 + t0 + rows, :], [], [CST])
                dma("sp", cst[:rows, 1, :], cd["sinT"][pos0 + t0:pos0 + t0 + rows, :], [], [CST])
                x1 = pt_[:rows, KVL:KVL + 32]
                x2 = pt_[:rows, KVL + 32:KVL + 64]
                o1 = ctm[:rows, KVL:KVL + 32]
                o2 = ctm[:rows, KVL + 32:KVL + 64]
                tt("dve", junk[:rows, 0:32], x2, cst[:rows, 1, :], ALU.mult, [pb_, CST], [JNK])
                tt("dve", o1, x1, cst[:rows, 0, :], ALU.mult, [pb_, CST], [CTM])
                tt("dve", o1, o1, junk[:rows, 0:32], ALU.subtract, [CTM, JNK], [CTM])
                tt("dve", junk[:rows, 32:64], x1, cst[:rows, 1, :], ALU.mult, [pb_, CST], [JNK])
                tt("dve", o2, x2, cst[:rows, 0, :], ALU.mult, [pb_, CST], [CTM])
                tt("dve", o2, o2, junk[:rows, 32:64], ALU.add, [CTM, JNK], [CTM])
                dma("sp", cp_ap[t0:t0 + rows, :], ctm[:rows, :KVL], [CTM], [OUTB])
                dma("sp", krp_ap[t0:t0 + rows, :], ctm[:rows, KVL:KVL + RD], [CTM], [OUTB])

        def mla_q(l, N):
            j = l - 2
            prenorm(4 * l + 0, N)
            linear("w_dq", j * D, D, 0, QL, lambda kc: A.hT[:, kc, :N], [A.HB], N,
                   lambda mi, mw, p_, b_: to_mix(mi, mw, p_, b_, N), PW=128)
            rstd_of(lambda kc: A.mixT[:, kc, :N], [A.MB], 3, N, QL)
            for mc in range(3):
                stt(qaT[:, mc, :N], A.mixT[:, mc, :N], qan_sb[:, j, mc:mc + 1], rstd[:, :N], ALU.mult, ALU.mult, [A.MB, RB, CB], [QAB])

        def mla_q_head(l, h, N, bs=0):
            j = l - 2
            qn_, qr_, qnb_, qrb_ = qn2[bs], qr2[bs], QNB2[bs], QRB2[bs]
            vw, vb = panel("w_uq", j * QL, QL, h * 192, 192)
            make_wrot(vw, vb, 3, 128)
            pq, pqb = ps_next()
            for kc in range(3):
                mm(pq[:, :N], vw[:, kc, 0:128], qaT[:, kc, :N], kc == 0, kc == 2, [vb, QAB], [pqb])
            cpy("act", qn_[:, :N], pq[:, :N], [pqb], [qnb_])
            pr, prb = ps_next()
            for kc in range(3):
                mm(pr[:64, :N], vw[:, kc, 128:192], qaT[:, kc, :N], kc == 0, kc == 2, [vb, QAB], [prb])
            prot, protb = ps_next()
            for kc in range(3):
                mm(prot[:64, :N], wrot[:, kc, :], qaT[:, kc, :N], kc == 0, kc == 2, [WRB, QAB], [protb])
            rope_fm(qr_[:, :N], pr, prb, prot, protb, N, [qrb_])

        def mla_prompt(l, g):
            N = GN
            j = l - 2
            mla_q(l, N)
            mla_q_head(l, 0, N, 0)
            for h in range(H):
                qn_, qr_, qnb_, qrb_ = qn2[h % 2], qr2[h % 2], QNB2[h % 2], QRB2[h % 2]
                for kb in range(g + 1):
                    pk, pkb = ps_next()
                    for cc in range(2):
                        mm(pk[:, :], wukv[:, cc, h * 256:h * 256 + 128], cT_all[:, cc, kb * 512:(kb + 1) * 512], cc == 0, cc == 1,
                           [WUB, CKB[kb]], [pkb])
                    cpy("act", knT[:, kb * 512:(kb + 1) * 512], pk[:, :], [pkb], [KNB])
                    pv, pvb = ps_next()
                    for tt_ in range(4):
                        for cc in range(2):
                            mm(pv[:, tt_ * 128:(tt_ + 1) * 128], cT_all[:, cc, kb * 512 + tt_ * 128:kb * 512 + (tt_ + 1) * 128],
                               wukv[:, cc, h * 256 + 128:h * 256 + 256], cc == 0, cc == 1, [WUB, CKB[kb]], [pvb])
                    cpy("act", vh[:, kb * 4:(kb + 1) * 4, :].rearrange("p a b -> p (a b)"), pv[:, :], [pvb], [VHB])
                if h + 1 < H:
                    mla_q_head(l, h + 1, N, (h + 1) % 2)
                po, pob = banks[4 + (h % 2)]
                pden, pdenb = banks[6 + (h % 2)]
                ntile = 4 * g + 4
                def S_(i):
                    r = i - 4 * g
                    q0 = 128 * r if r > 0 else 0
                    ps_, psb = ps_next()
                    mm(ps_[:, q0:N], knT[:, i * 128:(i + 1) * 128], qn_[:, q0:N], True, False, [KNB, qnb_], [psb])
                    mm(ps_[:, q0:N], krT_all[:, i * 128:(i + 1) * 128], qr_[:, q0:N], False, True, [CKB[i // 4], qrb_], [psb])
                    px = pexp[:, i % 2, :]
                    pxb = PXB[i % 2]
                    act(px[:, q0:N], ps_[:, q0:N], AF.Exp, [psb], [pxb], scale=SCALE)
                    if r >= 0:
                        tt("dve", px[:, q0:q0 + 128], px[:, q0:q0 + 128], ctri[:], ALU.mult, [pxb, CB], [pxb])

                def V_(i):
                    r = i - 4 * g
                    q0 = 128 * r if r > 0 else 0
                    px = pexp[:, i % 2, :]
                    pxb = PXB[i % 2]
                    mm(po[:, q0:N], vh[:, i, :], px[:, q0:N], i == 0, i == ntile - 1, [VHB, pxb], [pob], skip_group_check=True)
                    mm(pden[:, q0:N], onesb[:], px[:, q0:N], i == 0, i == ntile - 1, [CB, pxb], [pdenb], skip_group_check=True)

                S_(0)
                for i in range(ntile):
                    if i + 1 < ntile:
                        S_(i + 1)
                    V_(i)
                act(tmpN[:, :N], pden[:, :N], AF.Ln, [pdenb], [TB])
                act(rstd[:, :N], tmpN[:, :N], AF.Exp, [TB], [RB], scale=-1.0)
                tt("dve", A.big[:, h, :N], po[:, :N], rstd[:, :N], ALU.mult, [pob, RB], [A.BGB])
            linear("w_out_b", j * D, D, 0, D, lambda kc: A.big[:, kc, :N], [A.BGB], N,
                   lambda mi, mw, p_, b_: to_mix(mi, mw, p_, b_, N))
            postnorm_add(4 * l + 1, N)

        xin = sb("xin", [128, D])
        XIN = Buf("xin")

        def load_x(src_ap, rows, col0):
            dma("sp", xin[:rows, :], src_ap, [], [XIN])
            for half in range(2):
                pt_, pb_ = ps_next()
                for j in range(4):
                    kc = half * 4 + j
                    tr(pt_[:, j * 128:j * 128 + rows], xin[:rows, kc * 128:(kc + 1) * 128], ident[:rows, :rows], [XIN, CB], [pb_])
                for j in range(4):
                    kc = half * 4 + j
                    cpy("act", A.xT[:, kc, col0:col0 + rows], pt_[:, j * 128:j * 128 + rows], [pb_], [A.XB[kc]])

        def store_x(dst_ap, rows, col0):
            for half in range(2):
                pt_, pb_ = ps_next()
                for j in range(4):
                    kc = half * 4 + j
                    tr(pt_[:rows, j * 128:(j + 1) * 128], A.xT[:, kc, col0:col0 + rows], ident[:], [A.XB[kc], CB], [pb_])
                cpy("act", xin[:rows, half * 512:(half + 1) * 512], pt_[:rows, :], [pb_], [XIN])
            dma("sp", dst_ap, xin[:rows, :], [XIN], [OUTB])


        _pools = [[big[:].rearrange("p a b -> p (a b)"), 0, 22 * NMAX], [mixT[:].rearrange("p a b -> p (a b)").bitcast(BF16), 0, 16 * NMAX],
                  [xT[:].rearrange("p a b -> p (a b)").bitcast(BF16), 0, 16 * NMAX], [hT[:].rearrange("p a b -> p (a b)"), 0, 8 * NMAX]]

        def carve(shape, dt=F32):
            esz = 4 if dt in (F32, I32) else 2
            n = int(np.prod(shape[1:])) * esz // 2
            n = (n + 15) // 16 * 16
            for pl in _pools:
                if pl[1] + n <= pl[2]:
                    v = pl[0][:, pl[1]:pl[1] + n]
                    pl[1] += n
                    if esz == 4:
                        v = v.bitcast(dt)
                    v = v[:shape[0], :int(np.prod(shape[1:]))]
                    if len(shape) == 3:
                        v = v.rearrange("p (a b) -> p a b", a=shape[1])
                    elif len(shape) == 4:
                        v = v.rearrange("p (a b c) -> p a b c", a=shape[1], b=shape[2])
                    return v
            raise RuntimeError("carve: out of space " + str(shape))

        NST = 3
        stile = [(carve([128, H, 128]), Buf("stile%d" % i)) for i in range(NST)]
        sel32 = carve([NS, NS * 128])
        vs32 = carve([NS, D])
        for pl in _pools:
            pl[1] = 0
        NCT = 12
        ctile = [(carve([128, 4, KVL], BF16), carve([128, 4, RD], BF16), Buf("ct%d" % i)) for i in range(NCT)]
        qpad = carve([128, 2, NS, 128], BF16)
        rpad = carve([64, NS, 128], BF16)
        wukT = carve([128, H, KVL], BF16)
        cTs = carve([128, 2, 1024], BF16)
        rTs = sb("rTs", [64, 2, 512], BF16)
        pex2 = carve([128, 4, 512], BF16)
        snb = sb("snb", [128, H, 128], BF16)
        snb_b = carve([128, H, 128], BF16)
        snb2 = [snb, snb_b]
        SNB2 = [Buf("snb0"), Buf("snb1")]
        xTs = sb("xTs", [128, 8, NS])
        hTs = sb("hTs", [128, 8, NS], BF16)
        mixTs = sb("mixTs", [128, 8, NS])
        bigs = sb("bigs", [128, 22, NS], BF16)
        sq32 = sb("sq32", [128, H, NS])
        sf32 = sb("sf32", [128, H, NS])
        sk32 = sb("sk32", [128, H, NS])
        sgate = sb("sgate", [128, H, NS], BF16)
        sqb = sb("sqb", [128, H, NS], BF16)
        SQ32, SF32, SK32, SGT, SQBB = [Buf(n) for n in "sq32 sf32 sk32 sgate sqb".split()]
        VS32 = Buf("vs32")
        SNB = Buf("snb")
        stmp = sb("stmp", [128, 2, 128])
        STM = [Buf("stmp0"), Buf("stmp1")]

        def hgrn_sample(l):
            N = NS
            prenorm(4 * l + 0, N)
            for sec, fn, dst, db in ((0, AF.Silu, sq32, SQ32), (1, AF.Sigmoid, sf32, SF32), (3, AF.Silu, sgate, SGT)):
                linear("w_in_a", l * D, D, sec * D, D, lambda kc: A.hT[:, kc, :N], [A.HB], N,
                       (lambda fn, dst, db: (lambda mi, mw, p_, b_: act(dst[:, mi, :], p_[:, :N], fn, [b_], [db])))(fn, dst, db))
            for q4 in range(4):
                vw, vb = panel("w_in_a", l * D, D, 2 * D + q4 * 256, 256)
                pt_, pb_ = ps_next()
                for kc in range(8):
                    mm(pt_[:N, :256], A.hT[:, kc, :N], vw[:, kc, :], kc == 0, kc == 7, [vb, A.HB[kc]], [pb_])
                cpy("act", vs32[:, q4 * 256:(q4 + 1) * 256], pt_[:N, :256], [pb_], [VS32])
            lbb = lb_sb[:, l, :].unsqueeze(2).to_broadcast([128, H, NS])
            omb = oml_sb[:, l, :].unsqueeze(2).to_broadcast([128, H, NS])
            tt("dve", sf32[:], sf32[:], omb, ALU.mult, [SF32, CB], [SF32])
            tt("dve", sf32[:], sf32[:], lbb, ALU.add, [SF32, CB], [SF32])
            ts("dve", sk32[:], sf32[:], -1.0, 1.0, ALU.mult, ALU.add, [SF32], [SK32])
            ts("dve", sqb[:], sq32[:], 128 ** -0.5, None, ALU.mult, ALU.bypass, [SQ32], [SQBB])
            st_in = st.rearrange("(l s h k) v -> l s k h v", l=2, s=NS, h=H)
            st_out = sts.rearrange("(l s h k) v -> l s k h v", l=2, s=NS, h=H)
            po, pob = banks[4]
            vbanks = {}

            def vb_stage(s_):
                stl, stb = stile[s_ % NST]
                dma("sp", stl[:], st_in[l, s_], [], [stb])
                pair = []
                for half in range(2):
                    pvb_, pvbb = banks[(s_ % 2) * 2 + half]
                    mm(pvb_[:, :], sel32[:, s_ * 128:(s_ + 1) * 128], vs32[:, half * 512:(half + 1) * 512], True, True,
                       [CB, VS32], [pvbb])
                    pair.append((pvb_, pvbb))
                vbanks[s_] = pair

            def upd_stage(s_):
                stl, stb = stile[s_ % NST]
                for half in range(2):
                    pvb_, pvbb = vbanks[s_][half]
                    for hh in range(4):
                        h = half * 4 + hh
                        tb = (h % 2)
                        act(stmp[:, tb, :], pvb_[:, hh * 128:(hh + 1) * 128], AF.Copy, [pvbb, SK32], [STM[tb]],
                            scale=sk32[:, h, s_:s_ + 1])
                        stt(stl[:, h, :], stl[:, h, :], sf32[:, h, s_:s_ + 1], stmp[:, tb, :], ALU.mult, ALU.add,
                            [stb, SF32, STM[tb]], [stb])
                sn = snb2[s_ % 2]
                cpy("act", sn[:].rearrange("p a b -> p (a b)"), stl[:].rearrange("p a b -> p (a b)"), [stb], [SNB2[s_ % 2]])
                dma("sp", st_out[l, s_], stl[:], [stb], [OUTB])

            def o_stage(s_):
                sn = snb2[s_ % 2]
                for h in range(H):
                    mm(po[:, h * NS + s_:h * NS + s_ + 1], sn[:, h, :], sqb[:, h, s_:s_ + 1], True, True, [SNB2[s_ % 2], SQBB], [pob])

            vb_stage(0)
            for s_ in range(NS):
                if s_ + 1 < NS:
                    vb_stage(s_ + 1)
                upd_stage(s_)
                if s_ >= 1:
                    o_stage(s_ - 1)
            o_stage(NS - 1)
            for h in range(H):
                hgrn_post(l, h, po[:, h * NS:(h + 1) * NS], pob, N, gate=sgate[:, h, :], gate_bufs=[SGT])
            linear("w_out_a", l * D, D, 0, D, lambda kc: A.big[:, kc, :N], [A.BGB], N,
                   lambda mi, mw, p_, b_: to_mix(mi, mw, p_, b_, N))
            postnorm_add(4 * l + 1, N)

        WKT = Buf("wukT")
        qlat = sb("qlat", [128, 2, NS * H], BF16)
        QLB = Buf("qlat")
        qrall = sb("qrall", [64, NS * H], BF16)
        QRA = Buf("qrall")
        QPB = Buf("qpad")
        CTS = [Buf("cTs0"), Buf("cTs1")]
        RTS = [Buf("rTs0"), Buf("rTs1")]
        PX2 = [Buf("pex2_%d" % i) for i in range(4)]
        pTs = sb("pTs", [128, 4, 128], BF16)
        PTS = Buf("pTs")
        idx32 = sb("idx32", [128, 2 * 128], I32)
        IDXB = Buf("idx")
        ptd_sb = sb("ptd_sb", [128, 2, 4], I32)
        ptd_f = sb("ptd_f", [128, 2, 128])
        pmod = sb("pmod", [128, 1])
        newmask = sb("newmask", [128, NS])
        pnew = sb("pnew", [128, NS])
        pnewT = sb("pnewT", [NS, 128], BF16)
        ctmb = sb("ctmb", [NS, KVL], BF16)
        PNB = Buf("pnew")
        olat = sb("olat", [128, 2, NS * H], BF16)
        OLB = Buf("olat")

        def decode_setup_h():
            dma("sp", sel32[:], cd["sel"], [], [CB])

        def decode_setup():
            dma("sp", pmod[:], cd["pmod"], [], [CB])
            dma("sp", newmask[:], cd["newmask"], [], [CB])
            for h in range(H):
                pt_, pb_ = ps_next()
                pv_ = pt_[:].bitcast(BF16)
                for cc in range(2):
                    tr(pv_[:, cc * 128:(cc + 1) * 128], wukv[:, cc, h * 256:h * 256 + 128], identb[:], [WUB, CB], [pb_])
                cpy("act", wukT[:, h, :], pv_[:, 0:256], [pb_], [WKT])
            for hf_ in range(2):
                dma("sp", ptd_sb[:, hf_, :], ptd[hf_ * 128:(hf_ + 1) * 128, :], [], [IDXB])
            for hf_ in range(2):
                cpy("dve", ptd_f[:, hf_, :].rearrange("p (a b) -> p a b", b=32),
                    ptd_sb[:, hf_, :].unsqueeze(2).to_broadcast([128, 4, 32]), [IDXB], [IDXB])
                pt_, pb_ = ps_next()
                tr(pt_[:, 0:128], ptd_f[:, hf_, :], ident[:], [IDXB, CB], [pb_])
                ts("dve", idx32[:, hf_ * 128:(hf_ + 1) * 128], pt_[:, 0:128], 32.0, pmod[:, 0:1], ALU.mult, ALU.add, [pb_, CB], [IDXB])
            for i in range(4):
                S.op("dve", (lambda i=i: (lambda e: e.memset(pex2[:, i, :], 0.0)))(), [], [PX2[i]])
            S.op("dve", lambda e: e.memset(qpad[:], 0.0), [], [QPB])
            S.op("dve", lambda e: e.memset(rpad[:], 0.0), [], [QPB])

        def gather(s_, j4, slot):
            ct_, kt_, cb_ = ctile[slot]
            d_ = s_ * 16 + j4
            off = bass.IndirectOffsetOnAxis(ap=idx32[:, d_:d_ + 1], axis=0)
            S.op("pool", lambda e: e.indirect_dma_start(out=ct_[:].rearrange("p a b -> p (a b)"), out_offset=None, in_=ckv[:, :],
                                                        in_offset=off), [IDXB], [cb_], dma=True)
            off2 = bass.IndirectOffsetOnAxis(ap=idx32[:, d_:d_ + 1], axis=0)
            S.op("pool", lambda e: e.indirect_dma_start(out=kt_[:].rearrange("p a b -> p (a b)"), out_offset=None, in_=ckr[:, :],
                                                        in_offset=off2), [IDXB], [cb_], dma=True)

        def mla_sample(l):
            N = NS
            j = l - 2
            mla_q(l, N)
            for h in range(H):
                mla_q_head(l, h, N)
                pt_, pb_ = ps_next()
                for cc in range(2):
                    mm(pt_[:, cc * NS:(cc + 1) * NS], wukT[:, h, cc * 128:(cc + 1) * 128], qnT[:, :N], True, True, [WKT, QNB], [pb_])
                for cc in range(2):
                    cpy("act", qlat[:, cc, :].rearrange("p (s h) -> p s h", h=H)[:, :, h], pt_[:, cc * NS:(cc + 1) * NS], [pb_], [QLB])
                cpy("act", qrall[:, :].rearrange("p (s h) -> p s h", h=H)[:, :, h], qrT[:, :N], [QRB], [QRA])
            for s_ in range(NS):
                for cc in range(2):
                    cpy("dve", qpad[:, cc, s_, s_ * 8:(s_ + 1) * 8], qlat[:, cc, s_ * 8:(s_ + 1) * 8], [QLB], [QPB])
                cpy("dve", rpad[:, s_, s_ * 8:(s_ + 1) * 8], qrall[:, s_ * 8:(s_ + 1) * 8], [QRA], [QPB])
            po, pob = banks[4]
            pden, pdenb = banks[5]
            blocks = [(j4, sg) for j4 in range(16) for sg in range(4)]
            reqs = [(sg * 4 + k, j4) for (j4, sg) in blocks for k in range(4)]
            issued = {"n": 0}

            def ensure(n):
                while issued["n"] < min(n, len(reqs)):
                    s2, j42 = reqs[issued["n"]]
                    gather(s2, j42, issued["n"] % NCT)
                    issued["n"] += 1

            first = True
            for bi, (j4, sg) in enumerate(blocks):
                ensure(bi * 4 + NCT)
                psc, pscb = banks[6 + (bi % 2)]

                def T_(k):
                    ct_, kt_, cb_ = ctile[(bi * 4 + k) % NCT]
                    slot = (bi * 4 + k) % 2
                    ptA, ptAb = ps_next()
                    pvA = ptA[:].bitcast(BF16)
                    for cc in range(2):
                        for t4 in range(4):
                            tr(pvA[:, (cc * 4 + t4) * 128:(cc * 4 + t4 + 1) * 128], ct_[:, t4, cc * 128:(cc + 1) * 128], identb[:],
                               [cb_, CB], [ptAb])
                    cpy("act", cTs[:, slot, :], pvA[:, :], [ptAb], [CTS[slot]])
                    ptB, ptBb = ps_next()
                    pvB = ptB[:].bitcast(BF16)
                    for t4 in range(4):
                        tr(pvB[:64, t4 * 128:(t4 + 1) * 128], kt_[:, t4, :], identb[:], [cb_, CB], [ptBb])
                    cpy("dve", rTs[:, slot, :], pvB[:64, 0:512], [ptBb], [RTS[slot]])

                def S_(k):
                    s_ = sg * 4 + k
                    slot = (bi * 4 + k) % 2
                    for cc in range(2):
                        mm(psc[:, :], qpad[:, cc, s_, :], cTs[:, slot, cc * 512:(cc + 1) * 512], k == 0 and cc == 0, False,
                           [QPB, CTS[slot]], [pscb])
                    mm(psc[:, :], rpad[:, s_, :], rTs[:, slot, :], False, k == 3, [QPB, RTS[slot]], [pscb])

                T_(0)
                for k in range(4):
                    if k + 1 < 4:
                        T_(k + 1)
                    S_(k)
                band = slice(32 * sg, 32 * sg + 32)
                act(pex2[band, sg, :], psc[band, :], AF.Exp, [pscb], [PX2[sg]], scale=SCALE)
                ptP, ptPb = ps_next()
                pvP = ptP[:].bitcast(BF16)
                for t4 in range(4):
                    tr(pvP[:, t4 * 128:(t4 + 1) * 128], pex2[:, sg, t4 * 128:(t4 + 1) * 128], identb[:], [PX2[sg], CB], [ptPb])
                cpy("act", pTs[:].rearrange("p a b -> p (a b)"), pvP[:, 0:512], [ptPb], [PTS])
                for t4 in range(4):
                    mm(pden[:, 0:128], onesb[:], pTs[:, t4, :], first and t4 == 0, False, [CB, PTS], [pdenb], skip_group_check=True)
                for k in range(4):
                    s_ = sg * 4 + k
                    ct_, kt_, cb_ = ctile[(bi * 4 + k) % NCT]
                    for t4 in range(4):
                        for cc in range(2):
                            mm(po[:, cc * 128 + s_ * 8:cc * 128 + s_ * 8 + 8], ct_[:, t4, cc * 128:(cc + 1) * 128],
                               pTs[:, t4, s_ * 8:(s_ + 1) * 8], bi == 0 and k == 0 and t4 == 0 and cc == 0, False, [cb_, PTS], [pob], skip_group_check=True)
                first = False
            col0 = T
            psn, psnb = ps_next()
            for cc in range(2):
                mm(psn[:, :NS], qlat[:, cc, :], cT_all[:, cc, col0:col0 + NS], cc == 0, False, [QLB, CKB[NG]], [psnb])
            mm(psn[:, :NS], qrall[:, :], krT_all[:, col0:col0 + NS], False, True, [QRA, CKB[NG]], [psnb])
            act(pnew[:, :], psn[:, :NS], AF.Exp, [psnb], [PNB], scale=SCALE)
            tt("dve", pnew[:, :], pnew[:, :], newmask[:, :], ALU.mult, [PNB, CB], [PNB])
            ptn, ptnb = ps_next()
            tr(ptn[:NS, 0:128], pnew[:, :], ident[:], [PNB, CB], [ptnb])
            cpy("act", pnewT[:, :], ptn[:NS, 0:128], [ptnb], [PNB])
            mm(pden[:, 0:128], onesb[:NS, :], pnewT[:, :], False, True, [CB, PNB], [pdenb], skip_group_check=True)
            ptc, ptcb = ps_next()
            pvc = ptc[:].bitcast(BF16)
            for cc in range(2):
                tr(pvc[:NS, cc * 128:(cc + 1) * 128], cT_all[:, cc, col0:col0 + NS], identb[:], [CKB[NG], CB], [ptcb])
            cpy("act", ctmb[:, :], pvc[:NS, 0:KVL], [ptcb], [PNB])
            for cc in range(2):
                mm(po[:, cc * 128:(cc + 1) * 128], ctmb[:, cc * 128:(cc + 1) * 128], pnewT[:, :], False, True, [PNB], [pob],
                   skip_group_check=True)
            recip(rstd[:, :128], pden[:, 0:128], [pdenb], [RB])
            for cc in range(2):
                tt("dve", olat[:, cc, :], po[:, cc * 128:(cc + 1) * 128], rstd[:, :128], ALU.mult, [pob, RB], [OLB])
            for h in range(H):
                pt_, pb_ = ps_next()
                for cc in range(2):
                    mm(pt_[:, :NS], wukv[:, cc, h * 256 + 128:h * 256 + 256], olat[:, cc, :].rearrange("p (s h) -> p s h", h=H)[:, :, h],
                       cc == 0, cc == 1, [WUB, OLB], [pb_])
                cpy("act", A.big[:, h, :NS], pt_[:, :NS], [pb_], [A.BGB])
            linear("w_out_b", j * D, D, 0, D, lambda kc: A.big[:, kc, :N], [A.BGB], N,
                   lambda mi, mw, p_, b_: to_mix(mi, mw, p_, b_, N))
            postnorm_add(4 * l + 1, N)

        def sample_group():
            N = NS
            S.barrier()
            A.xT, A.hT, A.mixT, A.big = xTs, hTs, mixTs, bigs
            A.XB, A.HB, A.MB, A.BGB = [Buf("xTs%d" % i) for i in range(8)], [Buf("hTs%d" % i) for i in range(8)], [Buf("mixTs%d" % i) for i in range(8)], Buf("bigs")
            load_x(xs[:, :], NS, 0)
            if "sA" not in DBG:
                decode_setup_h()
            for l in range(n_layers):
                if "sA" in DBG or "sB" in DBG:
                    break
                if l < 2:
                    hgrn_sample(l)
                else:
                    if l == 2:
                        S.barrier()
                        decode_setup()
                        dma("sp", cosF[:, :N], cd["cosF"][:, T:T + N], [], [CSB])
                        dma("sp", sinF[:, :N], cd["sinF"][:, T:T + N], [], [CSB])
                        mla_shared(N, T, T, cs[:, :], krs[:, :], CKB[NG])
                    mla_sample(l)
                if "noffn" not in DBG:
                    ffn(l, N)
            store_x(ys[:, :], NS, 0)

        dma("pool", wukv[:], w_ukv.rearrange("(k p) c -> p k c", p=128), [], [WUB])
        for l in range(2):
            for h in range(H):
                S.op("dve", (lambda l=l, h=h: (lambda e: e.memset(Sst[l][:, h, :], 0.0)))(), [], [SSB[l][h]])

        for g in groups:
            if g == "s":
                sample_group()
                continue
            N = GN
            for tt_ in range(4):
                load_x(xp[g * GN + tt_ * 128: g * GN + (tt_ + 1) * 128, :], 128, tt_ * 128)
            for l in range(n_layers):
                if l < 2:
                    if "nohgrn" not in DBG:
                        hgrn_prompt(l, g)
                else:
                    if l == 2:
                        dma("sp", cosF[:, :N], cd["cosF"][:, g * GN:g * GN + N], [], [CSB])
                        dma("sp", sinF[:, :N], cd["sinF"][:, g * GN:g * GN + N], [], [CSB])
                        mla_shared(N, g * GN, g * GN, cp[g * GN:(g + 1) * GN, :], krp[g * GN:(g + 1) * GN, :], CKB[g])
                    mla_prompt(l, g)
                if "noffn" not in DBG:
                    ffn(l, N)
            for tt_ in range(4):
                store_x(yp[g * GN + tt_ * 128: g * GN + (tt_ + 1) * 128, :], 128, tt_ * 128)

        if "dump" in DBG:
            for nm_, t_, bufs_ in (("hT", hT, [A.HB]), ("big", big, [A.BGB]), ("mixT", mixT, [A.MB]), ("rstd", rstd, [RB]), ("xT", xT, [A.XB]),
                                     ("hqd", hqd, [HQD]), ("hkd", hkd, [HKD]), ("hkd2", hkd2, [HKD2]), ("hv", hv, [HV]), ("hA", hA, [HA]),
                                     ("lb_sb", lb_sb, [CB]), ("oml_sb", oml_sb, [CB]), ("lbl_sb", lbl_sb, [CB]), ("hq", hq, [HQ]), ("hb", hb, [HBB]), ("hlf", hlf, [HLF]), ("hf", hf, [HF]), ("hk", hk, [HK]), ("hbl", hbl, [HBL]), ("hgate", hgate, [HGT]), ("hos", hos, [HOS]), ("hkd2T", hkd2T, [HKT])):
                shp = list(t_.shape)
                flat = [shp[0], int(np.prod(shp[1:]))]
                dd = nc.dram_tensor("dbg_" + nm_, flat, t_.dtype, kind="ExternalOutput").ap()
                src = t_[:] if len(shp) == 2 else t_[:].rearrange("p a b -> p (a b)")
                dma("sp", dd, src, bufs_, [OUTB])
        S.op("sp", None, [OUTB], [])
        block = stack.enter_context(nc.Block())
        S.emit(block)
    return nc, req_log


def build2(n_pool, **kw):
    _, plan = build(n_pool, plan=None, **kw)
    nc, _ = build(n_pool, plan=plan, **kw)
    return nc


def _cols(v, kc):
    v = np.asarray(v, np.float32)
    R_ = v.shape[0]
    return np.ascontiguousarray(v.reshape(R_, kc, 128).transpose(2, 0, 1).reshape(128, R_ * kc))


def shared_inputs(inp):
    f = lambda a: np.ascontiguousarray(np.asarray(a, np.float32))
    m = {}
    n_pool = inp["cache_kv_latent"].shape[0]
    m["ckv"] = f(inp["cache_kv_latent"]).reshape(n_pool * 32, 4 * KVL)
    m["ckr"] = f(inp["cache_k_rope"]).reshape(n_pool * 32, 4 * RD)
    m["gains"] = _cols(f(inp["norm_gains"]).reshape(16, D), 8)
    m["w_ffn_in"] = f(inp["w_ffn_in"]).reshape(4 * D, 2 * DFF)
    m["w_ffn_out"] = f(inp["w_ffn_out"]).reshape(4 * DFF, D)
    m["w_in_a"] = f(inp["w_in_a"]).reshape(2 * D, 4 * D)
    m["lbl"] = _cols(f(inp["lb_logits"]), 8)
    m["gna"] = _cols(f(inp["g_norm_a"]), 8)
    m["w_out_a"] = f(inp["w_out_a"]).reshape(2 * D, D)
    m["kvn"] = _cols(f(inp["kv_norm"]).reshape(1, D), 8)
    m["w_dkv"] = f(inp["w_dkv"])
    m["kvan"] = _cols(f(inp["kv_a_norm"]).reshape(1, KVL), 2)
    m["kvan_b"] = np.ascontiguousarray(np.broadcast_to(f(inp["kv_a_norm"]).reshape(1, KVL), (128, KVL)))
    m["w_ukv"] = f(inp["w_ukv"])
    m["w_dq"] = f(inp["w_dq"]).reshape(2 * D, QL)
    m["qan"] = _cols(f(inp["q_a_norm"]), 3)
    m["w_uq"] = f(inp["w_uq"]).reshape(2 * QL, H * 192)
    m["w_out_b"] = f(inp["w_out_b"]).reshape(2 * D, D)
    for k, v in _consts().items():
        m["c_" + k] = v
    return m


def core_inputs(inp, shared, c):
    m = dict(shared)
    m["xp"] = np.ascontiguousarray(np.asarray(inp["x_prompt"][c], np.float32))
    m["xs"] = np.ascontiguousarray(np.asarray(inp["x_sample"][c * NS:(c + 1) * NS, 0], np.float32))
    m["st"] = np.ascontiguousarray(np.asarray(inp["state_hgrn"][:, c * NS:(c + 1) * NS], np.float32)).reshape(2 * NS * H * 128, 128)
    m["ptd"] = np.ascontiguousarray(np.asarray(inp["page_table"][c * NS:(c + 1) * NS], np.int32)).reshape(NS * 16, 4)
    return m


def kernel(**inp):
    n_cores = 8
    n_pool = inp["cache_kv_latent"].shape[0]
    nc = build2(n_pool)
    shared = shared_inputs(inp)
    in_maps = [core_inputs(inp, shared, c) for c in range(n_cores)]
    res = run_bass_kernel_spmd(nc, in_maps, core_ids=list(range(n_cores))).results
    y_p = np.stack([r["yp"] for r in res]).reshape(8, T, D)
    y_s = np.concatenate([r["ys"] for r in res]).reshape(128, 1, D)
    st_p = np.stack([r["stp"].reshape(2, H, 128, 128) for r in res], axis=1)
    c_p = np.stack([r["cp"] for r in res]).reshape(8, T, KVL)
    kr_p = np.stack([r["krp"] for r in res]).reshape(8, T, RD)
    st_s = np.concatenate([r["sts"].reshape(2, NS, H, 128, 128) for r in res], axis=1)
    c_s = np.concatenate([r["cs"] for r in res]).reshape(128, 1, KVL)
    kr_s = np.concatenate([r["krs"] for r in res]).reshape(128, 1, RD)
    return tuple(np.ascontiguousarray(a.astype(np.float32)) for a in (y_p, y_s, st_p, c_p, kr_p, st_s, c_s, kr_s))
```
